# Optimizing a Trainium2 kernel written in Bass

```python
import math
import jax, jax.numpy as jnp
from jax import lax
import numpy as np

D_MODEL = 4096
BATCH = 4
SEQ = 4096
DEPTH = 2

N_META = 16
N_A_LAYERS = DEPTH // 2
N_B_LAYERS = DEPTH - N_A_LAYERS
EPS = 1e-6

M_HEADS = 8
M_QK_DIM = D_MODEL // 2 // M_HEADS
M_V_DIM = D_MODEL // M_HEADS
M_QK_TOT = M_HEADS * M_QK_DIM
M_V_TOT = M_HEADS * M_V_DIM
M_CHUNK = 64
GATE_CAP = 15.0
M_IN_WIDTH = 2 * M_QK_TOT + 2 * M_V_TOT + 2 * M_HEADS
M_SPLITS = (M_QK_TOT, 2 * M_QK_TOT, 2 * M_QK_TOT + M_V_TOT,
            2 * M_QK_TOT + 2 * M_V_TOT, 2 * M_QK_TOT + 2 * M_V_TOT + M_HEADS)

A_HEADS = D_MODEL // 64
Q_LORA = D_MODEL // 4
KV_LORA = 512
NOPE_DIM = 128
ROPE_DIM = 64
V_HEAD_DIM = 128
QK_HEAD_DIM = NOPE_DIM + ROPE_DIM
ROPE_THETA = 10000.0
Q_BLOCK = 128

D_FF = -(-8 * D_MODEL // (3 * 256)) * 256

kernel_name = "yoco_mlstm_mla_hybrid"


def rms_norm(x, g):
    xf = x.astype(jnp.float32)
    y = xf * lax.rsqrt(jnp.mean(xf * xf, axis=-1, keepdims=True) + EPS)
    return (y * g.astype(jnp.float32)).astype(x.dtype)


def soft_cap(z):
    return GATE_CAP * jnp.tanh(z / GATE_CAP)


def rope_tables(pos):
    inv = 1.0 / (ROPE_THETA ** (jnp.arange(0, ROPE_DIM, 2, dtype=jnp.float32) / ROPE_DIM))
    ang = pos.astype(jnp.float32)[..., None] * inv
    return jnp.cos(ang), jnp.sin(ang)


def apply_rope(t, cos, sin):
    half = ROPE_DIM // 2
    t1, t2 = t[..., :half], t[..., half:]
    c, s = cos[:, :, None, :], sin[:, :, None, :]
    return jnp.concatenate([t1 * c - t2 * s, t1 * s + t2 * c], axis=-1).astype(t.dtype)


def mlstm_chunk(state, inp):
    c, n, m = state
    q, k, v, logi, logf = inp
    t_len = q.shape[2]
    b = jnp.cumsum(logf, axis=-1)
    causal = jnp.tril(jnp.ones((t_len, t_len), dtype=bool))
    d = jnp.where(causal, b[..., :, None] - b[..., None, :] + logi[..., None, :], -jnp.inf)
    inter = b + m[..., None]
    m_t = jnp.maximum(inter, jnp.max(d, axis=-1))
    w = jnp.exp(d - m_t[..., None])
    a = jnp.exp(inter - m_t)
    s = jnp.einsum('bhtd,bhsd->bhts', q, k) * w
    num = a[..., None] * jnp.einsum('bhtd,bhde->bhte', q, c) + jnp.einsum('bhts,bhse->bhte', s, v)
    den = a * jnp.einsum('bhtd,bhd->bht', q, n) + jnp.sum(s, axis=-1)
    h = num / jnp.maximum(jnp.abs(den), jnp.exp(-m_t))[..., None]
    b_last = b[..., -1]
    g = b_last[..., None] - b + logi
    m_new = jnp.maximum(b_last + m, jnp.max(g, axis=-1))
    decay = jnp.exp(b_last + m - m_new)
    kw = k * jnp.exp(g - m_new[..., None])[..., None]
    c_new = decay[..., None, None] * c + jnp.einsum('bhsd,bhse->bhde', kw, v)
    n_new = decay[..., None] * n + jnp.sum(kw, axis=2)
    return (c_new, n_new, m_new), h


def to_chunks(t):
    b_, h_, s_ = t.shape[:3]
    t = t.reshape((b_, h_, s_ // M_CHUNK, M_CHUNK) + t.shape[3:])
    return jnp.moveaxis(t, 2, 0)


def mlstm_mixer(xn, w_in, b_i, b_f, g_out, w_out):
    bsz, length, _ = xn.shape
    proj = xn @ w_in
    q, k, v, o, gi, gf = jnp.split(proj, M_SPLITS, axis=-1)

    def heads(t, dim):
        return t.reshape(bsz, length, M_HEADS, dim).transpose(0, 2, 1, 3).astype(jnp.float32)

    q = heads(q, M_QK_DIM)
    k = heads(k, M_QK_DIM) * (M_QK_DIM ** -0.5)
    v = heads(v, M_V_DIM)
    logi = soft_cap(gi.astype(jnp.float32) + b_i.astype(jnp.float32)).transpose(0, 2, 1)
    logf = jax.nn.log_sigmoid(soft_cap(gf.astype(jnp.float32) + b_f.astype(jnp.float32))).transpose(0, 2, 1)
    state0 = (jnp.zeros((bsz, M_HEADS, M_QK_DIM, M_V_DIM), jnp.float32),
              jnp.zeros((bsz, M_HEADS, M_QK_DIM), jnp.float32),
              jnp.zeros((bsz, M_HEADS), jnp.float32))
    seqs = (q, k, v, logi, logf)
    state, h_meta = mlstm_chunk(state0, tuple(t[:, :, :N_META] for t in seqs))
    _, h_rest = lax.scan(mlstm_chunk, state, tuple(to_chunks(t[:, :, N_META:]) for t in seqs))
    h_rest = jnp.moveaxis(h_rest, 0, 2).reshape(bsz, M_HEADS, length - N_META, M_V_DIM)
    hh = jnp.concatenate([h_meta, h_rest], axis=2).transpose(0, 2, 1, 3)
    hh = rms_norm(hh, g_out.reshape(M_HEADS, M_V_DIM))
    out = (hh.reshape(bsz, length, M_V_TOT) * jax.nn.sigmoid(o.astype(jnp.float32))).astype(xn.dtype)
    return out @ w_out


def swiglu(xn, w_gate_up, w_down):
    g, u = jnp.split(xn @ w_gate_up, 2, axis=-1)
    return (jax.nn.silu(g) * u) @ w_down


def shared_mla_kv(h, g_in, w_down, g_latent, w_up, g_k, cos, sin):
    bsz, length, _ = h.shape
    a = rms_norm(h, g_in) @ w_down
    c_kv, k_rope = a[..., :KV_LORA], a[..., KV_LORA:]
    kv = (rms_norm(c_kv, g_latent) @ w_up).reshape(bsz, length, A_HEADS, NOPE_DIM + V_HEAD_DIM)
    k_nope, v = kv[..., :NOPE_DIM], kv[..., NOPE_DIM:]
    k_rope = jnp.broadcast_to(k_rope[:, :, None, :], (bsz, length, A_HEADS, ROPE_DIM))
    k = rms_norm(jnp.concatenate([k_nope, k_rope], axis=-1), g_k)
    k = jnp.concatenate([k[..., :NOPE_DIM], apply_rope(k[..., NOPE_DIM:], cos, sin)], axis=-1)
    return k.transpose(0, 2, 1, 3), v.transpose(0, 2, 1, 3)


def causal_block_attention(q, k, v):
    length = q.shape[2]
    bounds = [0] + list(range(N_META, length, Q_BLOCK)) + [length]
    scale = QK_HEAD_DIM ** -0.5
    outs = []
    for lo, hi in zip(bounds[:-1], bounds[1:]):
        s = jnp.einsum('bhqd,bhkd->bhqk', q[:, :, lo:hi], k[:, :, :hi],
                       preferred_element_type=jnp.float32) * scale
        mask = (lo + jnp.arange(hi - lo))[:, None] >= jnp.arange(hi)[None, :]
        p = jax.nn.softmax(jnp.where(mask, s, -jnp.inf), axis=-1)
        outs.append(jnp.einsum('bhqk,bhkd->bhqd', p.astype(v.dtype), v[:, :, :hi]))
    return jnp.concatenate(outs, axis=2)


def mla_mixer(xn, k, v, w_dq, g_q_latent, w_uq, g_q, w_o, cos, sin):
    bsz, length, _ = xn.shape
    c_q = rms_norm(xn @ w_dq, g_q_latent)
    q = rms_norm((c_q @ w_uq).reshape(bsz, length, A_HEADS, QK_HEAD_DIM), g_q)
    q = jnp.concatenate([q[..., :NOPE_DIM], apply_rope(q[..., NOPE_DIM:], cos, sin)], axis=-1)
    o = causal_block_attention(q.transpose(0, 2, 1, 3), k, v)
    return o.transpose(0, 2, 1, 3).reshape(bsz, length, A_HEADS * V_HEAD_DIM) @ w_o


def setup_inputs(seed: int = 0) -> dict:
    key = jax.random.key(seed)
    ks = jax.random.split(key, 32)
    f32 = jnp.float32

    def nrm(k, shape, fan_in):
        return jax.random.normal(k, shape, f32) * (fan_in ** -0.5)

    def gain(k, shape):
        return 1.0 + 0.02 * jax.random.normal(k, shape, f32)

    x = jax.random.normal(ks[0], (BATCH, SEQ, D_MODEL), f32)
    offsets = jax.random.randint(ks[1], (BATCH, 1), 0, 256, dtype=jnp.int32)
    positions = (offsets + jnp.arange(SEQ, dtype=jnp.int32)[None, :]).astype(jnp.int32)
    meta_tokens = jax.random.normal(ks[2], (N_META, D_MODEL), f32)
    a_b_f = (jnp.broadcast_to(jnp.linspace(3.0, 6.0, M_HEADS, dtype=f32), (N_A_LAYERS, M_HEADS))
             + 0.1 * jax.random.normal(ks[6], (N_A_LAYERS, M_HEADS), f32))
    return {
        "x": x,
        "positions": positions,
        "meta_tokens": meta_tokens,
        "a_norm_g": gain(ks[3], (N_A_LAYERS, D_MODEL)),
        "a_w_in": nrm(ks[4], (N_A_LAYERS, D_MODEL, M_IN_WIDTH), D_MODEL),
        "a_b_i": 0.1 * jax.random.normal(ks[5], (N_A_LAYERS, M_HEADS), f32),
        "a_b_f": a_b_f,
        "a_out_norm_g": gain(ks[7], (N_A_LAYERS, M_V_TOT)),
        "a_w_out": nrm(ks[8], (N_A_LAYERS, M_V_TOT, D_MODEL), M_V_TOT),
        "ffn_norm_g": gain(ks[9], (DEPTH, D_MODEL)),
        "ffn_w_gate_up": nrm(ks[10], (DEPTH, D_MODEL, 2 * D_FF), D_MODEL),
        "ffn_w_down": nrm(ks[11], (DEPTH, D_FF, D_MODEL), D_FF),
        "kv_norm_g": gain(ks[12], (D_MODEL,)),
        "kv_w_down": nrm(ks[13], (D_MODEL, KV_LORA + ROPE_DIM), D_MODEL),
        "kv_latent_norm_g": gain(ks[14], (KV_LORA,)),
        "kv_w_up": nrm(ks[15], (KV_LORA, A_HEADS * (NOPE_DIM + V_HEAD_DIM)), KV_LORA),
        "k_norm_g": gain(ks[16], (QK_HEAD_DIM,)),
        "b_norm_g": gain(ks[17], (N_B_LAYERS, D_MODEL)),
        "b_w_dq": nrm(ks[18], (N_B_LAYERS, D_MODEL, Q_LORA), D_MODEL),
        "b_q_latent_norm_g": gain(ks[19], (N_B_LAYERS, Q_LORA)),
        "b_w_uq": nrm(ks[20], (N_B_LAYERS, Q_LORA, A_HEADS * QK_HEAD_DIM), Q_LORA),
        "q_norm_g": gain(ks[21], (N_B_LAYERS, QK_HEAD_DIM)),
        "b_w_o": nrm(ks[22], (N_B_LAYERS, A_HEADS * V_HEAD_DIM, D_MODEL), A_HEADS * V_HEAD_DIM),
    }


def reference(x, positions, meta_tokens, a_norm_g, a_w_in, a_b_i, a_b_f, a_out_norm_g, a_w_out,
              ffn_norm_g, ffn_w_gate_up, ffn_w_down, kv_norm_g, kv_w_down, kv_latent_norm_g,
              kv_w_up, k_norm_g, b_norm_g, b_w_dq, b_q_latent_norm_g, b_w_uq, q_norm_g, b_w_o):
    bsz = x.shape[0]
    meta = jnp.broadcast_to(meta_tokens[None].astype(x.dtype), (bsz, N_META, D_MODEL))
    h = jnp.concatenate([meta, x], axis=1)
    meta_pos = jnp.broadcast_to(jnp.arange(N_META, dtype=jnp.int32)[None, :], (bsz, N_META))
    pos = jnp.concatenate([meta_pos, positions.astype(jnp.int32) + N_META], axis=1)
    cos, sin = rope_tables(pos)
    k_shared, v_shared = None, None
    for layer in range(DEPTH):
        if layer < N_A_LAYERS:
            i = layer
            h = h + mlstm_mixer(rms_norm(h, a_norm_g[i]), a_w_in[i], a_b_i[i], a_b_f[i],
                                a_out_norm_g[i], a_w_out[i])
        else:
            j = layer - N_A_LAYERS
            if j == 0:
                k_shared, v_shared = shared_mla_kv(h, kv_norm_g, kv_w_down, kv_latent_norm_g,
                                                   kv_w_up, k_norm_g, cos, sin)
            h = h + mla_mixer(rms_norm(h, b_norm_g[j]), k_shared, v_shared, b_w_dq[j],
                              b_q_latent_norm_g[j], b_w_uq[j], q_norm_g[j], b_w_o[j], cos, sin)
        h = h + swiglu(rms_norm(h, ffn_norm_g[layer]), ffn_w_gate_up[layer], ffn_w_down[layer])
    return h[:, N_META:]
```

```python
import numpy as np
import concourse.bass as bass
import concourse.mybir as mybir
from concourse.bass_utils import run_bass_kernel_spmd

F32 = mybir.dt.float32
BF16 = mybir.dt.bfloat16
I32 = mybir.dt.int32
AF = mybir.ActivationFunctionType
ALU = mybir.AluOpType
AX = mybir.AxisListType

NCORES = 8
D = 4096
T = 2048
TPRE = 2064
NEUT = 2048
TT = [(i * 512, 512) for i in range(4)]
TT_P = [(0, 512), (512, 512), (1024, 512), (1536, 512), (2048, 16)]


class Sem:
    _k = 0

    def __init__(self, nc, name):
        Sem._k += 1
        self.h = nc.alloc_semaphore(f"{name}_{Sem._k}")
        self.n = 0


class Buf:
    _id = 0

    def __init__(self, name=None, multi=False):
        Buf._id += 1
        self.name = name or f"b{Buf._id}"
        self.w = {}
        self.r = {}
        self.sem_in = None
        self.sem_out = None
        self.multi = multi


class FW:
    ENG = ("pe", "act", "dve", "pool", "sp")

    def __init__(self, nc):
        self.nc = nc
        self.eng = {"pe": nc.tensor, "act": nc.scalar, "dve": nc.vector,
                    "pool": nc.gpsimd, "sp": nc.sync}
        self.prog = {e: Sem(nc, f"prog_{e}") for e in self.ENG}
        self.waited = {e: {} for e in self.ENG}
        self.ninst = 0
        self.sem_pool = []

    def get_sem(self):
        if self.sem_pool:
            return self.sem_pool.pop()
        return Sem(self.nc, "dq")

    def put_sems(self, bufs):
        for b in bufs:
            for a in ("sem_in", "sem_out"):
                sm = getattr(b, a)
                if sm is not None:
                    self.sem_pool.append(sm)
                    setattr(b, a, None)

    def _wait(self, e, sem, cnt):
        if cnt <= 0:
            return
        w = self.waited[e]
        if w.get(sem, 0) >= cnt:
            return
        if sem is self.prog[e] and cnt > sem.n:
            return
        self.eng[e].wait_ge(sem.h, cnt)
        w[sem] = cnt

    def _deps(self, e, reads, writes, skip=None):
        pe = self.prog["pe"]
        for b in reads:
            for s, c in b.w.items():
                if (e == "pe" and s is pe) or s is skip:
                    continue
                self._wait(e, s, c)
        for b in writes:
            for s, c in b.r.items():
                if (e == "pe" and s is pe) or s is skip:
                    continue
                self._wait(e, s, c)
            if not b.multi:
                for s, c in b.w.items():
                    if (e == "pe" and s is pe) or s is skip:
                        continue
                    self._wait(e, s, c)

    def _mark(self, tok, reads, writes):
        s, c = tok
        for b in reads:
            if b.r.get(s, 0) < c:
                b.r[s] = c
        for b in writes:
            if b.multi:
                if b.w.get(s, 0) < c:
                    b.w[s] = c
            else:
                b.w = {s: c}
                b.r = {}

    def op(self, e, fn, reads=(), writes=(), signal=True):
        self._deps(e, reads, writes)
        inst = fn()
        self.ninst += 1
        p = self.prog[e]
        if signal:
            p.n += 1
            inst.then_inc(p.h, 1)
            tok = (p, p.n)
        else:
            tok = (p, p.n + 1)
        self._mark(tok, reads, writes)
        self.last = inst
        return tok

    def no_ldweights(self):
        old = self.last.ins
        new = mybir.InstMatmult(
            name=old.name, opcode=old.opcode, engine=old.engine, debug=old.debug, ins=old.ins, outs=old.outs,
            sync_info=old.sync_info, start_tensor_calc=old.start_tensor_calc, stop_tensor_calc=old.stop_tensor_calc,
            is_transpose=old.is_transpose, tile_size=old.tile_size, tile_position=old.tile_position,
            perf_mode=old.perf_mode, bass_skip_group_check=old.bass_skip_group_check, ldweights=False)
        self.nc.register_instruction(new, overwrite=True)

    def dma(self, q, out_ap, in_ap, src, dst, side, **kw):
        if side == "in":
            if dst.sem_in is None:
                dst.sem_in = self.get_sem()
            sem = dst.sem_in
        else:
            if src.sem_out is None:
                src.sem_out = self.get_sem()
            sem = src.sem_out
        self._deps(q, [src], [dst], skip=sem)
        inst = self.eng[q].dma_start(out=out_ap, in_=in_ap, **kw)
        sem.n += 16
        inst.then_inc(sem.h, 16)
        self.ninst += 1
        s, c = sem, sem.n
        if src.r.get(s, 0) < c:
            src.r[s] = c
        if dst.multi:
            dst.w[s] = c
        else:
            dst.w = {s: c}
            dst.r = {}
        return (s, c)

    def drain(self, bufs):
        for b in bufs:
            for s, c in list(b.w.items()) + list(b.r.items()):
                self._wait("sp", s, c)
        for e in self.ENG:
            if e != "sp":
                self._wait("sp", self.prog[e], self.prog[e].n)
        self.nc.all_engine_barrier()


import os as _os
GEMM_KOUTER = bool(int(_os.environ.get('GEMM_KOUTER', '0')))
GEMM_ROWTILE = bool(int(_os.environ.get('GEMM_ROWTILE', '0')))
GEMM_NOLDW = bool(int(_os.environ.get('GEMM_NOLDW', '0')))


class Arena:
    def __init__(self, sc, Tn=T, kmax=32, nps=8):
        self.act, _ = sc.sb([128, kmax, Tn], BF16, "g_act")
        self.b_act = [sc.track(Buf(f"act{k}")) for k in range(kmax)]
        self.wb = [sc.sb([128, kmax, 256], BF16, f"g_wb{i}", multi=True) for i in range(2)]
        self.ws = [sc.sb([128, 8, 256], F32, f"g_ws{i}", multi=True) for i in range(2)]
        self.ps = [sc.ps(name=f"g_ps{i}") for i in range(nps)]
        self.nps = nps
        self.ips = 0
        self.iws = 0
        self.iwb = 0


def gemm(fw, ar, actT, b_actT, K, W, b_W, superblocks, epilogue, tts=TT, kpass=32, Tn=T):
    nc = fw.nc
    nkc = K // 128
    npass = -(-nkc // kpass)
    base, rem = nkc // npass, nkc % npass
    passes = []
    k0 = 0
    for p in range(npass):
        n = base + (1 if p < rem else 0)
        passes.append((k0, n))
        k0 += n
    for kp, (kc0, kc) in enumerate(passes):
        for k in range(kc):
            r0 = (kc0 + k) * 128
            fw.dma("sp", ar.act[:, k, 0:Tn], actT[r0:r0 + 128, 0:Tn], b_actT, ar.b_act[k], "in")
        for sbi, sb in enumerate(superblocks):
            offs = []
            o = 0
            for (c0, n) in sb:
                offs.append(o)
                o += n
            assert o <= 256
            segs = []
            for (c0, n), oo in zip(sb, offs):
                if segs and segs[-1][0] + segs[-1][1] == c0 and segs[-1][2] + segs[-1][1] == oo:
                    segs[-1][1] += n
                else:
                    segs.append([c0, n, oo])
            wb, b_wb = ar.wb[ar.iwb % 2]
            ar.iwb += 1
            for g0 in range(0, kc, 8):
                gn = min(8, kc - g0)
                ws, b_ws = ar.ws[ar.iws % 2]
                ar.iws += 1
                r0 = (kc0 + g0) * 128
                for (c0, n, oo) in segs:
                    src = W[r0:r0 + gn * 128, c0:c0 + n].rearrange("(c p) j -> p c j", p=128)
                    fw.dma("sp", ws[:, 0:gn, oo:oo + n], src, b_W, b_ws, "in")
                if ar.iws % 2 == 0:
                    fw.op("pool", lambda: nc.gpsimd.tensor_copy(wb[:, g0:g0 + gn, 0:o], ws[:, 0:gn, 0:o]),
                          reads=[b_ws], writes=[b_wb])
                else:
                    fw.op("act", lambda: nc.scalar.copy(wb[:, g0:g0 + gn, 0:o], ws[:, 0:gn, 0:o]),
                          reads=[b_ws], writes=[b_wb])
            if GEMM_KOUTER and len(tts) <= ar.nps:
                for (c0, ncol), oo in zip(sb, offs):
                    pss = []
                    for _ in tts:
                        pss.append(ar.ps[ar.ips % ar.nps])
                        ar.ips += 1
                    for k in range(kc):
                        for ti_, ((t0, tn), (ps, b_ps)) in enumerate(zip(tts, pss)):
                            fw.op("pe", lambda: nc.tensor.matmul(ps[0:ncol, 0:tn], wb[:, k, oo:oo + ncol],
                                                                 ar.act[:, k, t0:t0 + tn],
                                                                 start=(k == 0), stop=(k == kc - 1)),
                                  reads=[b_wb, ar.b_act[k]], writes=[b_ps], signal=(k == kc - 1))
                            if ti_ > 0 and GEMM_NOLDW:
                                fw.no_ldweights()
                    for (t0, tn), (ps, b_ps) in zip(tts, pss):
                        epilogue(sbi, t0, tn, [(c0, ncol, ps, b_ps)], kp, len(passes))
                continue
            if GEMM_ROWTILE:
                for (t0, tn) in tts:
                    blocks = []
                    for (c0, ncol), oo in zip(sb, offs):
                        psA, b_psA = ar.ps[ar.ips % ar.nps]
                        psB, b_psB = ar.ps[(ar.ips + 1) % ar.nps]
                        ar.ips += 2
                        for k in range(kc):
                            fw.op("pe", lambda: nc.tensor.matmul(psA[0:ncol, 0:tn], wb[0:64, k, oo:oo + ncol],
                                                                 ar.act[0:64, k, t0:t0 + tn],
                                                                 start=(k == 0), stop=(k == kc - 1)),
                                  reads=[b_wb, ar.b_act[k]], writes=[b_psA], signal=(k == kc - 1))
                            fw.op("pe", lambda: nc.tensor.matmul(psB[0:ncol, 0:tn], wb[64:128, k, oo:oo + ncol],
                                                                 ar.act[64:128, k, t0:t0 + tn],
                                                                 start=(k == 0), stop=(k == kc - 1)),
                                  reads=[b_wb, ar.b_act[k]], writes=[b_psB], signal=(k == kc - 1))
                        blocks.append((c0, ncol, psA, b_psA, psB, b_psB))
                    epilogue(sbi, t0, tn, blocks, kp, len(passes))
                continue
            for (t0, tn) in tts:
                blocks = []
                for (c0, ncol), oo in zip(sb, offs):
                    ps, b_ps = ar.ps[ar.ips % ar.nps]
                    ar.ips += 1
                    for k in range(kc):
                        fw.op("pe", lambda: nc.tensor.matmul(ps[0:ncol, 0:tn], wb[:, k, oo:oo + ncol],
                                                             ar.act[:, k, t0:t0 + tn],
                                                             start=(k == 0), stop=(k == kc - 1)),
                              reads=[b_wb, ar.b_act[k]], writes=[b_ps], signal=(k == kc - 1))
                    blocks.append((c0, ncol, ps, b_ps))
                epilogue(sbi, t0, tn, blocks, kp, len(passes))


def cb_range(c0, c1):
    out = []
    c = c0
    while c < c1:
        sb = []
        e = min(c + 256, c1)
        cc = c
        while cc < e:
            n = min(128, e - cc)
            sb.append((cc, n))
            cc += n
        out.append(sb)
        c = e
    return out


import contextlib


class Scope:
    def __init__(self, fw):
        self.fw = fw
        self.nc = fw.nc
        self.stack = contextlib.ExitStack()
        self.bufs = []

    def __enter__(self):
        self.stack.__enter__()
        return self

    def __exit__(self, *a):
        if a[0] is None:
            self.fw.drain(self.bufs)
            self.fw.put_sems(self.bufs)
        return self.stack.__exit__(*a)

    def sb(self, shape, dtype, name=None, multi=False):
        t = self.stack.enter_context(self.nc.sbuf_tensor(list(shape), dtype))
        b = Buf(name, multi=multi)
        self.bufs.append(b)
        return t, b

    def ps(self, shape=(128, 512), dtype=F32, name=None):
        t = self.stack.enter_context(self.nc.psum_tensor(list(shape), dtype))
        b = Buf(name)
        self.bufs.append(b)
        return t, b

    def track(self, b):
        self.bufs.append(b)
        return b


class Consts:
    def __init__(self, fw, ident_f32, ident_bf16, ones_bf16):
        nc = fw.nc
        self.b = Buf("consts", multi=True)
        self.ident = nc.alloc_sbuf_tensor("c_ident", [128, 128], F32)
        self.identb = nc.alloc_sbuf_tensor("c_identb", [128, 128], BF16)
        self.onesb = nc.alloc_sbuf_tensor("c_onesb", [128, 128], BF16)
        bd = Buf("cdram", multi=True)
        fw.dma("sp", self.ident[:, :], ident_f32, bd, self.b, "in")
        fw.dma("sp", self.identb[:, :], ident_bf16, bd, self.b, "in")
        fw.dma("sp", self.onesb[:, :], ones_bf16, bd, self.b, "in")


def tok_blocks(n, bs=128):
    out = []
    t = 0
    while t < n:
        out.append((t, min(bs, n - t)))
        t += bs
    return out


def transpose_in(fw, cs, x, b_x, hT, b_hT, Tn, Dn=D):
    nc = fw.nc
    nk = Dn // 128
    with Scope(fw) as sc:
        xin = [sc.sb([128, Dn], F32, f"ti_x{i}") for i in range(2)]
        ho = [sc.sb([128, nk, 512], F32, f"ti_h{i}") for i in range(2)]
        pss = [sc.ps(name=f"ti_ps{i}") for i in range(4)]
        ips = 0
        groups = tok_blocks(Tn, 512)
        for gi, (g0, gn) in enumerate(groups):
            h_t, h_b = ho[gi % 2]
            for bi, (t0, tn) in enumerate(tok_blocks(gn, 128)):
                x_t, x_b = xin[bi % 2]
                fw.dma("sp", x_t[0:tn, :], x[g0 + t0:g0 + t0 + tn, :], b_x, x_b, "in")
                for c4 in range(0, nk, 4):
                    ps_t, ps_b = pss[ips % 4]
                    ips += 1
                    for j in range(4):
                        c = c4 + j
                        fw.op("pe", lambda: nc.tensor.transpose(ps_t[:, j * 128:j * 128 + tn],
                                                                x_t[0:tn, c * 128:(c + 1) * 128],
                                                                cs.ident[0:tn, 0:tn]),
                              reads=[x_b, cs.b], writes=[ps_b], signal=(j == 3))
                    e = "dve" if (c4 // 4) % 2 == 0 else "act"
                    src = ps_t[:, :].rearrange("p (j t) -> p j t", j=4)[:, :, 0:tn]
                    dst = h_t[:, c4:c4 + 4, t0:t0 + tn]
                    if e == "dve":
                        fw.op("dve", lambda: nc.vector.tensor_copy(dst, src), reads=[ps_b], writes=[h_b])
                    else:
                        fw.op("act", lambda: nc.scalar.copy(dst, src), reads=[ps_b], writes=[h_b])
            fw.dma("pool", hT[:, g0:g0 + gn].rearrange("(c p) t -> p c t", p=128), h_t[:, :, 0:gn],
                   h_b, b_hT, "out")


def norm_fm(fw, cs, src, b_src, g_l, b_g, dst, b_dst, Tn, Kd, tw=256):
    nc = fw.nc
    nk = Kd // 128
    with Scope(fw) as sc:
        g_t, g_b = sc.sb([128, nk], F32, "nf_g")
        fw.dma("sp", g_t[:, :], g_l, b_g, g_b, "in")
        xs = [sc.sb([128, nk, tw], F32, f"nf_x{i}") for i in range(2)]
        sq = [sc.sb([128, nk, tw], BF16, f"nf_sq{i}") for i in range(2)]
        ys = [sc.sb([128, nk, tw], BF16, f"nf_y{i}") for i in range(2)]
        rs = [sc.sb([128, tw], F32, f"nf_r{i}") for i in range(2)]
        pss = [sc.ps(name=f"nf_ps{i}") for i in range(2)]
        for i, (t0, tn) in enumerate(tok_blocks(Tn, tw)):
            x_t, x_b = xs[i % 2]
            s_t, s_b = sq[i % 2]
            y_t, y_b = ys[i % 2]
            r_t, r_b = rs[i % 2]
            ps_t, ps_b = pss[i % 2]
            fw.dma("sp", x_t[:, :, 0:tn], src[:, t0:t0 + tn].rearrange("(c p) t -> p c t", p=128),
                   b_src, x_b, "in")
            fw.op("act", lambda: nc.scalar.activation(s_t[:, :, 0:tn], x_t[:, :, 0:tn], AF.Square),
                  reads=[x_b], writes=[s_b])
            for c in range(nk):
                fw.op("pe", lambda: nc.tensor.matmul(ps_t[:, 0:tn], cs.onesb[:, :], s_t[:, c, 0:tn],
                                                     start=(c == 0), stop=(c == nk - 1)),
                      reads=[s_b, cs.b], writes=[ps_b], signal=(c == nk - 1))
            rstd_from_sumsq(fw, ps_t[:, 0:tn], ps_b, r_t[:, 0:tn], r_b, 1.0 / Kd)
            for c in range(nk):
                fw.op("dve", lambda: nc.vector.scalar_tensor_tensor(
                    y_t[:, c, 0:tn], x_t[:, c, 0:tn], g_t[:, c:c + 1], r_t[:, 0:tn],
                    ALU.mult, ALU.mult), reads=[x_b, g_b, r_b], writes=[y_b], signal=(c == nk - 1))
            fw.dma("pool", dst[:, t0:t0 + tn].rearrange("(c p) t -> p c t", p=128), y_t[:, :, 0:tn],
                   y_b, b_dst, "out")


EPS = 1e-6


def rstd_from_sumsq(fw, ss_ap, ss_b, out_ap, out_b, inv_n):
    nc = fw.nc
    fw.op("act", lambda: nc.scalar.activation(out_ap, ss_ap, AF.Ln, bias=EPS, scale=inv_n),
          reads=[ss_b], writes=[out_b])
    fw.op("act", lambda: nc.scalar.activation(out_ap, out_ap, AF.Exp, scale=-0.5),
          reads=[out_b], writes=[out_b])


class Evac:
    def __init__(self, sc, n=4, dtype=BF16, w=512, name="ev"):
        self.bufs = [sc.sb([128, w], dtype, f"{name}{i}") for i in range(n)]
        self.i = 0

    def next(self):
        t = self.bufs[self.i % len(self.bufs)]
        self.i += 1
        return t


def evac_copy(fw, dst_ap, dst_b, src_ap, src_b, idx, func=None, scale=1.0):
    nc = fw.nc
    if func is not None or idx % 2 == 1:
        f = func if func is not None else AF.Copy
        fw.op("act", lambda: nc.scalar.activation(dst_ap, src_ap, f, scale=scale), reads=[src_b], writes=[dst_b])
    else:
        if scale == 1.0:
            fw.op("dve", lambda: nc.vector.tensor_copy(dst_ap, src_ap), reads=[src_b], writes=[dst_b])
        else:
            fw.op("dve", lambda: nc.vector.tensor_scalar(dst_ap, src_ap, scale, None, ALU.mult),
                  reads=[src_b], writes=[dst_b])


def store_epilogue(fw, sc, routes):
    evb = Evac(sc, 4, BF16, name="evb")
    evf = Evac(sc, 2, F32, name="evf")
    cnt = [0]

    def epi(sbi, t0, tn, blocks, kp, nkp):
        for (c0, ncol, ps, b_ps) in blocks:
            for (lo, hi, dst, roff, toff, func, scale, dt, b_dst) in routes:
                if lo <= c0 < hi:
                    break
            else:
                raise AssertionError(c0)
            ev, b_ev = (evb if dt == BF16 else evf).next()
            evac_copy(fw, ev[0:ncol, 0:tn], b_ev, ps[0:ncol, 0:tn], b_ps, cnt[0], func, scale)
            cnt[0] += 1
            r = roff + c0 - lo
            fw.dma("pool", dst[r:r + ncol, toff + t0:toff + t0 + tn], ev[0:ncol, 0:tn], b_ev, b_dst, "out")
    return epi


def resid_epilogue(fw, sc, hT, b_hT):
    nc = fw.nc
    hb = Evac(sc, 3, F32, name="rh")

    def epi(sbi, t0, tn, blocks, kp, nkp):
        for (c0, ncol, ps, b_ps) in blocks:
            h, b_h = hb.next()
            fw.dma("sp", h[0:ncol, 0:tn], hT[c0:c0 + ncol, t0:t0 + tn], b_hT, b_h, "in")
            fw.op("dve", lambda: nc.vector.tensor_tensor(h[0:ncol, 0:tn], ps[0:ncol, 0:tn], h[0:ncol, 0:tn], ALU.add),
                  reads=[b_ps, b_h], writes=[b_h])
            fw.dma("pool", hT[c0:c0 + ncol, t0:t0 + tn], h[0:ncol, 0:tn], b_h, b_hT, "out")
    return epi


def swiglu_epilogue(fw, sc, hidT, b_hid, dff):
    nc = fw.nc
    sg = Evac(sc, 2, F32, name="sg")
    hb = Evac(sc, 3, BF16, name="hb")

    def epi(sbi, t0, tn, blocks, kp, nkp):
        (cg, ncol, psg, b_psg), (cu, ncol2, psu, b_psu) = blocks
        assert cu == cg + dff and ncol == ncol2
        s, b_s = sg.next()
        h, b_h = hb.next()
        fw.op("act", lambda: nc.scalar.activation(s[0:ncol, 0:tn], psg[0:ncol, 0:tn], AF.Silu),
              reads=[b_psg], writes=[b_s])
        fw.op("dve", lambda: nc.vector.tensor_tensor(h[0:ncol, 0:tn], s[0:ncol, 0:tn], psu[0:ncol, 0:tn], ALU.mult),
              reads=[b_s, b_psu], writes=[b_h])
        fw.dma("pool", hidT[cg:cg + ncol, t0:t0 + tn], h[0:ncol, 0:tn], b_h, b_hid, "out")
    return epi


DFF = 11008


class Seg:
    def __init__(self, hT, b_hT, xnT, b_xnT, Tn, tts, off):
        self.hT, self.b_hT, self.xnT, self.b_xnT, self.Tn, self.tts, self.off = hT, b_hT, xnT, b_xnT, Tn, tts, off


def seg_L(P):
    return Seg(P.hT, P.b_hT, P.xnT, P.b_xnT, T, TT, TPRE)


def seg_P(P):
    return Seg(P.hpT, P.b_hpT, P.xnpT, P.b_xnpT, TPRE, TT_P, 0)


def ffn(fw, cs, P, layer, sg=None):
    sg = sg or seg_L(P)
    norm_fm(fw, cs, sg.hT, sg.b_hT, P.ffn_g[layer], P.b_in, sg.xnT, sg.b_xnT, sg.Tn, D)
    hid = P.hidT[:, 0:sg.Tn]
    with Scope(fw) as sc:
        ar = Arena(sc, Tn=sg.Tn)
        sbs = [[(j * 128, 128), (DFF + j * 128, 128)] for j in range(DFF // 128)]
        gemm(fw, ar, sg.xnT, sg.b_xnT, D, P.w_gu[layer], P.b_in, sbs,
             swiglu_epilogue(fw, sc, hid, P.b_hidT, DFF), tts=sg.tts, Tn=sg.Tn)
    with Scope(fw) as sc:
        ar = Arena(sc, Tn=sg.Tn, kmax=29)
        gemm(fw, ar, hid, P.b_hidT, DFF, P.w_dn[layer], P.b_in, cb_range(0, D),
             resid_epilogue(fw, sc, sg.hT, sg.b_hT), kpass=29, tts=sg.tts, Tn=sg.Tn)


TALL = TPRE + T
MH = 8
MDK = 256
MDV = 512
CHUNKS = [(128 * c, 128, True) for c in range(16)] + [(NEUT, 16, True)] + \
         [(NEUT + 16 + 128 * c, 128, True) for c in range(16)]
NCH = len(CHUNKS)


def l0_proj(fw, cs, P):
    for (xn, b_xn, Tn, tts, off) in ((P.xnT, P.b_xnT, T, TT, TPRE), (P.xnpT, P.b_xnpT, TPRE, TT_P, 0)):
        with Scope(fw) as sc:
            ar = Arena(sc, Tn=Tn)
            routes = [
                (0, 2048, P.qT, 0, off, None, 1.0, BF16, P.b_qT),
                (2048, 4096, P.kT, 0, off, None, 1.0 / 16.0, BF16, P.b_kT),
                (4096, 8192, P.vT, 0, off, None, 1.0, BF16, P.b_vT),
                (8192, 12288, P.ogT, 0, off, AF.Sigmoid, 1.0, BF16, P.b_ogT),
                (12288, 12296, P.graw, 0, off, None, 1.0, F32, P.b_graw),
                (12296, 12304, P.graw, 8, off, None, 1.0, F32, P.b_graw),
            ]
            gemm(fw, ar, xn, b_xn, D, P.w_in, P.b_in, cb_range(0, 12288) + [[(12288, 8), (12296, 8)]],
                 store_epilogue(fw, sc, routes), tts=tts, Tn=Tn)


def mlstm_gates(fw, cs, P, efc, b_efc, lam, b_lam):
    nc = fw.nc
    with Scope(fw) as sc:
        A1, b1 = sc.sb([8, TALL], F32, "mg1")
        A2, b2 = sc.sb([8, TALL], F32, "mg2")
        A3, b3 = sc.sb([8, TALL], F32, "mg3")
        A4, b4 = sc.sb([8, TALL + 1], F32, "mg4")
        sm, bsm = sc.sb([8, 8], F32, "mgs")
        sel, bsel = sc.sb([8, 8, 128], F32, "mgsel")
        zb, bzb = sc.sb([128, 8, 34], F32, "mgzb")
        ps1, bp1 = sc.ps(name="mgp1")
        ps2, bp2 = sc.ps(name="mgp2")
        fw.dma("sp", A1[:, :], P.graw[0:8, :], P.b_graw, b1, "in")
        fw.dma("sp", A2[:, :], P.graw[8:16, :], P.b_graw, b2, "in")
        fw.dma("sp", sm[:, 0:3], P.mparams, P.b_in, bsm, "in")
        fw.dma("sp", sel[:, :, :], P.sel8, P.b_in, bsel, "in")
        V = nc.vector
        fw.op("dve", lambda: V.tensor_scalar(sm[:, 3:5], sm[:, 0:2], 1.0 / 15.0, None, ALU.mult), reads=[bsm], writes=[bsm])
        fw.op("dve", lambda: V.tensor_scalar(sm[:, 5:6], sm[:, 2:3], -1.0, 30000.0, ALU.add, ALU.mult), reads=[bsm], writes=[bsm])
        fw.op("act", lambda: nc.scalar.activation(A1[:, :], A1[:, :], AF.Tanh, bias=sm[:, 3:4], scale=1.0 / 15.0),
              reads=[b1, bsm], writes=[b1])
        fw.op("dve", lambda: V.tensor_scalar(A1[:, :], A1[:, :], 15.0, None, ALU.mult), reads=[b1], writes=[b1])
        fw.op("act", lambda: nc.scalar.activation(A2[:, :], A2[:, :], AF.Tanh, bias=sm[:, 4:5], scale=1.0 / 15.0),
              reads=[b2, bsm], writes=[b2])
        fw.op("act", lambda: nc.scalar.activation(A2[:, :], A2[:, :], AF.Exp, scale=-15.0), reads=[b2], writes=[b2])
        fw.op("act", lambda: nc.scalar.activation(A2[:, :], A2[:, :], AF.Ln, bias=1.0, scale=1.0), reads=[b2], writes=[b2])
        fw.op("dve", lambda: V.tensor_scalar(A2[:, :], A2[:, :], -1.0, None, ALU.mult), reads=[b2], writes=[b2])
        fw.op("dve", lambda: V.tensor_scalar(A2[:, 0:NEUT], A2[:, 0:NEUT], sm[:, 2:3], None, ALU.mult),
              reads=[b2, bsm], writes=[b2])
        fw.op("dve", lambda: V.tensor_scalar(A1[:, 0:NEUT], A1[:, 0:NEUT], sm[:, 2:3], sm[:, 5:6], ALU.mult, ALU.add),
              reads=[b1, bsm], writes=[b1])
        fw.op("pool", lambda: nc.gpsimd.memset(A4[:, :], 0.0), writes=[b4])
        fw.op("dve", lambda: V.tensor_tensor_scan(A3[:, :], A2[:, :], A4[:, 0:TALL], 0.0, ALU.add, ALU.add),
              reads=[b2, b4], writes=[b3])
        fw.op("dve", lambda: V.tensor_tensor(A1[:, :], A1[:, :], A3[:, :], ALU.subtract), reads=[b1, b3], writes=[b1])
        fw.op("dve", lambda: V.tensor_tensor_scan(A4[:, 1:TALL + 1], A1[:, :], A1[:, :], 0.0, ALU.max, ALU.max),
              reads=[b1], writes=[b4])
        fw.op("dve", lambda: V.tensor_scalar(A4[:, 1:TALL + 1], A4[:, 1:TALL + 1], -1.0, None, ALU.mult),
              reads=[b4], writes=[b4])
        for c, (t0, L, _) in enumerate(CHUNKS):
            fw.op("act", lambda: nc.scalar.activation(A1[:, t0:t0 + L], A1[:, t0:t0 + L], AF.Exp,
                                                      bias=A4[:, t0:t0 + 1], scale=1.0),
                  reads=[b1, b4], writes=[b1], signal=False)
            fw.op("act", lambda: nc.scalar.activation(A3[:, t0:t0 + L], A3[:, t0:t0 + L], AF.Exp,
                                                      bias=A4[:, t0:t0 + 1], scale=-1.0),
                  reads=[b3, b4], writes=[b3], signal=(c == NCH - 1))
        for c, (t0, L, _) in enumerate(CHUNKS):
            pst, bpt = (ps1, bp1) if c % 2 == 0 else (ps2, bp2)
            fw.op("pe", lambda: nc.tensor.transpose(pst[0:L, 0:8], A1[:, t0:t0 + L], cs.ident[0:8, 0:8]),
                  reads=[b1, cs.b], writes=[bpt], signal=False)
            fw.op("pe", lambda: nc.tensor.transpose(pst[0:L, 8:16], A3[:, t0:t0 + L], cs.ident[0:8, 0:8]),
                  reads=[b3, cs.b], writes=[bpt])
            fw.op("dve", lambda: V.tensor_copy(efc[0:L, c, :], pst[0:L, 0:16]), reads=[bpt], writes=[b_efc])
        for h in range(8):
            pst, bpt = (ps1, bp1) if h % 2 == 0 else (ps2, bp2)
            fw.op("pe", lambda: nc.tensor.matmul(pst[:, 0:17], sel[:, h, :], A4[:, 0:NEUT + 1:128], start=True, stop=True),
                  reads=[bsel, b4], writes=[bpt], signal=False)
            fw.op("pe", lambda: nc.tensor.matmul(pst[:, 17:34], sel[:, h, :], A4[:, NEUT + 16:TALL + 1:128], start=True, stop=True),
                  reads=[bsel, b4], writes=[bpt])
            fw.op("dve", lambda: V.tensor_copy(zb[:, h, :], pst[:, 0:34]), reads=[bpt], writes=[bzb])
        fw.op("dve", lambda: V.tensor_tensor(lam[:, :, :], zb[:, :, 1:34], zb[:, :, 0:33], ALU.subtract),
              reads=[bzb], writes=[b_lam])
        fw.op("act", lambda: nc.scalar.activation(lam[:, :, :], lam[:, :, :], AF.Exp), reads=[b_lam], writes=[b_lam])


def mlstm(fw, cs, P):
    import os
    nc = fw.nc
    V = nc.vector
    A = nc.scalar
    G = nc.gpsimd
    efc = nc.alloc_sbuf_tensor("m_efc", [128, NCH, 16], F32)
    lam = nc.alloc_sbuf_tensor("m_lam", [128, 8, NCH], F32)
    b_efc, b_lam = Buf("efc"), Buf("lam")
    mlstm_gates(fw, cs, P, efc, b_efc, lam, b_lam)
    if getattr(P, "debug_gates", False):
        fw.dma("sp", P.dbg_efc, efc[:, :, :].rearrange("p a b -> p (a b)"), b_efc, P.b_dbg_efc, "out")
        fw.dma("sp", P.dbg_lam, lam[:, :, :].rearrange("p a b -> p (a b)"), b_lam, P.b_dbg_lam, "out")
        return
    groups = [(256 * g, 256, [2 * g, 2 * g + 1]) for g in range(8)] + [(NEUT, 16, [16])] + \
             [(NEUT + 16 + 256 * g, 256, [17 + 2 * g, 18 + 2 * g]) for g in range(8)]
    with Scope(fw) as sc:
        kTg = [sc.sb([128, 16, 256], BF16, f"m_k{i}") for i in range(2)]
        vTg = [sc.sb([128, 32, 256], BF16, f"m_v{i}") for i in range(2)]
        qTg = [sc.sb([128, 16, 256], BF16, f"m_q{i}") for i in range(2)]
        ogg = [sc.sb([128, 32, 256], BF16, f"m_o{i}", multi=True) for i in range(2)]
        Cst = [sc.sb([128, 2, 513], F32, f"m_C{h}") for h in range(8)]
        Cbf = [sc.sb([128, 2, 513], BF16, f"m_Cb{h}") for h in range(8)]
        Clt = [sc.sb([128, 513], F32, f"m_Cl{i}") for i in range(2)]
        NR = 4
        kE = [sc.sb([128, 256], BF16, f"m_kE{i}") for i in range(NR)]
        vx = [sc.sb([128, 513], BF16, f"m_vx{i}", multi=True) for i in range(NR)]
        Sp = [sc.sb([128, 128], BF16, f"m_Sp{i}") for i in range(NR)]
        junk = [sc.sb([128, 512], BF16, f"m_jk{i}") for i in range(2)]
        hn = [sc.sb([128, 512], BF16, f"m_hn{i}") for i in range(NR)]
        sml = [sc.sb([128, 8], F32, f"m_sm{i}") for i in range(NR)]
        recl = [sc.sb([128, 1], F32, f"m_rec{i}") for i in range(NR)]
        scal = [sc.sb([128, 1], F32, f"m_scl{i}") for i in range(NR)]
        mask, b_mask = sc.sb([128, 128], BF16, "m_mask")
        gout, b_gout = sc.sb([128, 32], F32, "m_gout")
        B0, b0 = sc.ps([128, 1024], BF16, "m_B0")
        B1, bb1 = sc.ps(name="m_B1")
        B2, bb2 = sc.ps(name="m_B2")
        B3, bb3 = sc.ps(name="m_B3")
        B4, bb4 = sc.ps([128, 1024], BF16, "m_B4")
        B5, bb5 = sc.ps(name="m_B5")
        B6, bb6 = sc.ps(name="m_B6")
        B7, bb7 = sc.ps(name="m_B7")
        fw.dma("sp", mask[:, :], P.maskT, P.b_in, b_mask, "in")
        fw.dma("sp", gout[:, :], P.gout_l, P.b_in, b_gout, "in")
        for h in range(8):
            fw.op("pool", lambda: G.memset(Cst[h][0][:, :, :], 0.0), writes=[Cst[h][1]])
            fw.op("pool", lambda: G.memset(Cbf[h][0][:, :, :], 0.0), writes=[Cbf[h][1]])
        for i in range(NR):
            fw.op("pool", lambda: G.memset(vx[i][0][:, 512:513], 1.0), writes=[vx[i][1]])

        items = []
        for gi, (g0, gn, chs) in enumerate(groups):
            for c in chs:
                for h in range(8):
                    items.append((gi, c, h))
        loaded = set()

        def load_group(gi):
            if gi in loaded or gi >= len(groups):
                return
            loaded.add(gi)
            g0, gn, chs = groups[gi]
            main = CHUNKS[chs[0]][2]
            kt, bk = kTg[gi % 2]
            vt, bv = vTg[gi % 2]
            fw.dma("sp", kt[:, :, 0:gn], P.kT[:, g0:g0 + gn].rearrange("(c p) t -> p c t", p=128), P.b_kT, bk, "in")
            fw.dma("sp", vt[:, :, 0:gn], P.vT[:, g0:g0 + gn].rearrange("(c p) t -> p c t", p=128), P.b_vT, bv, "in")
            if main:
                qt, bq = qTg[gi % 2]
                ot, bo = ogg[gi % 2]
                l0 = g0
                fw.dma("sp", qt[:, :, 0:gn], P.qT[:, l0:l0 + gn].rearrange("(c p) t -> p c t", p=128), P.b_qT, bq, "in")
                fw.dma("sp", ot[:, :, 0:gn], P.ogT[:, l0:l0 + gn].rearrange("(c p) t -> p c t", p=128), P.b_ogT, bo, "in")

        def ctx(idx):
            gi, c, h = items[idx]
            g0, gn, chs = groups[gi]
            t0, L, main = CHUNKS[c]
            return gi, c, h, t0 - g0, L, main

        def stage1(idx):
            gi, c, h, o, L, main = ctx(idx)
            r = idx % NR
            kt, bk = kTg[gi % 2]
            vt, bv = vTg[gi % 2]
            parts = os.environ.get("S1_PARTS", "12")
            if "1" in parts:
                for j in range(2):
                    fw.op("pe", lambda: nc.tensor.transpose(B0[0:L, j * 128:(j + 1) * 128], kt[:, 2 * h + j, o:o + L], cs.identb[:, :]),
                          reads=[bk, cs.b], writes=[b0], signal=(j == 1))
                fw.op("dve", lambda: V.tensor_scalar(kE[r][0][0:L, :], B0[0:L, 0:256], efc[0:L, c, h:h + 1], None, ALU.mult),
                      reads=[b0, b_efc], writes=[kE[r][1]])
            if "2" in parts:
                for j in range(4):
                    fw.op("pe", lambda: nc.tensor.transpose(B0[0:L, 256 + j * 128:256 + (j + 1) * 128], vt[:, 4 * h + j, o:o + L], cs.identb[:, :]),
                          reads=[bv, cs.b], writes=[b0], signal=(j == 3))
                fw.op("act", lambda: A.copy(vx[r][0][0:L, 0:512], B0[0:L, 256:768]), reads=[b0], writes=[vx[r][1]])
            if main:
                qt, bq = qTg[gi % 2]
                for j in range(2):
                    fw.op("pe", lambda: nc.tensor.matmul(B1[0:L, 0:L], kt[:, 2 * h + j, o:o + L], qt[:, 2 * h + j, o:o + L],
                                                         start=(j == 0), stop=(j == 1)),
                          reads=[bk, bq], writes=[bb1], signal=(j == 1))
                fw.op("dve", lambda: V.scalar_tensor_tensor(Sp[r][0][0:L, 0:L], B1[0:L, 0:L], efc[0:L, c, h:h + 1],
                                                            mask[0:L, 0:L], ALU.mult, ALU.mult),
                      reads=[bb1, b_efc, b_mask], writes=[Sp[r][1]])

        def stage2(idx):
            gi, c, h, o, L, main = ctx(idx)
            r = idx % NR
            Ct, bC = Cst[h]
            Cb, bCb = Cbf[h]
            if main:
                qt, bq = qTg[gi % 2]
                sm_t, b_sm = sml[r]
                for j in range(2):
                    fw.op("pe", lambda: nc.tensor.matmul(B2[0:L, 0:512], qt[:, 2 * h + j, o:o + L], Cb[:, j, 0:512],
                                                         start=(j == 0), stop=False),
                          reads=[bq, bCb], writes=[bb2], signal=False)
                fw.op("pe", lambda: nc.tensor.matmul(B2[0:L, 0:512], Sp[r][0][0:L, 0:L], vx[r][0][0:L, 0:512], start=False, stop=True),
                      reads=[Sp[r][1], vx[r][1]], writes=[bb2])
                for j in range(2):
                    fw.op("pe", lambda: nc.tensor.matmul(B3[0:L, 0:1], qt[:, 2 * h + j, o:o + L], Cb[:, j, 512:513],
                                                         start=(j == 0), stop=False),
                          reads=[bq, bCb], writes=[bb3], signal=False)
                fw.op("pe", lambda: nc.tensor.matmul(B3[0:L, 0:1], Sp[r][0][0:L, 0:L], vx[r][0][0:L, 512:513], start=False, stop=True),
                      reads=[Sp[r][1], vx[r][1]], writes=[bb3])
                fw.op("act", lambda: A.activation(sm_t[0:L, 0:1], B3[0:L, 0:1], AF.Abs), reads=[bb3], writes=[b_sm])
                fw.op("dve", lambda: V.tensor_tensor(sm_t[0:L, 0:1], sm_t[0:L, 0:1], efc[0:L, c, 8 + h:9 + h], ALU.max),
                      reads=[b_sm, b_efc], writes=[b_sm])
                rc_t, b_rc = recl[r]
                sl_t, b_sl = scal[r]
                fw.op("dve", lambda: V.reciprocal(rc_t[0:L, 0:1], sm_t[0:L, 0:1]), reads=[b_sm], writes=[b_rc])
                jk, b_jk = junk[idx % 2]
                fw.op("act", lambda: A.activation(jk[0:L, :], B2[0:L, 0:512], AF.Square, scale=rc_t[0:L, 0:1], accum_out=sm_t[0:L, 2:3]),
                      reads=[bb2, b_rc], writes=[b_jk, b_sm])
                fw.op("act", lambda: A.activation(sm_t[0:L, 3:4], sm_t[0:L, 2:3], AF.Ln, bias=EPS, scale=1.0 / MDV),
                      reads=[b_sm], writes=[b_sm])
                fw.op("act", lambda: A.activation(sm_t[0:L, 3:4], sm_t[0:L, 3:4], AF.Exp, scale=-0.5), reads=[b_sm], writes=[b_sm])
                fw.op("dve", lambda: V.tensor_tensor(sl_t[0:L, 0:1], sm_t[0:L, 3:4], rc_t[0:L, 0:1], ALU.mult), reads=[b_sm, b_rc], writes=[b_sl])
                fw.op("act", lambda: A.activation(hn[r][0][0:L, :], B2[0:L, 0:512], AF.Copy, scale=sl_t[0:L, 0:1]),
                      reads=[bb2, b_sl], writes=[hn[r][1]])
            lsc = lam[:, h, c:c + 1]
            for j, (Bj, bbj) in enumerate(((B5, bb5), (B6, bb6))):
                fw.op("pe", lambda: nc.tensor.matmul(Bj[:, 0:512], kE[r][0][0:L, j * 128:(j + 1) * 128], vx[r][0][0:L, 0:512], start=True, stop=True),
                      reads=[kE[r][1], vx[r][1]], writes=[bbj])
                fw.op("pe", lambda: nc.tensor.matmul(B7[:, j:j + 1], kE[r][0][0:L, j * 128:(j + 1) * 128], vx[r][0][0:L, 512:513], start=True, stop=True),
                      reads=[kE[r][1], vx[r][1]], writes=[bb7])
                cl, b_cl = Clt[j]
                fw.op("pool", lambda: G.tensor_scalar(cl[:, :], Ct[:, j, :], lsc, 0.0, ALU.mult, ALU.add), reads=[bC, b_lam], writes=[b_cl])
                fw.op("dve", lambda: V.scalar_tensor_tensor(Ct[:, j, 0:512], Bj[:, 0:512], lsc, cl[:, 0:512], ALU.mult, ALU.add),
                      reads=[bbj, b_lam, b_cl], writes=[bC])
                fw.op("dve", lambda: V.scalar_tensor_tensor(Ct[:, j, 512:513], B7[:, j:j + 1], lsc, cl[:, 512:513], ALU.mult, ALU.add),
                      reads=[bb7, b_lam, b_cl], writes=[bC])
                fw.op("act", lambda: A.copy(Cb[:, j, :], Ct[:, j, :]), reads=[bC], writes=[bCb])

        def stage3(idx):
            gi, c, h, o, L, main = ctx(idx)
            if not main:
                return
            r = idx % NR
            ot, bo = ogg[gi % 2]
            for j in range(4):
                fw.op("pe", lambda: nc.tensor.transpose(B4[:, j * 128:j * 128 + L], hn[r][0][0:L, j * 128:(j + 1) * 128], cs.identb[0:L, 0:L]),
                      reads=[hn[r][1], cs.b], writes=[bb4], signal=(j == 3))
            for j in range(4):
                fw.op("dve", lambda: V.scalar_tensor_tensor(ot[:, 4 * h + j, o:o + L], B4[:, j * 128:j * 128 + L], gout[:, 4 * h + j:4 * h + j + 1],
                                                            ot[:, 4 * h + j, o:o + L], ALU.mult, ALU.mult),
                      reads=[bb4, b_gout, bo], writes=[bo], signal=(j == 3))
            g0, gn, chs = groups[gi]
            if h == 7 and c == chs[-1]:
                l0 = g0
                fw.dma("pool", P.yT[:, l0:l0 + gn].rearrange("(c p) t -> p c t", p=128), ot[:, :, 0:gn], bo, P.b_yT, "out")

        load_group(0)
        load_group(1)
        import os
        n = len(items)
        if os.environ.get("MLSTM_ITEMS"):
            n = int(os.environ["MLSTM_ITEMS"])
        for it in range(n + 2):
            if it < n:
                stage1(it)
            if 0 <= it - 1 < n and not os.environ.get("MLSTM_SKIP2"):
                stage2(it - 1)
            if 0 <= it - 2 < n:
                stage3(it - 2)
                gi, c, h = items[it - 2]
                if h == 7 and c == groups[gi][2][-1]:
                    load_group(gi + 2)


def w_out_phase(fw, cs, P, sg=None):
    sg = sg or seg_L(P)
    with Scope(fw) as sc:
        ar = Arena(sc, Tn=sg.Tn)
        gemm(fw, ar, P.yT[:, sg.off:sg.off + sg.Tn], P.b_yT, D, P.w_out, P.b_in, cb_range(0, D),
             resid_epilogue(fw, sc, sg.hT, sg.b_hT), tts=sg.tts, Tn=sg.Tn)


class Params:
    pass


def declare(nc, name, shape, dtype, kind):
    return nc.dram_tensor(name, list(shape), dtype, kind=kind).ap()


def build_layer0(nc, P, fw, cs):
    transpose_in(fw, cs, P.xloc, P.b_in, P.hT, P.b_hT, T)
    transpose_in(fw, cs, P.xpre, P.b_in, P.hpT, P.b_hpT, TPRE)
    norm_fm(fw, cs, P.hT, P.b_hT, P.a_norm_g, P.b_in, P.xnT, P.b_xnT, T, D)
    norm_fm(fw, cs, P.hpT, P.b_hpT, P.a_norm_g, P.b_in, P.xnpT, P.b_xnpT, TPRE, D)
    l0_proj(fw, cs, P)
    mlstm(fw, cs, P)
    for sg in (seg_L(P), seg_P(P)):
        w_out_phase(fw, cs, P, sg)
        ffn(fw, cs, P, 0, sg)


INPUTS = {
    "xloc": ([T, D], F32), "xpre": ([TPRE, D], F32),
    "w_in": ([D, 12304], F32), "w_out": ([D, D], F32),
    "w_gu0": ([D, 2 * DFF], F32), "w_gu1": ([D, 2 * DFF], F32),
    "w_dn0": ([DFF, D], F32), "w_dn1": ([DFF, D], F32),
    "kv_w_down": ([D, 576], F32), "kv_w_up": ([512, 16384], F32),
    "w_dq": ([D, 1024], F32), "w_uq": ([1024, 12288], F32), "w_o": ([8192, D], F32),
    "a_norm_g": ([128, 32], F32), "ffn_g0": ([128, 32], F32), "ffn_g1": ([128, 32], F32),
    "kv_norm_g": ([128, 32], F32), "b_norm_g": ([128, 32], F32),
    "kv_lat_g": ([128, 4], F32), "q_lat_g": ([128, 8], F32), "gout_l": ([128, 32], F32),
    "kq_g": ([128, 8], F32),
    "mparams": ([8, 3], F32), "sel8": ([8, 8, 128], F32), "maskT": ([128, 128], BF16),
    "ident_f": ([128, 128], F32), "ident_b": ([128, 128], BF16), "ones_b": ([128, 128], BF16),
    "posf": ([1, TALL], I32), "invf": ([32, 1], F32), "aflag": ([128, 2], F32),
}
SCRATCH = {
    "hT": ([D, T], F32), "hpT": ([D, TPRE], F32), "xnT": ([D, T], BF16), "xnpT": ([D, TPRE], BF16),
    "qT": ([2048, TALL], BF16), "kT": ([2048, TALL], BF16), "vT": ([4096, TALL], BF16), "ogT": ([D, TALL], BF16),
    "graw": ([16, TALL], F32), "yT": ([D, TALL], BF16), "hidT": ([DFF, TPRE], BF16),
}


def make_P(nc, need, outputs=()):
    P = Params()
    P.b_in = Buf("inputs", multi=True)
    P.ffn_g, P.w_gu, P.w_dn = {}, {}, {}
    for name in need:
        shape, dt = INPUTS[name]
        ap = declare(nc, name, shape, dt, "ExternalInput")
        setattr(P, name, ap)
    for l in (0, 1):
        if f"ffn_g{l}" in need:
            P.ffn_g[l] = getattr(P, f"ffn_g{l}")
            P.w_gu[l] = getattr(P, f"w_gu{l}")
            P.w_dn[l] = getattr(P, f"w_dn{l}")
    for name, (shape, dt) in SCRATCH.items():
        kind = "ExternalOutput" if name in outputs else "Internal"
        setattr(P, name, declare(nc, name, shape, dt, kind))
        setattr(P, "b_" + name, Buf(name, multi=True))
    return P


def host_consts():
    import ml_dtypes
    bf = ml_dtypes.bfloat16
    sel8 = np.zeros((8, 8, 128), np.float32)
    for h in range(8):
        sel8[h, h, :] = 1.0
    s = np.arange(128)
    maskT = (s[:, None] <= s[None, :]).astype(np.float32).astype(bf)
    inv = (1.0 / (10000.0 ** (np.arange(0, 64, 2, dtype=np.float32) / 64.0))).astype(np.float32)
    return {
        "sel8": sel8, "maskT": maskT, "ident_f": np.eye(128, dtype=np.float32),
        "ident_b": np.eye(128, dtype=np.float32).astype(bf), "ones_b": np.ones((128, 128), np.float32).astype(bf),
        "invf": inv.reshape(32, 1),
    }


def lay(g, nk):
    return np.ascontiguousarray(np.asarray(g, np.float32).reshape(nk, 128).T)


def host_inputs(inputs, core):
    b, s = core // 2, core % 2
    x = inputs["x"][b]
    meta = inputs["meta_tokens"]
    if s == 0:
        xloc = x[0:2048]
        xpre = np.concatenate([np.zeros((NEUT, D), np.float32), meta], axis=0)
    else:
        xloc = x[2048:4096]
        xpre = np.concatenate([meta, x[0:2048]], axis=0)
    m = {
        "xloc": np.ascontiguousarray(xloc, dtype=np.float32), "xpre": np.ascontiguousarray(xpre, dtype=np.float32),
        "w_in": inputs["a_w_in"][0], "w_out": inputs["a_w_out"][0],
        "w_gu0": inputs["ffn_w_gate_up"][0], "w_gu1": inputs["ffn_w_gate_up"][1],
        "w_dn0": inputs["ffn_w_down"][0], "w_dn1": inputs["ffn_w_down"][1],
        "kv_w_down": inputs["kv_w_down"], "kv_w_up": inputs["kv_w_up"],
        "w_dq": inputs["b_w_dq"][0], "w_uq": inputs["b_w_uq"][0], "w_o": inputs["b_w_o"][0],
        "a_norm_g": lay(inputs["a_norm_g"][0], 32), "ffn_g0": lay(inputs["ffn_norm_g"][0], 32),
        "ffn_g1": lay(inputs["ffn_norm_g"][1], 32), "kv_norm_g": lay(inputs["kv_norm_g"], 32),
        "b_norm_g": lay(inputs["b_norm_g"][0], 32), "kv_lat_g": lay(inputs["kv_latent_norm_g"], 4),
        "q_lat_g": lay(inputs["b_q_latent_norm_g"][0], 8), "gout_l": lay(inputs["a_out_norm_g"][0], 32),
        "mparams": np.stack([np.asarray(inputs["a_b_i"][0], np.float32), np.asarray(inputs["a_b_f"][0], np.float32),
                             np.full(8, float(s), np.float32)], axis=1),
    }
    kq = np.zeros((128, 8), np.float32)
    gk = np.asarray(inputs["k_norm_g"], np.float32)
    gq = np.asarray(inputs["q_norm_g"][0], np.float32)
    kq[:, 0] = gk[0:128]
    kq[0:32, 1] = gk[128:160]
    kq[0:32, 2] = gk[160:192]
    kq[:, 3] = gq[0:128]
    kq[0:32, 4] = gq[128:160]
    kq[0:32, 5] = gq[160:192]
    m["kq_g"] = kq
    pos = np.asarray(inputs["positions"][b], np.int32)
    metap = np.arange(16, dtype=np.int32) - 16
    if s == 0:
        posf = np.concatenate([np.zeros(NEUT, np.int32), metap, pos[0:2048]])
    else:
        posf = np.concatenate([metap, pos[0:2032], pos[2032:4096]])
    m["posf"] = posf.reshape(1, TALL).astype(np.int32)
    af = np.full((128, 2), -10.0, np.float32)
    if s == 0:
        af[:, 1] = -30000.0
    m["aflag"] = af
    m.update(host_consts())
    return m


NH = 64
TT_ALL = TT_P + [(TPRE + i * 512, 512) for i in range(4)]
TWO_PI = 6.283185307179586
C1_2PI = 6.28125
C2_2PI = TWO_PI - C1_2PI

SCRATCH.update({
    "aT": ([576, T], F32), "apT": ([576, TPRE], F32), "ckvnT": ([512, TALL], BF16),
    "kTh": ([NH * 192, TALL], BF16), "vtok": ([TALL, NH * 128], BF16),
    "cqT": ([1024, T], F32), "cqnT": ([1024, T], BF16), "qTh": ([NH * 192, T], BF16),
    "oT": ([NH * 128, T], BF16),
})


def kv_down(fw, cs, P):
    for sg, dst, b_dst in ((seg_L(P), P.aT, P.b_aT), (seg_P(P), P.apT, P.b_apT)):
        norm_fm(fw, cs, sg.hT, sg.b_hT, P.kv_norm_g, P.b_in, sg.xnT, sg.b_xnT, sg.Tn, D)
        with Scope(fw) as sc:
            ar = Arena(sc, Tn=sg.Tn)
            routes = [(0, 576, dst, 0, 0, None, 1.0, F32, b_dst)]
            gemm(fw, ar, sg.xnT, sg.b_xnT, D, P.kv_w_down, P.b_in, cb_range(0, 512) + [[(512, 32), (544, 32)]],
                 store_epilogue(fw, sc, routes), tts=sg.tts, Tn=sg.Tn)


def rope_tables(fw, P, cosT, sinT, b_tab, sc):
    nc = fw.nc
    V = nc.vector
    A = nc.scalar
    pi_, b_pi = sc.sb([32, TALL], I32, "rp_i")
    ang, b_ang = sc.sb([32, TALL], F32, "rp_a")
    kk, b_kk = sc.sb([32, TALL], F32, "rp_k")
    inv, b_inv = sc.sb([32, 1], F32, "rp_inv")
    fw.dma("sp", pi_[:, :], P.posf.partition_broadcast(32), P.b_in, b_pi, "in")
    fw.dma("sp", inv[:, :], P.invf, P.b_in, b_inv, "in")
    fw.op("dve", lambda: V.tensor_copy(ang[:, :], pi_[:, :]), reads=[b_pi], writes=[b_ang])
    fw.op("dve", lambda: V.tensor_scalar(ang[:, :], ang[:, :], 16.0, inv[:, 0:1], ALU.add, ALU.mult),
          reads=[b_ang, b_inv], writes=[b_ang])
    fw.op("dve", lambda: V.tensor_scalar(kk[:, :], ang[:, :], 1.0 / TWO_PI, 12582912.0, ALU.mult, ALU.add),
          reads=[b_ang], writes=[b_kk])
    fw.op("dve", lambda: V.tensor_scalar(kk[:, :], kk[:, :], 12582912.0, None, ALU.subtract), reads=[b_kk], writes=[b_kk])
    fw.op("dve", lambda: V.scalar_tensor_tensor(ang[:, :], kk[:, :], -C1_2PI, ang[:, :], ALU.mult, ALU.add),
          reads=[b_kk, b_ang], writes=[b_ang])
    fw.op("dve", lambda: V.scalar_tensor_tensor(ang[:, :], kk[:, :], -C2_2PI, ang[:, :], ALU.mult, ALU.add),
          reads=[b_kk, b_ang], writes=[b_ang])
    fw.op("dve", lambda: V.tensor_scalar(ang[:, :], ang[:, :], 3.1415925, -3.1415925, ALU.min, ALU.max),
          reads=[b_ang], writes=[b_ang])
    fw.op("act", lambda: A.activation(sinT[:, :], ang[:, :], AF.Sin), reads=[b_ang], writes=[b_tab])
    fw.op("act", lambda: A.activation(kk[:, :], ang[:, :], AF.Abs), reads=[b_ang], writes=[b_kk])
    fw.op("act", lambda: A.activation(cosT[:, :], kk[:, :], AF.Sin, bias=1.5707963, scale=-1.0), reads=[b_kk], writes=[b_tab])


def a_src(P, t0):
    return (P.apT, P.b_apT, t0) if t0 < TPRE else (P.aT, P.b_aT, t0 - TPRE)


def kv_k(fw, cs, P):
    nc = fw.nc
    V, A, G = nc.vector, nc.scalar, nc.gpsimd
    norm_fm(fw, cs, P.apT[0:512, :], P.b_apT, P.kv_lat_g, P.b_in, P.ckvnT[:, 0:TPRE], P.b_ckvnT, TPRE, 512)
    norm_fm(fw, cs, P.aT[0:512, :], P.b_aT, P.kv_lat_g, P.b_in, P.ckvnT[:, TPRE:TALL], P.b_ckvnT, T, 512)
    with Scope(fw) as sc:
        cosT, _ = sc.sb([32, TALL], F32, "kk_cos")
        sinT, _ = sc.sb([32, TALL], F32, "kk_sin")
        b_tab = sc.track(Buf("kk_tab", multi=True))
        kr1, b_kr1 = sc.sb([32, TALL], F32, "kk_kr1", multi=True)
        kr2, b_kr2 = sc.sb([32, TALL], F32, "kk_kr2", multi=True)
        ssr, b_ssr = sc.sb([128, TALL], F32, "kk_ssr", multi=True)
        gk, b_gk = sc.sb([128, 8], F32, "kk_g")
        fw.dma("sp", gk[:, :], P.kq_g, P.b_in, b_gk, "in")
        with Scope(fw) as s2:
            rope_tables(fw, P, cosT, sinT, b_tab, s2)
        with Scope(fw) as s3:
            t1s = [s3.sb([32, 512], F32, f"kk_t1{i}") for i in range(2)]
            t2s = [s3.sb([32, 512], F32, f"kk_t2{i}") for i in range(2)]
            sqs = [s3.sb([32, 1024], BF16, f"kk_sq{i}") for i in range(2)]
            m1s = [s3.sb([32, 512], F32, f"kk_m1{i}") for i in range(2)]
            m2s = [s3.sb([32, 512], F32, f"kk_m2{i}") for i in range(2)]
            pss = [s3.ps(name=f"kk_ps{i}") for i in range(2)]
            for i, (t0, tn) in enumerate(TT_ALL):
                src, b_src, l0 = a_src(P, t0)
                (t1, b1), (t2, b2), (sq, bsq), (m1, bm1), (m2, bm2), (ps, bps) = \
                    t1s[i % 2], t2s[i % 2], sqs[i % 2], m1s[i % 2], m2s[i % 2], pss[i % 2]
                fw.dma("sp", t1[:, 0:tn], src[512:544, l0:l0 + tn], b_src, b1, "in")
                fw.dma("sp", t2[:, 0:tn], src[544:576, l0:l0 + tn], b_src, b2, "in")
                fw.op("act", lambda: A.activation(sq[:, 0:tn], t1[:, 0:tn], AF.Square), reads=[b1], writes=[bsq])
                fw.op("act", lambda: A.activation(sq[:, 512:512 + tn], t2[:, 0:tn], AF.Square), reads=[b2], writes=[bsq])
                fw.op("pe", lambda: nc.tensor.matmul(ps[:, 0:tn], cs.onesb[0:32, :], sq[:, 0:tn], start=True, stop=False),
                      reads=[bsq, cs.b], writes=[bps], signal=False)
                fw.op("pe", lambda: nc.tensor.matmul(ps[:, 0:tn], cs.onesb[0:32, :], sq[:, 512:512 + tn], start=False, stop=True),
                      reads=[bsq, cs.b], writes=[bps])
                fw.op("act", lambda: A.copy(ssr[:, t0:t0 + tn], ps[:, 0:tn]), reads=[bps], writes=[b_ssr])
                fw.op("dve", lambda: V.tensor_scalar(t1[:, 0:tn], t1[:, 0:tn], gk[0:32, 1:2], None, ALU.mult), reads=[b1, b_gk], writes=[b1])
                fw.op("dve", lambda: V.tensor_scalar(t2[:, 0:tn], t2[:, 0:tn], gk[0:32, 2:3], None, ALU.mult), reads=[b2, b_gk], writes=[b2])
                rope_pair(fw, t1[:, 0:tn], b1, t2[:, 0:tn], b2, cosT[:, t0:t0 + tn], sinT[:, t0:t0 + tn], b_tab,
                          m1[:, 0:tn], bm1, m2[:, 0:tn], bm2, kr1[:, t0:t0 + tn], b_kr1, kr2[:, t0:t0 + tn], b_kr2)
        with Scope(fw) as s4:
            ar = Arena(s4, Tn=TALL, kmax=4, nps=6)
            sqb = [s4.sb([128, 512], BF16, f"ke_sq{i}") for i in range(2)]
            rsb = [s4.sb([128, 512], F32, f"ke_rs{i}") for i in range(2)]
            knb = [s4.sb([128, 512], BF16, f"ke_kn{i}") for i in range(3)]
            k1b = [s4.sb([32, 1024], BF16, f"ke_k1{i}") for i in range(3)]
            pss = [s4.ps(name=f"ke_ps{i}") for i in range(2)]
            cnt = [0]

            def k_epi(sbi, t0, tn, blocks, kp, nkp):
                for (c0, ncol, ps, b_ps) in blocks:
                    h = c0 // 256
                    i = cnt[0]
                    cnt[0] += 1
                    (sq, bsq), (rs, brs), (kn, bkn), (k1, bk1), (p2, bp2) = sqb[i % 2], rsb[i % 2], knb[i % 3], k1b[i % 3], pss[i % 2]
                    fw.op("act", lambda: A.activation(sq[:, 0:tn], ps[:, 0:tn], AF.Square), reads=[b_ps], writes=[bsq])
                    fw.op("pe", lambda: nc.tensor.matmul(p2[:, 0:tn], cs.onesb[:, :], sq[:, 0:tn], start=True, stop=True),
                          reads=[bsq, cs.b], writes=[bp2])
                    fw.op("dve", lambda: V.tensor_tensor(rs[:, 0:tn], p2[:, 0:tn], ssr[:, t0:t0 + tn], ALU.add),
                          reads=[bp2, b_ssr], writes=[brs])
                    rstd_from_sumsq(fw, rs[:, 0:tn], brs, rs[:, 0:tn], brs, 1.0 / 192.0)
                    fw.op("dve", lambda: V.scalar_tensor_tensor(kn[:, 0:tn], ps[:, 0:tn], gk[:, 0:1], rs[:, 0:tn], ALU.mult, ALU.mult),
                          reads=[b_ps, b_gk, brs], writes=[bkn])
                    fw.op("pool", lambda: G.tensor_tensor(k1[:, 0:tn], kr1[:, t0:t0 + tn], rs[0:32, 0:tn], ALU.mult),
                          reads=[b_kr1, brs], writes=[bk1])
                    fw.op("pool", lambda: G.tensor_tensor(k1[:, 512:512 + tn], kr2[:, t0:t0 + tn], rs[0:32, 0:tn], ALU.mult),
                          reads=[b_kr2, brs], writes=[bk1])
                    r0 = h * 192
                    fw.dma("pool", P.kTh[r0:r0 + 128, t0:t0 + tn], kn[:, 0:tn], bkn, P.b_kTh, "out")
                    fw.dma("pool", P.kTh[r0 + 128:r0 + 160, t0:t0 + tn], k1[:, 0:tn], bk1, P.b_kTh, "out")
                    fw.dma("pool", P.kTh[r0 + 160:r0 + 192, t0:t0 + tn], k1[:, 512:512 + tn], bk1, P.b_kTh, "out")

            sbs = [[(h * 256, 128), ((h + 1) * 256, 128)] for h in range(0, NH, 2)]
            gemm(fw, ar, P.ckvnT, P.b_ckvnT, 512, P.kv_w_up, P.b_in, sbs, k_epi, tts=TT_ALL, Tn=TALL)


def rope_pair(fw, u1, b1, u2, b2, cos, sin, b_tab, m1, bm1, m2, bm2, o1, bo1, o2, bo2):
    nc = fw.nc
    V, G = nc.vector, nc.gpsimd
    fw.op("pool", lambda: G.tensor_tensor(m1, u1, cos, ALU.mult), reads=[b1, b_tab], writes=[bm1])
    fw.op("pool", lambda: G.tensor_tensor(m2, u2, sin, ALU.mult), reads=[b2, b_tab], writes=[bm2])
    fw.op("dve", lambda: V.tensor_tensor(o1, m1, m2, ALU.subtract), reads=[bm1, bm2], writes=[bo1])
    fw.op("pool", lambda: G.tensor_tensor(m1, u1, sin, ALU.mult), reads=[b1, b_tab], writes=[bm1])
    fw.op("pool", lambda: G.tensor_tensor(m2, u2, cos, ALU.mult), reads=[b2, b_tab], writes=[bm2])
    fw.op("dve", lambda: V.tensor_tensor(o2, m1, m2, ALU.add), reads=[bm1, bm2], writes=[bo2])


def kv_v(fw, cs, P):
    nc = fw.nc
    V, A, G = nc.vector, nc.scalar, nc.gpsimd
    wv = P.kv_w_up.rearrange("k (h two c) -> k h two c", two=2, c=128)
    with Scope(fw) as sc:
        act, b_act = sc.sb([128, 4, TALL], BF16, "vv_act")
        fw.dma("sp", act[:, :, :], P.ckvnT.rearrange("(c p) t -> p c t", p=128), P.b_ckvnT, b_act, "in")
        wss = [sc.sb([128, 4, 4, 128], F32, f"vv_ws{i}") for i in range(2)]
        wbs = [sc.sb([128, 4, 512], BF16, f"vv_wb{i}") for i in range(2)]
        evs = [sc.sb([128, 512], BF16, f"vv_ev{i}") for i in range(3)]
        pss = [sc.ps(name=f"vv_ps{i}") for i in range(4)]
        n = 0
        for hg in range(NH // 4):
            (ws, bws), (wb, bwb) = wss[hg % 2], wbs[hg % 2]
            for c in range(4):
                fw.dma("sp", ws[:, c, :, :], wv[c * 128:(c + 1) * 128, hg * 4:hg * 4 + 4, 1, :], P.b_in, bws, "in")
            fw.op("pool", lambda: G.tensor_copy(wb[:, :, :], ws[:, :, :, :].rearrange("p c h d -> p c (h d)")),
                  reads=[bws], writes=[bwb])
            for (t0, tn) in tok_blocks(TALL, 128):
                ps, bps = pss[n % 4]
                ev, bev = evs[n % 3]
                n += 1
                for c in range(4):
                    fw.op("pe", lambda: nc.tensor.matmul(ps[0:tn, :], act[:, c, t0:t0 + tn], wb[:, c, :], start=(c == 0), stop=(c == 3)),
                          reads=[b_act, bwb], writes=[bps], signal=(c == 3))
                evac_copy(fw, ev[0:tn, :], bev, ps[0:tn, :], bps, n)
                fw.dma("pool", P.vtok[t0:t0 + tn, hg * 512:(hg + 1) * 512], ev[0:tn, :], bev, P.b_vtok, "out")


def q_proj(fw, cs, P):
    nc = fw.nc
    V, A, G = nc.vector, nc.scalar, nc.gpsimd
    norm_fm(fw, cs, P.hT, P.b_hT, P.b_norm_g, P.b_in, P.xnT, P.b_xnT, T, D)
    with Scope(fw) as sc:
        ar = Arena(sc)
        routes = [(0, 1024, P.cqT, 0, 0, None, 1.0, F32, P.b_cqT)]
        gemm(fw, ar, P.xnT, P.b_xnT, D, P.w_dq, P.b_in, cb_range(0, 1024), store_epilogue(fw, sc, routes))
    norm_fm(fw, cs, P.cqT, P.b_cqT, P.q_lat_g, P.b_in, P.cqnT, P.b_cqnT, T, 1024)
    with Scope(fw) as sc:
        cosT, _ = sc.sb([32, TALL], F32, "qq_cos")
        sinT, _ = sc.sb([32, TALL], F32, "qq_sin")
        b_tab = sc.track(Buf("qq_tab", multi=True))
        gq, b_gq = sc.sb([128, 8], F32, "qq_g")
        fw.dma("sp", gq[:, :], P.kq_g, P.b_in, b_gq, "in")
        with Scope(fw) as s2:
            rope_tables(fw, P, cosT, sinT, b_tab, s2)
        with Scope(fw) as s4:
            ar = Arena(s4, Tn=T, kmax=8, nps=6)
            sqb = [s4.sb([128, 512], BF16, f"qe_sq{i}") for i in range(2)]
            sqa = [s4.sb([32, 1024], BF16, f"qe_sa{i}") for i in range(2)]
            rsb = [s4.sb([128, 512], F32, f"qe_rs{i}") for i in range(2)]
            qnb = [s4.sb([128, 512], BF16, f"qe_qn{i}") for i in range(3)]
            uab = [s4.sb([32, 1024], F32, f"qe_u{i}") for i in range(2)]
            m1b = [s4.sb([32, 512], F32, f"qe_m1{i}") for i in range(2)]
            m2b = [s4.sb([32, 512], F32, f"qe_m2{i}") for i in range(2)]
            o1b = [s4.sb([32, 1024], BF16, f"qe_o{i}", multi=True) for i in range(3)]
            pss = [s4.ps(name=f"qe_ps{i}") for i in range(2)]
            cnt = [0]

            def q_epi(sbi, t0, tn, blocks, kp, nkp):
                (cn, _, psn, bpn), (ca, _, psa, bpa), (cb_, _, psb, bpb) = blocks
                h = cn // 192
                i = cnt[0]
                cnt[0] += 1
                (sq, bsq), (sa, bsa), (rs, brs), (qn, bqn), (ua, bua) = sqb[i % 2], sqa[i % 2], rsb[i % 2], qnb[i % 3], uab[i % 2]
                (m1, bm1), (m2, bm2), (o1, bo1), (p2, bp2) = m1b[i % 2], m2b[i % 2], o1b[i % 3], pss[i % 2]
                fw.op("act", lambda: A.activation(sq[:, 0:tn], psn[:, 0:tn], AF.Square), reads=[bpn], writes=[bsq])
                fw.op("act", lambda: A.activation(sa[:, 0:tn], psa[0:32, 0:tn], AF.Square), reads=[bpa], writes=[bsa])
                fw.op("act", lambda: A.activation(sa[:, 512:512 + tn], psb[0:32, 0:tn], AF.Square), reads=[bpb], writes=[bsa])
                fw.op("pe", lambda: nc.tensor.matmul(p2[:, 0:tn], cs.onesb[:, :], sq[:, 0:tn], start=True, stop=False),
                      reads=[bsq, cs.b], writes=[bp2], signal=False)
                fw.op("pe", lambda: nc.tensor.matmul(p2[:, 0:tn], cs.onesb[0:32, :], sa[:, 0:tn], start=False, stop=False),
                      reads=[bsa, cs.b], writes=[bp2], signal=False)
                fw.op("pe", lambda: nc.tensor.matmul(p2[:, 0:tn], cs.onesb[0:32, :], sa[:, 512:512 + tn], start=False, stop=True),
                      reads=[bsa, cs.b], writes=[bp2])
                rstd_from_sumsq(fw, p2[:, 0:tn], bp2, rs[:, 0:tn], brs, 1.0 / 192.0)
                fw.op("dve", lambda: V.scalar_tensor_tensor(qn[:, 0:tn], psn[:, 0:tn], gq[:, 3:4], rs[:, 0:tn], ALU.mult, ALU.mult),
                      reads=[bpn, b_gq, brs], writes=[bqn])
                fw.op("dve", lambda: V.scalar_tensor_tensor(ua[:, 0:tn], psa[0:32, 0:tn], gq[0:32, 4:5], rs[0:32, 0:tn], ALU.mult, ALU.mult),
                      reads=[bpa, b_gq, brs], writes=[bua])
                fw.op("dve", lambda: V.scalar_tensor_tensor(ua[:, 512:512 + tn], psb[0:32, 0:tn], gq[0:32, 5:6], rs[0:32, 0:tn], ALU.mult, ALU.mult),
                      reads=[bpb, b_gq, brs], writes=[bua])
                tt0 = TPRE + t0
                rope_pair(fw, ua[:, 0:tn], bua, ua[:, 512:512 + tn], bua, cosT[:, tt0:tt0 + tn], sinT[:, tt0:tt0 + tn], b_tab,
                          m1[:, 0:tn], bm1, m2[:, 0:tn], bm2, o1[:, 0:tn], bo1, o1[:, 512:512 + tn], bo1)
                r0 = h * 192
                fw.dma("pool", P.qTh[r0:r0 + 128, t0:t0 + tn], qn[:, 0:tn], bqn, P.b_qTh, "out")
                fw.dma("pool", P.qTh[r0 + 128:r0 + 160, t0:t0 + tn], o1[:, 0:tn], bo1, P.b_qTh, "out")
                fw.dma("pool", P.qTh[r0 + 160:r0 + 192, t0:t0 + tn], o1[:, 512:512 + tn], bo1, P.b_qTh, "out")

            sbs = [[(h * 192, 128), (h * 192 + 128, 32), (h * 192 + 160, 32)] for h in range(NH)]
            gemm(fw, ar, P.cqnT, P.b_cqnT, 1024, P.w_uq, P.b_in, sbs, q_epi, kpass=8)


KB = [(128 * i, 128) for i in range(16)] + [(NEUT, 16)] + [(TPRE + 128 * i, 128) for i in range(16)]
ATT_SCALE = 192.0 ** -0.5


def attention(fw, cs, P):
    nc = fw.nc
    V, A, G = nc.vector, nc.scalar, nc.gpsimd
    with Scope(fw) as sc:
        tri, b_tri = sc.sb([128, 128], BF16, "at_tri")
        bias, b_bias = sc.sb([128, 2], F32, "at_bias")
        fw.dma("sp", tri[:, :], P.maskT, P.b_in, b_tri, "in")
        fw.dma("sp", bias[:, :], P.aflag, P.b_in, b_bias, "in")
        qnb = [sc.sb([128, T], BF16, f"at_qn{i}") for i in range(2)]
        qrb = [sc.sb([64, T], BF16, f"at_qr{i}") for i in range(2)]
        knb = [sc.sb([128, TALL], BF16, f"at_kn{i}") for i in range(2)]
        krb = [sc.sb([64, TALL], BF16, f"at_kr{i}") for i in range(2)]
        vgb = [sc.sb([128, 33, 512], BF16, f"at_vg{i}", multi=True) for i in range(2)]
        ptb = [sc.sb([128, 512], BF16, f"at_pt{i}") for i in range(4)]
        rcb = [sc.sb([128, 512], F32, f"at_rc{i}") for i in range(2)]
        accb = [sc.sb([128, 512], F32, f"at_acc{i}") for i in range(2)]
        onesf, b_onesf = sc.sb([128, 128], F32, "at_onesf")
        fw.op("pool", lambda: G.memset(onesf[:, :], 1.0), writes=[b_onesf])
        obb = [sc.sb([128, 512], BF16, f"at_ob{i}") for i in range(2)]
        Sb = [sc.ps(name=f"at_S{i}") for i in range(2)]
        OTb = [sc.ps(name=f"at_O{i}") for i in range(2)]
        SMb = [sc.ps(name=f"at_M{i}") for i in range(2)]

        def load_head(h):
            if h >= NH:
                return
            r0 = h * 192
            fw.dma("sp", qnb[h % 2][0][:, :], P.qTh[r0:r0 + 128, :], P.b_qTh, qnb[h % 2][1], "in")
            fw.dma("sp", qrb[h % 2][0][:, :], P.qTh[r0 + 128:r0 + 192, :], P.b_qTh, qrb[h % 2][1], "in")
            fw.dma("sp", knb[h % 2][0][:, :], P.kTh[r0:r0 + 128, :], P.b_kTh, knb[h % 2][1], "in")
            fw.dma("sp", krb[h % 2][0][:, :], P.kTh[r0 + 128:r0 + 192, :], P.b_kTh, krb[h % 2][1], "in")
            if h % 4 == 0:
                g = h // 4
                vt, bvt = vgb[g % 2]
                fw.dma("sp", vt[:, 0:16, :], P.vtok[0:NEUT, g * 512:(g + 1) * 512].rearrange("(b p) c -> p b c", p=128),
                       P.b_vtok, bvt, "in")
                fw.dma("sp", vt[0:16, 16, :], P.vtok[NEUT:TPRE, g * 512:(g + 1) * 512], P.b_vtok, bvt, "in")
                fw.dma("sp", vt[:, 17:33, :], P.vtok[TPRE:TALL, g * 512:(g + 1) * 512].rearrange("(b p) c -> p b c", p=128),
                       P.b_vtok, bvt, "in")

        units = []
        ti = 0
        for h in range(NH):
            for (t0, tn) in TT:
                us = []
                for kb, (k0, kn_) in enumerate(KB):
                    if k0 < TPRE:
                        us.append([h, ti, t0, tn, kb, k0, kn_, 0, False, k0 < NEUT])
                    else:
                        lk0 = k0 - TPRE
                        if lk0 > t0 + tn - 1:
                            continue
                        if lk0 + kn_ - 1 <= t0:
                            us.append([h, ti, t0, tn, kb, k0, kn_, 0, False, False])
                        else:
                            us.append([h, ti, t0, tn, kb, k0, kn_, lk0 - t0, True, False])
                for i, u in enumerate(us):
                    u.append(i == 0)
                    u.append(i == len(us) - 1)
                units += us
                ti += 1

        def emit_S(ui):
            h, ti, t0, tn, kb, k0, kn_, c0, diag, pre, first, last = units[ui]
            S, bS = Sb[ui % 2]
            pt, bpt = ptb[ui % 4]
            ncol = tn - c0
            (qn, bqn), (qr, bqr), (kn, bkn), (kr, bkr) = qnb[h % 2], qrb[h % 2], knb[h % 2], krb[h % 2]
            fw.op("pe", lambda: nc.tensor.matmul(S[0:kn_, 0:ncol], kn[:, k0:k0 + kn_], qn[:, t0 + c0:t0 + tn], start=True, stop=False),
                  reads=[bkn, bqn], writes=[bS], signal=False)
            fw.op("pe", lambda: nc.tensor.matmul(S[0:kn_, 0:ncol], kr[:, k0:k0 + kn_], qr[:, t0 + c0:t0 + tn], start=False, stop=True),
                  reads=[bkr, bqr], writes=[bS])
            bcol = bias[0:kn_, 1:2] if pre else bias[0:kn_, 0:1]
            fw.op("act", lambda: A.activation(pt[0:kn_, 0:ncol], S[0:kn_, 0:ncol], AF.Exp, bias=bcol, scale=ATT_SCALE),
                  reads=[bS, b_bias], writes=[bpt])
            if diag:
                fw.op("pool", lambda: G.tensor_tensor(pt[0:kn_, 0:kn_], pt[0:kn_, 0:kn_], tri[0:kn_, 0:kn_], ALU.mult),
                      reads=[bpt, b_tri], writes=[bpt])

        def emit_PV(ui):
            h, ti, t0, tn, kb, k0, kn_, c0, diag, pre, first, last = units[ui]
            pt, bpt = ptb[ui % 4]
            OT, bOT = OTb[ti % 2]
            SM, bSM = SMb[ti % 2]
            acc, bacc = accb[ti % 2]
            vt, bvt = vgb[(h // 4) % 2]
            hh = h % 4
            fw.op("pe", lambda: nc.tensor.matmul(OT[:, c0:tn], vt[0:kn_, kb, hh * 128:(hh + 1) * 128], pt[0:kn_, 0:tn - c0],
                                                 start=first, stop=last),
                  reads=[bvt, bpt], writes=[bOT], signal=False)
            fw.op("pe", lambda: nc.tensor.matmul(SM[:, c0:tn], cs.onesb[0:kn_, :], pt[0:kn_, 0:tn - c0], start=first, stop=last),
                  reads=[cs.b, bpt], writes=[bSM])
            if last:
                rc, brc = rcb[ti % 2]
                ob, bob = obb[ti % 2]
                fw.op("dve", lambda: V.reciprocal(rc[:, 0:tn], SM[:, 0:tn]), reads=[bSM], writes=[brc])
                fw.op("dve", lambda: V.tensor_tensor(ob[:, 0:tn], OT[:, 0:tn], rc[:, 0:tn], ALU.mult), reads=[bOT, brc], writes=[bob])
                fw.dma("pool", P.oT[h * 128:(h + 1) * 128, t0:t0 + tn], ob[:, 0:tn], bob, P.b_oT, "out")

        load_head(0)
        n = len(units)
        import os
        if os.environ.get("ATT_UNITS"):
            n = int(os.environ["ATT_UNITS"])
        for ui in range(n + 1):
            if ui < n:
                emit_S(ui)
            if ui >= 1:
                emit_PV(ui - 1)
            if ui < n and (ui == 0 or units[ui][0] != units[ui - 1][0]):
                load_head(units[ui][0] + 1)


def o_proj(fw, cs, P):
    with Scope(fw) as sc:
        ar = Arena(sc)
        gemm(fw, ar, P.oT, P.b_oT, NH * 128, P.w_o, P.b_in, cb_range(0, D), resid_epilogue(fw, sc, P.hT, P.b_hT))


def transpose_out(fw, cs, P, out, b_out):
    nc = fw.nc
    with Scope(fw) as sc:
        hin = [sc.sb([128, 32, 512], F32, f"to_h{i}") for i in range(2)]
        xo = [sc.sb([128, D], F32, f"to_x{i}") for i in range(2)]
        pss = [sc.ps(name=f"to_ps{i}") for i in range(4)]
        ips = 0
        nb = 0
        for gi in range(4):
            g0 = gi * 512
            h_t, h_b = hin[gi % 2]
            fw.dma("sp", h_t[:, :, :], P.hT[:, g0:g0 + 512].rearrange("(c p) t -> p c t", p=128), P.b_hT, h_b, "in")
            for bi in range(4):
                x_t, x_b = xo[nb % 2]
                nb += 1
                for c4 in range(0, 32, 4):
                    ps_t, ps_b = pss[ips % 4]
                    ips += 1
                    for j in range(4):
                        c = c4 + j
                        fw.op("pe", lambda: nc.tensor.transpose(ps_t[:, j * 128:(j + 1) * 128], h_t[:, c, bi * 128:(bi + 1) * 128],
                                                                cs.ident[:, :]),
                              reads=[h_b, cs.b], writes=[ps_b], signal=(j == 3))
                    if (c4 // 4) % 2 == 0:
                        fw.op("dve", lambda: nc.vector.tensor_copy(x_t[:, c4 * 128:(c4 + 4) * 128], ps_t[:, :]), reads=[ps_b], writes=[x_b])
                    else:
                        fw.op("act", lambda: nc.scalar.copy(x_t[:, c4 * 128:(c4 + 4) * 128], ps_t[:, :]), reads=[ps_b], writes=[x_b])
                r0 = gi * 512 + bi * 128
                fw.dma("pool", out[r0:r0 + 128, :], x_t[:, :], x_b, b_out, "out")


INPUTS.update({"hT_in": ([D, T], F32), "aT_in": ([576, T], F32), "apT_in": ([576, TPRE], F32)})
SCRATCH.update({"out": ([T, D], F32)})


def copy_dram(fw, dst, b_dst, src, b_src, rows, tmpname="cp"):
    b_tmp = Buf(tmpname)
    step = 512
    for r0 in range(0, rows, step):
        r1 = min(rows, r0 + step)
        fw.dma("sp", dst[r0:r1, :], src[r0:r1, :], b_src, b_dst, "in")


def build_layer1(fw, cs, P, out, b_out):
    kv_k(fw, cs, P)
    kv_v(fw, cs, P)
    q_proj(fw, cs, P)
    attention(fw, cs, P)
    o_proj(fw, cs, P)
    ffn(fw, cs, P, 1)
    transpose_out(fw, cs, P, out, b_out)


NEED_A = ["xloc", "xpre", "w_in", "w_out", "a_norm_g", "gout_l", "mparams", "sel8", "maskT", "ident_f", "ident_b", "ones_b",
          "w_gu0", "w_dn0", "ffn_g0", "kv_norm_g", "kv_w_down"]
NEED_B = ["hT_in", "aT_in", "apT_in", "kv_w_up", "kv_lat_g", "kq_g", "posf", "invf", "aflag", "maskT", "ident_f", "ident_b", "ones_b",
          "b_norm_g", "w_dq", "q_lat_g", "w_uq", "w_o", "ffn_g1", "w_gu1", "w_dn1"]


def build_A():
    nc = bass.Bass("TRN2", target_bir_lowering=False)
    P = make_P(nc, NEED_A, ["hT", "aT"])
    fw = FW(nc)
    cs = Consts(fw, P.ident_f, P.ident_b, P.ones_b)
    build_layer0(nc, P, fw, cs)
    kv_down(fw, cs, P)
    fw.drain([P.b_hT, P.b_aT])
    return nc


def build_B():
    nc = bass.Bass("TRN2", target_bir_lowering=False)
    P = make_P(nc, NEED_B, ["out"])
    fw = FW(nc)
    cs = Consts(fw, P.ident_f, P.ident_b, P.ones_b)
    copy_dram(fw, P.hT, P.b_hT, P.hT_in, P.b_in, D)
    P.aT, P.b_aT = P.aT_in, P.b_in
    P.apT, P.b_apT = P.apT_in, P.b_in
    build_layer1(fw, cs, P, P.out, P.b_out)
    fw.drain([P.b_out])
    return nc


def kernel_2launch(**inputs):
    inputs = {k: np.asarray(v) for k, v in inputs.items()}
    maps = [host_inputs(inputs, c) for c in range(NCORES)]
    ncA = build_A()
    resA = run_bass_kernel_spmd(ncA, [{k: m[k] for k in NEED_A} for m in maps], core_ids=list(range(NCORES)))
    hTs = [np.asarray(r["hT"]) for r in resA.results]
    aTs = [np.asarray(r["aT"]) for r in resA.results]
    del resA
    for c in range(NCORES):
        maps[c]["hT_in"] = hTs[c]
        maps[c]["aT_in"] = aTs[c]
        if c % 2 == 1:
            maps[c]["apT_in"] = np.ascontiguousarray(aTs[c - 1][:, 0:TPRE])
        else:
            maps[c]["apT_in"] = np.zeros((576, TPRE), np.float32)
    ncB = build_B()
    resB = run_bass_kernel_spmd(ncB, [{k: m[k] for k in NEED_B} for m in maps], core_ids=list(range(NCORES)))
    out = np.empty((4, 4096, D), np.float32)
    for c in range(NCORES):
        b, s = c // 2, c % 2
        out[b, s * 2048:(s + 1) * 2048] = np.asarray(resB.results[c]["out"])
    return out


SCRATCH_UNUSED = {"agT": ([NCORES * 576, T], F32)}
INPUTS.update({"selw": ([128, NCORES], F32)})


def exchange_latent(fw, cs, P):
    nc = fw.nc
    V = nc.vector
    sem = fw.get_sem()
    fw._deps("pool", [P.b_aT], [P.b_agT])
    inst = nc.gpsimd.collective_compute("AllGather", mybir.AluOpType.bypass,
                                        replica_groups=[list(range(NCORES))],
                                        ins=[P.aT[:, :]], outs=[P.agT[:, :]])
    sem.n += 16
    inst.then_inc(sem.h, 16)
    P.b_agT.w[sem] = sem.n
    P.b_aT.r[sem] = sem.n
    with Scope(fw) as sc:
        sw, b_sw = sc.sb([128, NCORES], F32, "ex_w")
        fw.dma("sp", sw[:, :], P.selw, P.b_in, b_sw, "in")
        ins_ = [sc.sb([128, 512], F32, f"ex_i{i}") for i in range(4)]
        accs = [sc.sb([128, 512], F32, f"ex_a{i}") for i in range(2)]
        n = 0
        k = 0
        for (r0, rn) in [(0, 128), (128, 128), (256, 128), (384, 128), (512, 64)]:
            for t0 in range(0, TPRE, 512):
                acc, b_acc = accs[n % 2]
                n += 1
                for r in range(NCORES):
                    it, b_it = ins_[k % 4]
                    k += 1
                    fw.dma("sp", it[0:rn, :], P.agT[r * 576 + r0:r * 576 + r0 + rn, t0:t0 + 512], P.b_agT, b_it, "in")
                    if r == 0:
                        fw.op("dve", lambda: V.tensor_scalar(acc[0:rn, :], it[0:rn, :], sw[0:rn, 0:1], None, ALU.mult),
                              reads=[b_it, b_sw], writes=[b_acc])
                    else:
                        fw.op("dve", lambda: V.scalar_tensor_tensor(acc[0:rn, :], it[0:rn, :], sw[0:rn, r:r + 1], acc[0:rn, :], ALU.mult, ALU.add),
                              reads=[b_it, b_sw, b_acc], writes=[b_acc])
                fw.dma("pool", P.apT[r0:r0 + rn, t0:t0 + 512], acc[0:rn, :], b_acc, P.b_apT, "out")
    fw.put_sems([])
    fw.sem_pool.append(sem)


NEED_F = sorted(set(NEED_A + [k for k in NEED_B if k not in ("hT_in", "aT_in", "apT_in")]))


def build_fused():
    nc = bass.Bass("TRN2", target_bir_lowering=False)
    P = make_P(nc, NEED_F, ["out"])
    fw = FW(nc)
    cs = Consts(fw, P.ident_f, P.ident_b, P.ones_b)
    build_layer0(nc, P, fw, cs)
    kv_down(fw, cs, P)
    build_layer1(fw, cs, P, P.out, P.b_out)
    fw.drain([P.b_out])
    return nc


def kernel(**inputs):
    inputs = {k: np.asarray(v) for k, v in inputs.items()}
    maps = [host_inputs(inputs, c) for c in range(NCORES)]
    nc = build_fused()
    res = run_bass_kernel_spmd(nc, [{k: m[k] for k in NEED_F} for m in maps], core_ids=list(range(NCORES)))
    out = np.empty((4, 4096, D), np.float32)
    for c in range(NCORES):
        b, s = c // 2, c % 2
        out[b, s * 2048:(s + 1) * 2048] = np.asarray(res.results[c]["out"])
    return out
```

```python
import numpy as np
import concourse.bass as bass
import concourse.mybir as mybir
from concourse.bass_utils import run_bass_kernel_spmd

F32 = mybir.dt.float32
BF16 = mybir.dt.bfloat16
I32 = mybir.dt.int32
AF = mybir.ActivationFunctionType
ALU = mybir.AluOpType
AX = mybir.AxisListType

NCORES = 8
D = 4096
T = 2048
TPRE = 2064
NEUT = 2048
TT = [(i * 512, 512) for i in range(4)]
TT_P = [(0, 512), (512, 512), (1024, 512), (1536, 512), (2048, 16)]


class Sem:
    _k = 0

    def __init__(self, nc, name):
        Sem._k += 1
        self.h = nc.alloc_semaphore(f"{name}_{Sem._k}")
        self.n = 0


class Buf:
    _id = 0

    def __init__(self, name=None, multi=False):
        Buf._id += 1
        self.name = name or f"b{Buf._id}"
        self.w = {}
        self.r = {}
        self.sem_in = None
        self.sem_out = None
        self.multi = multi


class FW:
    ENG = ("pe", "act", "dve", "pool", "sp")

    def __init__(self, nc):
        self.nc = nc
        self.eng = {"pe": nc.tensor, "act": nc.scalar, "dve": nc.vector,
                    "pool": nc.gpsimd, "sp": nc.sync}
        self.prog = {e: Sem(nc, f"prog_{e}") for e in self.ENG}
        self.waited = {e: {} for e in self.ENG}
        self.ninst = 0
        self.sem_pool = []

    def get_sem(self):
        if self.sem_pool:
            return self.sem_pool.pop()
        return Sem(self.nc, "dq")

    def put_sems(self, bufs):
        for b in bufs:
            for a in ("sem_in", "sem_out"):
                sm = getattr(b, a)
                if sm is not None:
                    self.sem_pool.append(sm)
                    setattr(b, a, None)

    def _wait(self, e, sem, cnt):
        if cnt <= 0:
            return
        w = self.waited[e]
        if w.get(sem, 0) >= cnt:
            return
        if sem is self.prog[e] and cnt > sem.n:
            return
        self.eng[e].wait_ge(sem.h, cnt)
        w[sem] = cnt

    def _deps(self, e, reads, writes, skip=None):
        pe = self.prog["pe"]
        for b in reads:
            for s, c in b.w.items():
                if (e == "pe" and s is pe) or s is skip:
                    continue
                self._wait(e, s, c)
        for b in writes:
            for s, c in b.r.items():
                if (e == "pe" and s is pe) or s is skip:
                    continue
                self._wait(e, s, c)
            if not b.multi:
                for s, c in b.w.items():
                    if (e == "pe" and s is pe) or s is skip:
                        continue
                    self._wait(e, s, c)

    def _mark(self, tok, reads, writes):
        s, c = tok
        for b in reads:
            if b.r.get(s, 0) < c:
                b.r[s] = c
        for b in writes:
            if b.multi:
                if b.w.get(s, 0) < c:
                    b.w[s] = c
            else:
                b.w = {s: c}
                b.r = {}

    def op(self, e, fn, reads=(), writes=(), signal=True):
        self._deps(e, reads, writes)
        inst = fn()
        self.ninst += 1
        p = self.prog[e]
        if signal:
            p.n += 1
            inst.then_inc(p.h, 1)
            tok = (p, p.n)
        else:
            tok = (p, p.n + 1)
        self._mark(tok, reads, writes)
        self.last = inst
        return tok

    def no_ldweights(self):
        old = self.last.ins
        new = mybir.InstMatmult(
            name=old.name, opcode=old.opcode, engine=old.engine, debug=old.debug, ins=old.ins, outs=old.outs,
            sync_info=old.sync_info, start_tensor_calc=old.start_tensor_calc, stop_tensor_calc=old.stop_tensor_calc,
            is_transpose=old.is_transpose, tile_size=old.tile_size, tile_position=old.tile_position,
            perf_mode=old.perf_mode, bass_skip_group_check=old.bass_skip_group_check, ldweights=False)
        self.nc.register_instruction(new, overwrite=True)

    def dma(self, q, out_ap, in_ap, src, dst, side, **kw):
        if side == "in":
            if dst.sem_in is None:
                dst.sem_in = self.get_sem()
            sem = dst.sem_in
        else:
            if src.sem_out is None:
                src.sem_out = self.get_sem()
            sem = src.sem_out
        self._deps(q, [src], [dst], skip=sem)
        inst = self.eng[q].dma_start(out=out_ap, in_=in_ap, **kw)
        sem.n += 16
        inst.then_inc(sem.h, 16)
        self.ninst += 1
        s, c = sem, sem.n
        if src.r.get(s, 0) < c:
            src.r[s] = c
        if dst.multi:
            dst.w[s] = c
        else:
            dst.w = {s: c}
            dst.r = {}
        return (s, c)

    def drain(self, bufs):
        for b in bufs:
            for s, c in list(b.w.items()) + list(b.r.items()):
                self._wait("sp", s, c)
        for e in self.ENG:
            if e != "sp":
                self._wait("sp", self.prog[e], self.prog[e].n)
        self.nc.all_engine_barrier()


import os as _os
GEMM_KOUTER = bool(int(_os.environ.get('GEMM_KOUTER', '0')))
GEMM_ROWTILE = bool(int(_os.environ.get('GEMM_ROWTILE', '0')))
GEMM_NOLDW = bool(int(_os.environ.get('GEMM_NOLDW', '0')))


class Arena:
    def __init__(self, sc, Tn=T, kmax=32, nps=8):
        self.act, _ = sc.sb([128, kmax, Tn], BF16, "g_act")
        self.b_act = [sc.track(Buf(f"act{k}")) for k in range(kmax)]
        self.wb = [sc.sb([128, kmax, 256], BF16, f"g_wb{i}", multi=True) for i in range(2)]
        self.ws = [sc.sb([128, 8, 256], F32, f"g_ws{i}", multi=True) for i in range(2)]
        self.ps = [sc.ps(name=f"g_ps{i}") for i in range(nps)]
        self.nps = nps
        self.ips = 0
        self.iws = 0
        self.iwb = 0


def gemm(fw, ar, actT, b_actT, K, W, b_W, superblocks, epilogue, tts=TT, kpass=32, Tn=T):
    nc = fw.nc
    nkc = K // 128
    npass = -(-nkc // kpass)
    base, rem = nkc // npass, nkc % npass
    passes = []
    k0 = 0
    for p in range(npass):
        n = base + (1 if p < rem else 0)
        passes.append((k0, n))
        k0 += n
    for kp, (kc0, kc) in enumerate(passes):
        for k in range(kc):
            r0 = (kc0 + k) * 128
            fw.dma("sp", ar.act[:, k, 0:Tn], actT[r0:r0 + 128, 0:Tn], b_actT, ar.b_act[k], "in")
        for sbi, sb in enumerate(superblocks):
            offs = []
            o = 0
            for (c0, n) in sb:
                offs.append(o)
                o += n
            assert o <= 256
            segs = []
            for (c0, n), oo in zip(sb, offs):
                if segs and segs[-1][0] + segs[-1][1] == c0 and segs[-1][2] + segs[-1][1] == oo:
                    segs[-1][1] += n
                else:
                    segs.append([c0, n, oo])
            wb, b_wb = ar.wb[ar.iwb % 2]
            ar.iwb += 1
            for g0 in range(0, kc, 8):
                gn = min(8, kc - g0)
                ws, b_ws = ar.ws[ar.iws % 2]
                ar.iws += 1
                r0 = (kc0 + g0) * 128
                for (c0, n, oo) in segs:
                    src = W[r0:r0 + gn * 128, c0:c0 + n].rearrange("(c p) j -> p c j", p=128)
                    fw.dma("sp", ws[:, 0:gn, oo:oo + n], src, b_W, b_ws, "in")
                if ar.iws % 2 == 0:
                    fw.op("pool", lambda: nc.gpsimd.tensor_copy(wb[:, g0:g0 + gn, 0:o], ws[:, 0:gn, 0:o]),
                          reads=[b_ws], writes=[b_wb])
                else:
                    fw.op("act", lambda: nc.scalar.copy(wb[:, g0:g0 + gn, 0:o], ws[:, 0:gn, 0:o]),
                          reads=[b_ws], writes=[b_wb])
            if GEMM_KOUTER and len(tts) <= ar.nps:
                for (c0, ncol), oo in zip(sb, offs):
                    pss = []
                    for _ in tts:
                        pss.append(ar.ps[ar.ips % ar.nps])
                        ar.ips += 1
                    for k in range(kc):
                        for ti_, ((t0, tn), (ps, b_ps)) in enumerate(zip(tts, pss)):
                            fw.op("pe", lambda: nc.tensor.matmul(ps[0:ncol, 0:tn], wb[:, k, oo:oo + ncol],
                                                                 ar.act[:, k, t0:t0 + tn],
                                                                 start=(k == 0), stop=(k == kc - 1)),
                                  reads=[b_wb, ar.b_act[k]], writes=[b_ps], signal=(k == kc - 1))
                            if ti_ > 0 and GEMM_NOLDW:
                                fw.no_ldweights()
                    for (t0, tn), (ps, b_ps) in zip(tts, pss):
                        epilogue(sbi, t0, tn, [(c0, ncol, ps, b_ps)], kp, len(passes))
                continue
            if GEMM_ROWTILE:
                for (t0, tn) in tts:
                    blocks = []
                    for (c0, ncol), oo in zip(sb, offs):
                        psA, b_psA = ar.ps[ar.ips % ar.nps]
                        psB, b_psB = ar.ps[(ar.ips + 1) % ar.nps]
                        ar.ips += 2
                        for k in range(kc):
                            fw.op("pe", lambda: nc.tensor.matmul(psA[0:ncol, 0:tn], wb[0:64, k, oo:oo + ncol],
                                                                 ar.act[0:64, k, t0:t0 + tn],
                                                                 start=(k == 0), stop=(k == kc - 1)),
                                  reads=[b_wb, ar.b_act[k]], writes=[b_psA], signal=(k == kc - 1))
                            fw.op("pe", lambda: nc.tensor.matmul(psB[0:ncol, 0:tn], wb[64:128, k, oo:oo + ncol],
                                                                 ar.act[64:128, k, t0:t0 + tn],
                                                                 start=(k == 0), stop=(k == kc - 1)),
                                  reads=[b_wb, ar.b_act[k]], writes=[b_psB], signal=(k == kc - 1))
                        blocks.append((c0, ncol, psA, b_psA, psB, b_psB))
                    epilogue(sbi, t0, tn, blocks, kp, len(passes))
                continue
            for (t0, tn) in tts:
                blocks = []
                for (c0, ncol), oo in zip(sb, offs):
                    ps, b_ps = ar.ps[ar.ips % ar.nps]
                    ar.ips += 1
                    for k in range(kc):
                        fw.op("pe", lambda: nc.tensor.matmul(ps[0:ncol, 0:tn], wb[:, k, oo:oo + ncol],
                                                             ar.act[:, k, t0:t0 + tn],
                                                             start=(k == 0), stop=(k == kc - 1)),
                              reads=[b_wb, ar.b_act[k]], writes=[b_ps], signal=(k == kc - 1))
                    blocks.append((c0, ncol, ps, b_ps))
                epilogue(sbi, t0, tn, blocks, kp, len(passes))


def cb_range(c0, c1):
    out = []
    c = c0
    while c < c1:
        sb = []
        e = min(c + 256, c1)
        cc = c
        while cc < e:
            n = min(128, e - cc)
            sb.append((cc, n))
            cc += n
        out.append(sb)
        c = e
    return out


import contextlib


class Scope:
    def __init__(self, fw):
        self.fw = fw
        self.nc = fw.nc
        self.stack = contextlib.ExitStack()
        self.bufs = []

    def __enter__(self):
        self.stack.__enter__()
        return self

    def __exit__(self, *a):
        if a[0] is None:
            self.fw.drain(self.bufs)
            self.fw.put_sems(self.bufs)
        return self.stack.__exit__(*a)

    def sb(self, shape, dtype, name=None, multi=False):
        t = self.stack.enter_context(self.nc.sbuf_tensor(list(shape), dtype))
        b = Buf(name, multi=multi)
        self.bufs.append(b)
        return t, b

    def ps(self, shape=(128, 512), dtype=F32, name=None):
        t = self.stack.enter_context(self.nc.psum_tensor(list(shape), dtype))
        b = Buf(name)
        self.bufs.append(b)
        return t, b

    def track(self, b):
        self.bufs.append(b)
        return b


class Consts:
    def __init__(self, fw, ident_f32, ident_bf16, ones_bf16):
        nc = fw.nc
        self.b = Buf("consts", multi=True)
        self.ident = nc.alloc_sbuf_tensor("c_ident", [128, 128], F32)
        self.identb = nc.alloc_sbuf_tensor("c_identb", [128, 128], BF16)
        self.onesb = nc.alloc_sbuf_tensor("c_onesb", [128, 128], BF16)
        bd = Buf("cdram", multi=True)
        fw.dma("sp", self.ident[:, :], ident_f32, bd, self.b, "in")
        fw.dma("sp", self.identb[:, :], ident_bf16, bd, self.b, "in")
        fw.dma("sp", self.onesb[:, :], ones_bf16, bd, self.b, "in")


def tok_blocks(n, bs=128):
    out = []
    t = 0
    while t < n:
        out.append((t, min(bs, n - t)))
        t += bs
    return out


def transpose_in(fw, cs, x, b_x, hT, b_hT, Tn, Dn=D):
    nc = fw.nc
    nk = Dn // 128
    with Scope(fw) as sc:
        xin = [sc.sb([128, Dn], F32, f"ti_x{i}") for i in range(2)]
        ho = [sc.sb([128, nk, 512], F32, f"ti_h{i}") for i in range(2)]
        pss = [sc.ps(name=f"ti_ps{i}") for i in range(4)]
        ips = 0
        groups = tok_blocks(Tn, 512)
        for gi, (g0, gn) in enumerate(groups):
            h_t, h_b = ho[gi % 2]
            for bi, (t0, tn) in enumerate(tok_blocks(gn, 128)):
                x_t, x_b = xin[bi % 2]
                fw.dma("sp", x_t[0:tn, :], x[g0 + t0:g0 + t0 + tn, :], b_x, x_b, "in")
                for c4 in range(0, nk, 4):
                    ps_t, ps_b = pss[ips % 4]
                    ips += 1
                    for j in range(4):
                        c = c4 + j
                        fw.op("pe", lambda: nc.tensor.transpose(ps_t[:, j * 128:j * 128 + tn],
                                                                x_t[0:tn, c * 128:(c + 1) * 128],
                                                                cs.ident[0:tn, 0:tn]),
                              reads=[x_b, cs.b], writes=[ps_b], signal=(j == 3))
                    e = "dve" if (c4 // 4) % 2 == 0 else "act"
                    src = ps_t[:, :].rearrange("p (j t) -> p j t", j=4)[:, :, 0:tn]
                    dst = h_t[:, c4:c4 + 4, t0:t0 + tn]
                    if e == "dve":
                        fw.op("dve", lambda: nc.vector.tensor_copy(dst, src), reads=[ps_b], writes=[h_b])
                    else:
                        fw.op("act", lambda: nc.scalar.copy(dst, src), reads=[ps_b], writes=[h_b])
            fw.dma("pool", hT[:, g0:g0 + gn].rearrange("(c p) t -> p c t", p=128), h_t[:, :, 0:gn],
                   h_b, b_hT, "out")


def norm_fm(fw, cs, src, b_src, g_l, b_g, dst, b_dst, Tn, Kd, tw=256):
    nc = fw.nc
    nk = Kd // 128
    with Scope(fw) as sc:
        g_t, g_b = sc.sb([128, nk], F32, "nf_g")
        fw.dma("sp", g_t[:, :], g_l, b_g, g_b, "in")
        xs = [sc.sb([128, nk, tw], F32, f"nf_x{i}") for i in range(2)]
        sq = [sc.sb([128, nk, tw], BF16, f"nf_sq{i}") for i in range(2)]
        ys = [sc.sb([128, nk, tw], BF16, f"nf_y{i}") for i in range(2)]
        rs = [sc.sb([128, tw], F32, f"nf_r{i}") for i in range(2)]
        pss = [sc.ps(name=f"nf_ps{i}") for i in range(2)]
        for i, (t0, tn) in enumerate(tok_blocks(Tn, tw)):
            x_t, x_b = xs[i % 2]
            s_t, s_b = sq[i % 2]
            y_t, y_b = ys[i % 2]
            r_t, r_b = rs[i % 2]
            ps_t, ps_b = pss[i % 2]
            fw.dma("sp", x_t[:, :, 0:tn], src[:, t0:t0 + tn].rearrange("(c p) t -> p c t", p=128),
                   b_src, x_b, "in")
            fw.op("act", lambda: nc.scalar.activation(s_t[:, :, 0:tn], x_t[:, :, 0:tn], AF.Square),
                  reads=[x_b], writes=[s_b])
            for c in range(nk):
                fw.op("pe", lambda: nc.tensor.matmul(ps_t[:, 0:tn], cs.onesb[:, :], s_t[:, c, 0:tn],
                                                     start=(c == 0), stop=(c == nk - 1)),
                      reads=[s_b, cs.b], writes=[ps_b], signal=(c == nk - 1))
            rstd_from_sumsq(fw, ps_t[:, 0:tn], ps_b, r_t[:, 0:tn], r_b, 1.0 / Kd)
            for c in range(nk):
                fw.op("dve", lambda: nc.vector.scalar_tensor_tensor(
                    y_t[:, c, 0:tn], x_t[:, c, 0:tn], g_t[:, c:c + 1], r_t[:, 0:tn],
                    ALU.mult, ALU.mult), reads=[x_b, g_b, r_b], writes=[y_b], signal=(c == nk - 1))
            fw.dma("pool", dst[:, t0:t0 + tn].rearrange("(c p) t -> p c t", p=128), y_t[:, :, 0:tn],
                   y_b, b_dst, "out")


EPS = 1e-6


def rstd_from_sumsq(fw, ss_ap, ss_b, out_ap, out_b, inv_n):
    nc = fw.nc
    fw.op("act", lambda: nc.scalar.activation(out_ap, ss_ap, AF.Ln, bias=EPS, scale=inv_n),
          reads=[ss_b], writes=[out_b])
    fw.op("act", lambda: nc.scalar.activation(out_ap, out_ap, AF.Exp, scale=-0.5),
          reads=[out_b], writes=[out_b])


class Evac:
    def __init__(self, sc, n=4, dtype=BF16, w=512, name="ev"):
        self.bufs = [sc.sb([128, w], dtype, f"{name}{i}") for i in range(n)]
        self.i = 0

    def next(self):
        t = self.bufs[self.i % len(self.bufs)]
        self.i += 1
        return t


def evac_copy(fw, dst_ap, dst_b, src_ap, src_b, idx, func=None, scale=1.0):
    nc = fw.nc
    if func is not None or idx % 2 == 1:
        f = func if func is not None else AF.Copy
        fw.op("act", lambda: nc.scalar.activation(dst_ap, src_ap, f, scale=scale), reads=[src_b], writes=[dst_b])
    else:
        if scale == 1.0:
            fw.op("dve", lambda: nc.vector.tensor_copy(dst_ap, src_ap), reads=[src_b], writes=[dst_b])
        else:
            fw.op("dve", lambda: nc.vector.tensor_scalar(dst_ap, src_ap, scale, None, ALU.mult),
                  reads=[src_b], writes=[dst_b])


def store_epilogue(fw, sc, routes):
    evb = Evac(sc, 4, BF16, name="evb")
    evf = Evac(sc, 2, F32, name="evf")
    cnt = [0]

    def epi(sbi, t0, tn, blocks, kp, nkp):
        for (c0, ncol, ps, b_ps) in blocks:
            for (lo, hi, dst, roff, toff, func, scale, dt, b_dst) in routes:
                if lo <= c0 < hi:
                    break
            else:
                raise AssertionError(c0)
            ev, b_ev = (evb if dt == BF16 else evf).next()
            evac_copy(fw, ev[0:ncol, 0:tn], b_ev, ps[0:ncol, 0:tn], b_ps, cnt[0], func, scale)
            cnt[0] += 1
            r = roff + c0 - lo
            fw.dma("pool", dst[r:r + ncol, toff + t0:toff + t0 + tn], ev[0:ncol, 0:tn], b_ev, b_dst, "out")
    return epi


def resid_epilogue(fw, sc, hT, b_hT):
    nc = fw.nc
    hb = Evac(sc, 3, F32, name="rh")

    def epi(sbi, t0, tn, blocks, kp, nkp):
        for (c0, ncol, ps, b_ps) in blocks:
            h, b_h = hb.next()
            fw.dma("pool", h[0:ncol, 0:tn], hT[c0:c0 + ncol, t0:t0 + tn], b_hT, b_h, "in")
            fw.op("dve", lambda: nc.vector.tensor_tensor(h[0:ncol, 0:tn], ps[0:ncol, 0:tn], h[0:ncol, 0:tn], ALU.add),
                  reads=[b_ps, b_h], writes=[b_h])
            fw.dma("pool", hT[c0:c0 + ncol, t0:t0 + tn], h[0:ncol, 0:tn], b_h, b_hT, "out")
    return epi


def swiglu_epilogue(fw, sc, hidT, b_hid, dff):
    nc = fw.nc
    sg = Evac(sc, 2, F32, name="sg")
    hb = Evac(sc, 3, BF16, name="hb")

    def epi(sbi, t0, tn, blocks, kp, nkp):
        (cg, ncol, psg, b_psg), (cu, ncol2, psu, b_psu) = blocks
        assert cu == cg + dff and ncol == ncol2
        s, b_s = sg.next()
        h, b_h = hb.next()
        fw.op("act", lambda: nc.scalar.activation(s[0:ncol, 0:tn], psg[0:ncol, 0:tn], AF.Silu),
              reads=[b_psg], writes=[b_s])
        fw.op("dve", lambda: nc.vector.tensor_tensor(h[0:ncol, 0:tn], s[0:ncol, 0:tn], psu[0:ncol, 0:tn], ALU.mult),
              reads=[b_s, b_psu], writes=[b_h])
        fw.dma("pool", hidT[cg:cg + ncol, t0:t0 + tn], h[0:ncol, 0:tn], b_h, b_hid, "out")
    return epi


DFF = 11008


class Seg:
    def __init__(self, hT, b_hT, xnT, b_xnT, Tn, tts, off):
        self.hT, self.b_hT, self.xnT, self.b_xnT, self.Tn, self.tts, self.off = hT, b_hT, xnT, b_xnT, Tn, tts, off


def seg_L(P):
    return Seg(P.hT, P.b_hT, P.xnT, P.b_xnT, T, TT, TPRE)


def seg_P(P):
    return Seg(P.hpT, P.b_hpT, P.xnpT, P.b_xnpT, TPRE, TT_P, 0)


def ffn(fw, cs, P, layer, sg=None):
    sg = sg or seg_L(P)
    norm_fm(fw, cs, sg.hT, sg.b_hT, P.ffn_g[layer], P.b_in, sg.xnT, sg.b_xnT, sg.Tn, D)
    hid = P.hidT[:, 0:sg.Tn]
    with Scope(fw) as sc:
        ar = Arena(sc, Tn=sg.Tn)
        sbs = [[(j * 128, 128), (DFF + j * 128, 128)] for j in range(DFF // 128)]
        gemm(fw, ar, sg.xnT, sg.b_xnT, D, P.w_gu[layer], P.b_in, sbs,
             swiglu_epilogue(fw, sc, hid, P.b_hidT, DFF), tts=sg.tts, Tn=sg.Tn)
    with Scope(fw) as sc:
        ar = Arena(sc, Tn=sg.Tn, kmax=29)
        gemm(fw, ar, hid, P.b_hidT, DFF, P.w_dn[layer], P.b_in, cb_range(0, D),
             resid_epilogue(fw, sc, sg.hT, sg.b_hT), kpass=29, tts=sg.tts, Tn=sg.Tn)


TALL = TPRE + T
MH = 8
MDK = 256
MDV = 512
CHUNKS = [(128 * c, 128, True) for c in range(16)] + [(NEUT, 16, True)] + \
         [(NEUT + 16 + 128 * c, 128, True) for c in range(16)]
NCH = len(CHUNKS)


def l0_proj(fw, cs, P):
    for (xn, b_xn, Tn, tts, off) in ((P.xnT, P.b_xnT, T, TT, TPRE), (P.xnpT, P.b_xnpT, TPRE, TT_P, 0)):
        with Scope(fw) as sc:
            ar = Arena(sc, Tn=Tn)
            routes = [
                (0, 2048, P.qT, 0, off, None, 1.0, BF16, P.b_qT),
                (2048, 4096, P.kT, 0, off, None, 1.0 / 16.0, BF16, P.b_kT),
                (4096, 8192, P.vT, 0, off, None, 1.0, BF16, P.b_vT),
                (8192, 12288, P.ogT, 0, off, AF.Sigmoid, 1.0, BF16, P.b_ogT),
                (12288, 12296, P.graw, 0, off, None, 1.0, F32, P.b_graw),
                (12296, 12304, P.graw, 8, off, None, 1.0, F32, P.b_graw),
            ]
            gemm(fw, ar, xn, b_xn, D, P.w_in, P.b_in, cb_range(0, 12288) + [[(12288, 8), (12296, 8)]],
                 store_epilogue(fw, sc, routes), tts=tts, Tn=Tn)


def mlstm_gates(fw, cs, P, efc, b_efc, lam, b_lam):
    nc = fw.nc
    with Scope(fw) as sc:
        A1, b1 = sc.sb([8, TALL], F32, "mg1")
        A2, b2 = sc.sb([8, TALL], F32, "mg2")
        A3, b3 = sc.sb([8, TALL], F32, "mg3")
        A4, b4 = sc.sb([8, TALL + 1], F32, "mg4")
        sm, bsm = sc.sb([8, 8], F32, "mgs")
        sel, bsel = sc.sb([8, 8, 128], F32, "mgsel")
        zb, bzb = sc.sb([128, 8, 34], F32, "mgzb")
        ps1, bp1 = sc.ps(name="mgp1")
        ps2, bp2 = sc.ps(name="mgp2")
        fw.dma("sp", A1[:, :], P.graw[0:8, :], P.b_graw, b1, "in")
        fw.dma("sp", A2[:, :], P.graw[8:16, :], P.b_graw, b2, "in")
        fw.dma("sp", sm[:, 0:3], P.mparams, P.b_in, bsm, "in")
        fw.dma("sp", sel[:, :, :], P.sel8, P.b_in, bsel, "in")
        V = nc.vector
        fw.op("dve", lambda: V.tensor_scalar(sm[:, 3:5], sm[:, 0:2], 1.0 / 15.0, None, ALU.mult), reads=[bsm], writes=[bsm])
        fw.op("dve", lambda: V.tensor_scalar(sm[:, 5:6], sm[:, 2:3], -1.0, 30000.0, ALU.add, ALU.mult), reads=[bsm], writes=[bsm])
        fw.op("act", lambda: nc.scalar.activation(A1[:, :], A1[:, :], AF.Tanh, bias=sm[:, 3:4], scale=1.0 / 15.0),
              reads=[b1, bsm], writes=[b1])
        fw.op("dve", lambda: V.tensor_scalar(A1[:, :], A1[:, :], 15.0, None, ALU.mult), reads=[b1], writes=[b1])
        fw.op("act", lambda: nc.scalar.activation(A2[:, :], A2[:, :], AF.Tanh, bias=sm[:, 4:5], scale=1.0 / 15.0),
              reads=[b2, bsm], writes=[b2])
        fw.op("act", lambda: nc.scalar.activation(A2[:, :], A2[:, :], AF.Exp, scale=-15.0), reads=[b2], writes=[b2])
        fw.op("act", lambda: nc.scalar.activation(A2[:, :], A2[:, :], AF.Ln, bias=1.0, scale=1.0), reads=[b2], writes=[b2])
        fw.op("dve", lambda: V.tensor_scalar(A2[:, :], A2[:, :], -1.0, None, ALU.mult), reads=[b2], writes=[b2])
        fw.op("dve", lambda: V.tensor_scalar(A2[:, 0:NEUT], A2[:, 0:NEUT], sm[:, 2:3], None, ALU.mult),
              reads=[b2, bsm], writes=[b2])
        fw.op("dve", lambda: V.tensor_scalar(A1[:, 0:NEUT], A1[:, 0:NEUT], sm[:, 2:3], sm[:, 5:6], ALU.mult, ALU.add),
              reads=[b1, bsm], writes=[b1])
        fw.op("pool", lambda: nc.gpsimd.memset(A4[:, :], 0.0), writes=[b4])
        fw.op("dve", lambda: V.tensor_tensor_scan(A3[:, :], A2[:, :], A4[:, 0:TALL], 0.0, ALU.add, ALU.add),
              reads=[b2, b4], writes=[b3])
        fw.op("dve", lambda: V.tensor_tensor(A1[:, :], A1[:, :], A3[:, :], ALU.subtract), reads=[b1, b3], writes=[b1])
        fw.op("dve", lambda: V.tensor_tensor_scan(A4[:, 1:TALL + 1], A1[:, :], A1[:, :], 0.0, ALU.max, ALU.max),
              reads=[b1], writes=[b4])
        fw.op("dve", lambda: V.tensor_scalar(A4[:, 1:TALL + 1], A4[:, 1:TALL + 1], -1.0, None, ALU.mult),
              reads=[b4], writes=[b4])
        for c, (t0, L, _) in enumerate(CHUNKS):
            fw.op("act", lambda: nc.scalar.activation(A1[:, t0:t0 + L], A1[:, t0:t0 + L], AF.Exp,
                                                      bias=A4[:, t0:t0 + 1], scale=1.0),
                  reads=[b1, b4], writes=[b1], signal=False)
            fw.op("act", lambda: nc.scalar.activation(A3[:, t0:t0 + L], A3[:, t0:t0 + L], AF.Exp,
                                                      bias=A4[:, t0:t0 + 1], scale=-1.0),
                  reads=[b3, b4], writes=[b3], signal=(c == NCH - 1))
        for c, (t0, L, _) in enumerate(CHUNKS):
            pst, bpt = (ps1, bp1) if c % 2 == 0 else (ps2, bp2)
            fw.op("pe", lambda: nc.tensor.transpose(pst[0:L, 0:8], A1[:, t0:t0 + L], cs.ident[0:8, 0:8]),
                  reads=[b1, cs.b], writes=[bpt], signal=False)
            fw.op("pe", lambda: nc.tensor.transpose(pst[0:L, 8:16], A3[:, t0:t0 + L], cs.ident[0:8, 0:8]),
                  reads=[b3, cs.b], writes=[bpt])
            fw.op("dve", lambda: V.tensor_copy(efc[0:L, c, :], pst[0:L, 0:16]), reads=[bpt], writes=[b_efc])
        for h in range(8):
            pst, bpt = (ps1, bp1) if h % 2 == 0 else (ps2, bp2)
            fw.op("pe", lambda: nc.tensor.matmul(pst[:, 0:17], sel[:, h, :], A4[:, 0:NEUT + 1:128], start=True, stop=True),
                  reads=[bsel, b4], writes=[bpt], signal=False)
            fw.op("pe", lambda: nc.tensor.matmul(pst[:, 17:34], sel[:, h, :], A4[:, NEUT + 16:TALL + 1:128], start=True, stop=True),
                  reads=[bsel, b4], writes=[bpt])
            fw.op("dve", lambda: V.tensor_copy(zb[:, h, :], pst[:, 0:34]), reads=[bpt], writes=[bzb])
        fw.op("dve", lambda: V.tensor_tensor(lam[:, :, :], zb[:, :, 1:34], zb[:, :, 0:33], ALU.subtract),
              reads=[bzb], writes=[b_lam])
        fw.op("act", lambda: nc.scalar.activation(lam[:, :, :], lam[:, :, :], AF.Exp), reads=[b_lam], writes=[b_lam])


def mlstm(fw, cs, P):
    import os
    nc = fw.nc
    V = nc.vector
    A = nc.scalar
    G = nc.gpsimd
    efc = nc.alloc_sbuf_tensor("m_efc", [128, NCH, 16], F32)
    lam = nc.alloc_sbuf_tensor("m_lam", [128, 8, NCH], F32)
    b_efc, b_lam = Buf("efc"), Buf("lam")
    mlstm_gates(fw, cs, P, efc, b_efc, lam, b_lam)
    if getattr(P, "debug_gates", False):
        fw.dma("sp", P.dbg_efc, efc[:, :, :].rearrange("p a b -> p (a b)"), b_efc, P.b_dbg_efc, "out")
        fw.dma("sp", P.dbg_lam, lam[:, :, :].rearrange("p a b -> p (a b)"), b_lam, P.b_dbg_lam, "out")
        return
    groups = [(256 * g, 256, [2 * g, 2 * g + 1]) for g in range(8)] + [(NEUT, 16, [16])] + \
             [(NEUT + 16 + 256 * g, 256, [17 + 2 * g, 18 + 2 * g]) for g in range(8)]
    with Scope(fw) as sc:
        kTg = [sc.sb([128, 16, 256], BF16, f"m_k{i}") for i in range(2)]
        vTg = [sc.sb([128, 32, 256], BF16, f"m_v{i}") for i in range(2)]
        qTg = [sc.sb([128, 16, 256], BF16, f"m_q{i}") for i in range(2)]
        ogg = [sc.sb([128, 32, 256], BF16, f"m_o{i}", multi=True) for i in range(2)]
        Cst = [sc.sb([128, 2, 513], F32, f"m_C{h}") for h in range(8)]
        Cbf = [sc.sb([128, 2, 513], BF16, f"m_Cb{h}") for h in range(8)]
        Clt = [sc.sb([128, 513], F32, f"m_Cl{i}") for i in range(2)]
        NR = 4
        kE = [sc.sb([128, 256], BF16, f"m_kE{i}") for i in range(NR)]
        vx = [sc.sb([128, 513], BF16, f"m_vx{i}", multi=True) for i in range(NR)]
        Sp = [sc.sb([128, 128], BF16, f"m_Sp{i}") for i in range(NR)]
        junk = [sc.sb([128, 512], BF16, f"m_jk{i}") for i in range(2)]
        hn = [sc.sb([128, 512], BF16, f"m_hn{i}") for i in range(NR)]
        sml = [sc.sb([128, 8], F32, f"m_sm{i}") for i in range(NR)]
        recl = [sc.sb([128, 1], F32, f"m_rec{i}") for i in range(NR)]
        scal = [sc.sb([128, 1], F32, f"m_scl{i}") for i in range(NR)]
        mask, b_mask = sc.sb([128, 128], BF16, "m_mask")
        gout, b_gout = sc.sb([128, 32], F32, "m_gout")
        B0, b0 = sc.ps([128, 1024], BF16, "m_B0")
        B1, bb1 = sc.ps(name="m_B1")
        B2, bb2 = sc.ps(name="m_B2")
        B3, bb3 = sc.ps(name="m_B3")
        B4, bb4 = sc.ps([128, 1024], BF16, "m_B4")
        B5, bb5 = sc.ps(name="m_B5")
        B6, bb6 = sc.ps(name="m_B6")
        B7, bb7 = sc.ps(name="m_B7")
        fw.dma("sp", mask[:, :], P.maskT, P.b_in, b_mask, "in")
        fw.dma("sp", gout[:, :], P.gout_l, P.b_in, b_gout, "in")
        for h in range(8):
            fw.op("pool", lambda: G.memset(Cst[h][0][:, :, :], 0.0), writes=[Cst[h][1]])
            fw.op("pool", lambda: G.memset(Cbf[h][0][:, :, :], 0.0), writes=[Cbf[h][1]])
        for i in range(NR):
            fw.op("pool", lambda: G.memset(vx[i][0][:, 512:513], 1.0), writes=[vx[i][1]])

        items = []
        for gi, (g0, gn, chs) in enumerate(groups):
            for c in chs:
                for h in range(8):
                    items.append((gi, c, h))
        loaded = set()

        def load_group(gi):
            if gi in loaded or gi >= len(groups):
                return
            loaded.add(gi)
            g0, gn, chs = groups[gi]
            main = CHUNKS[chs[0]][2]
            kt, bk = kTg[gi % 2]
            vt, bv = vTg[gi % 2]
            fw.dma("sp", kt[:, :, 0:gn], P.kT[:, g0:g0 + gn].rearrange("(c p) t -> p c t", p=128), P.b_kT, bk, "in")
            fw.dma("sp", vt[:, :, 0:gn], P.vT[:, g0:g0 + gn].rearrange("(c p) t -> p c t", p=128), P.b_vT, bv, "in")
            if main:
                qt, bq = qTg[gi % 2]
                ot, bo = ogg[gi % 2]
                l0 = g0
                fw.dma("sp", qt[:, :, 0:gn], P.qT[:, l0:l0 + gn].rearrange("(c p) t -> p c t", p=128), P.b_qT, bq, "in")
                fw.dma("sp", ot[:, :, 0:gn], P.ogT[:, l0:l0 + gn].rearrange("(c p) t -> p c t", p=128), P.b_ogT, bo, "in")

        def ctx(idx):
            gi, c, h = items[idx]
            g0, gn, chs = groups[gi]
            t0, L, main = CHUNKS[c]
            return gi, c, h, t0 - g0, L, main

        def stage1(idx):
            gi, c, h, o, L, main = ctx(idx)
            r = idx % NR
            kt, bk = kTg[gi % 2]
            vt, bv = vTg[gi % 2]
            parts = os.environ.get("S1_PARTS", "12")
            if "1" in parts:
                for j in range(2):
                    fw.op("pe", lambda: nc.tensor.transpose(B0[0:L, j * 128:(j + 1) * 128], kt[:, 2 * h + j, o:o + L], cs.identb[:, :]),
                          reads=[bk, cs.b], writes=[b0], signal=(j == 1))
                fw.op("dve", lambda: V.tensor_scalar(kE[r][0][0:L, :], B0[0:L, 0:256], efc[0:L, c, h:h + 1], None, ALU.mult),
                      reads=[b0, b_efc], writes=[kE[r][1]])
            if "2" in parts:
                for j in range(4):
                    fw.op("pe", lambda: nc.tensor.transpose(B0[0:L, 256 + j * 128:256 + (j + 1) * 128], vt[:, 4 * h + j, o:o + L], cs.identb[:, :]),
                          reads=[bv, cs.b], writes=[b0], signal=(j == 3))
                fw.op("act", lambda: A.copy(vx[r][0][0:L, 0:512], B0[0:L, 256:768]), reads=[b0], writes=[vx[r][1]])
            if main:
                qt, bq = qTg[gi % 2]
                for j in range(2):
                    fw.op("pe", lambda: nc.tensor.matmul(B1[0:L, 0:L], kt[:, 2 * h + j, o:o + L], qt[:, 2 * h + j, o:o + L],
                                                         start=(j == 0), stop=(j == 1)),
                          reads=[bk, bq], writes=[bb1], signal=(j == 1))
                fw.op("dve", lambda: V.scalar_tensor_tensor(Sp[r][0][0:L, 0:L], B1[0:L, 0:L], efc[0:L, c, h:h + 1],
                                                            mask[0:L, 0:L], ALU.mult, ALU.mult),
                      reads=[bb1, b_efc, b_mask], writes=[Sp[r][1]])

        def stage2(idx):
            gi, c, h, o, L, main = ctx(idx)
            r = idx % NR
            Ct, bC = Cst[h]
            Cb, bCb = Cbf[h]
            if main:
                qt, bq = qTg[gi % 2]
                sm_t, b_sm = sml[r]
                for j in range(2):
                    fw.op("pe", lambda: nc.tensor.matmul(B2[0:L, 0:512], qt[:, 2 * h + j, o:o + L], Cb[:, j, 0:512],
                                                         start=(j == 0), stop=False),
                          reads=[bq, bCb], writes=[bb2], signal=False)
                fw.op("pe", lambda: nc.tensor.matmul(B2[0:L, 0:512], Sp[r][0][0:L, 0:L], vx[r][0][0:L, 0:512], start=False, stop=True),
                      reads=[Sp[r][1], vx[r][1]], writes=[bb2])
                for j in range(2):
                    fw.op("pe", lambda: nc.tensor.matmul(B3[0:L, 0:1], qt[:, 2 * h + j, o:o + L], Cb[:, j, 512:513],
                                                         start=(j == 0), stop=False),
                          reads=[bq, bCb], writes=[bb3], signal=False)
                fw.op("pe", lambda: nc.tensor.matmul(B3[0:L, 0:1], Sp[r][0][0:L, 0:L], vx[r][0][0:L, 512:513], start=False, stop=True),
                      reads=[Sp[r][1], vx[r][1]], writes=[bb3])
                fw.op("act", lambda: A.activation(sm_t[0:L, 0:1], B3[0:L, 0:1], AF.Abs), reads=[bb3], writes=[b_sm])
                fw.op("dve", lambda: V.tensor_tensor(sm_t[0:L, 0:1], sm_t[0:L, 0:1], efc[0:L, c, 8 + h:9 + h], ALU.max),
                      reads=[b_sm, b_efc], writes=[b_sm])
                rc_t, b_rc = recl[r]
                sl_t, b_sl = scal[r]
                fw.op("dve", lambda: V.reciprocal(rc_t[0:L, 0:1], sm_t[0:L, 0:1]), reads=[b_sm], writes=[b_rc])
                jk, b_jk = junk[idx % 2]
                fw.op("act", lambda: A.activation(jk[0:L, :], B2[0:L, 0:512], AF.Square, scale=rc_t[0:L, 0:1], accum_out=sm_t[0:L, 2:3]),
                      reads=[bb2, b_rc], writes=[b_jk, b_sm])
                fw.op("act", lambda: A.activation(sm_t[0:L, 3:4], sm_t[0:L, 2:3], AF.Ln, bias=EPS, scale=1.0 / MDV),
                      reads=[b_sm], writes=[b_sm])
                fw.op("act", lambda: A.activation(sm_t[0:L, 3:4], sm_t[0:L, 3:4], AF.Exp, scale=-0.5), reads=[b_sm], writes=[b_sm])
                fw.op("dve", lambda: V.tensor_tensor(sl_t[0:L, 0:1], sm_t[0:L, 3:4], rc_t[0:L, 0:1], ALU.mult), reads=[b_sm, b_rc], writes=[b_sl])
                fw.op("act", lambda: A.activation(hn[r][0][0:L, :], B2[0:L, 0:512], AF.Copy, scale=sl_t[0:L, 0:1]),
                      reads=[bb2, b_sl], writes=[hn[r][1]])
            lsc = lam[:, h, c:c + 1]
            for j, (Bj, bbj) in enumerate(((B5, bb5), (B6, bb6))):
                fw.op("pe", lambda: nc.tensor.matmul(Bj[:, 0:512], kE[r][0][0:L, j * 128:(j + 1) * 128], vx[r][0][0:L, 0:512], start=True, stop=True),
                      reads=[kE[r][1], vx[r][1]], writes=[bbj])
                fw.op("pe", lambda: nc.tensor.matmul(B7[:, j:j + 1], kE[r][0][0:L, j * 128:(j + 1) * 128], vx[r][0][0:L, 512:513], start=True, stop=True),
                      reads=[kE[r][1], vx[r][1]], writes=[bb7])
                cl, b_cl = Clt[j]
                fw.op("pool", lambda: G.tensor_scalar(cl[:, :], Ct[:, j, :], lsc, 0.0, ALU.mult, ALU.add), reads=[bC, b_lam], writes=[b_cl])
                fw.op("dve", lambda: V.scalar_tensor_tensor(Ct[:, j, 0:512], Bj[:, 0:512], lsc, cl[:, 0:512], ALU.mult, ALU.add),
                      reads=[bbj, b_lam, b_cl], writes=[bC])
                fw.op("dve", lambda: V.scalar_tensor_tensor(Ct[:, j, 512:513], B7[:, j:j + 1], lsc, cl[:, 512:513], ALU.mult, ALU.add),
                      reads=[bb7, b_lam, b_cl], writes=[bC])
                fw.op("act", lambda: A.copy(Cb[:, j, :], Ct[:, j, :]), reads=[bC], writes=[bCb])

        def stage3(idx):
            gi, c, h, o, L, main = ctx(idx)
            if not main:
                return
            r = idx % NR
            ot, bo = ogg[gi % 2]
            for j in range(4):
                fw.op("pe", lambda: nc.tensor.transpose(B4[:, j * 128:j * 128 + L], hn[r][0][0:L, j * 128:(j + 1) * 128], cs.identb[0:L, 0:L]),
                      reads=[hn[r][1], cs.b], writes=[bb4], signal=(j == 3))
            for j in range(4):
                fw.op("dve", lambda: V.scalar_tensor_tensor(ot[:, 4 * h + j, o:o + L], B4[:, j * 128:j * 128 + L], gout[:, 4 * h + j:4 * h + j + 1],
                                                            ot[:, 4 * h + j, o:o + L], ALU.mult, ALU.mult),
                      reads=[bb4, b_gout, bo], writes=[bo], signal=(j == 3))
            g0, gn, chs = groups[gi]
            if h == 7 and c == chs[-1]:
                l0 = g0
                fw.dma("pool", P.yT[:, l0:l0 + gn].rearrange("(c p) t -> p c t", p=128), ot[:, :, 0:gn], bo, P.b_yT, "out")

        load_group(0)
        load_group(1)
        import os
        n = len(items)
        if os.environ.get("MLSTM_ITEMS"):
            n = int(os.environ["MLSTM_ITEMS"])
        for it in range(n + 2):
            if it < n:
                stage1(it)
            if 0 <= it - 1 < n and not os.environ.get("MLSTM_SKIP2"):
                stage2(it - 1)
            if 0 <= it - 2 < n:
                stage3(it - 2)
                gi, c, h = items[it - 2]
                if h == 7 and c == groups[gi][2][-1]:
                    load_group(gi + 2)


def w_out_phase(fw, cs, P, sg=None):
    sg = sg or seg_L(P)
    with Scope(fw) as sc:
        ar = Arena(sc, Tn=sg.Tn)
        gemm(fw, ar, P.yT[:, sg.off:sg.off + sg.Tn], P.b_yT, D, P.w_out, P.b_in, cb_range(0, D),
             resid_epilogue(fw, sc, sg.hT, sg.b_hT), tts=sg.tts, Tn=sg.Tn)


class Params:
    pass


def declare(nc, name, shape, dtype, kind):
    return nc.dram_tensor(name, list(shape), dtype, kind=kind).ap()


def build_layer0(nc, P, fw, cs):
    transpose_in(fw, cs, P.xloc, P.b_in, P.hT, P.b_hT, T)
    transpose_in(fw, cs, P.xpre, P.b_in, P.hpT, P.b_hpT, TPRE)
    norm_fm(fw, cs, P.hT, P.b_hT, P.a_norm_g, P.b_in, P.xnT, P.b_xnT, T, D)
    norm_fm(fw, cs, P.hpT, P.b_hpT, P.a_norm_g, P.b_in, P.xnpT, P.b_xnpT, TPRE, D)
    l0_proj(fw, cs, P)
    mlstm(fw, cs, P)
    for sg in (seg_L(P), seg_P(P)):
        w_out_phase(fw, cs, P, sg)
        ffn(fw, cs, P, 0, sg)


INPUTS = {
    "xloc": ([T, D], F32), "xpre": ([TPRE, D], F32),
    "w_in": ([D, 12304], F32), "w_out": ([D, D], F32),
    "w_gu0": ([D, 2 * DFF], F32), "w_gu1": ([D, 2 * DFF], F32),
    "w_dn0": ([DFF, D], F32), "w_dn1": ([DFF, D], F32),
    "kv_w_down": ([D, 576], F32), "kv_w_up": ([512, 16384], F32),
    "w_dq": ([D, 1024], F32), "w_uq": ([1024, 12288], F32), "w_o": ([8192, D], F32),
    "a_norm_g": ([128, 32], F32), "ffn_g0": ([128, 32], F32), "ffn_g1": ([128, 32], F32),
    "kv_norm_g": ([128, 32], F32), "b_norm_g": ([128, 32], F32),
    "kv_lat_g": ([128, 4], F32), "q_lat_g": ([128, 8], F32), "gout_l": ([128, 32], F32),
    "kq_g": ([128, 8], F32),
    "mparams": ([8, 3], F32), "sel8": ([8, 8, 128], F32), "maskT": ([128, 128], BF16),
    "ident_f": ([128, 128], F32), "ident_b": ([128, 128], BF16), "ones_b": ([128, 128], BF16),
    "posf": ([1, TALL], I32), "invf": ([32, 1], F32), "aflag": ([128, 2], F32),
}
SCRATCH = {
    "hT": ([D, T], F32), "hpT": ([D, TPRE], F32), "xnT": ([D, T], BF16), "xnpT": ([D, TPRE], BF16),
    "qT": ([2048, TALL], BF16), "kT": ([2048, TALL], BF16), "vT": ([4096, TALL], BF16), "ogT": ([D, TALL], BF16),
    "graw": ([16, TALL], F32), "yT": ([D, TALL], BF16), "hidT": ([DFF, TPRE], BF16),
}


def make_P(nc, need, outputs=()):
    P = Params()
    P.b_in = Buf("inputs", multi=True)
    P.ffn_g, P.w_gu, P.w_dn = {}, {}, {}
    for name in need:
        shape, dt = INPUTS[name]
        ap = declare(nc, name, shape, dt, "ExternalInput")
        setattr(P, name, ap)
    for l in (0, 1):
        if f"ffn_g{l}" in need:
            P.ffn_g[l] = getattr(P, f"ffn_g{l}")
            P.w_gu[l] = getattr(P, f"w_gu{l}")
            P.w_dn[l] = getattr(P, f"w_dn{l}")
    for name, (shape, dt) in SCRATCH.items():
        kind = "ExternalOutput" if name in outputs else "Internal"
        setattr(P, name, declare(nc, name, shape, dt, kind))
        setattr(P, "b_" + name, Buf(name, multi=True))
    return P


def host_consts():
    import ml_dtypes
    bf = ml_dtypes.bfloat16
    sel8 = np.zeros((8, 8, 128), np.float32)
    for h in range(8):
        sel8[h, h, :] = 1.0
    s = np.arange(128)
    maskT = (s[:, None] <= s[None, :]).astype(np.float32).astype(bf)
    inv = (1.0 / (10000.0 ** (np.arange(0, 64, 2, dtype=np.float32) / 64.0))).astype(np.float32)
    return {
        "sel8": sel8, "maskT": maskT, "ident_f": np.eye(128, dtype=np.float32),
        "ident_b": np.eye(128, dtype=np.float32).astype(bf), "ones_b": np.ones((128, 128), np.float32).astype(bf),
        "invf": inv.reshape(32, 1),
    }


def lay(g, nk):
    return np.ascontiguousarray(np.asarray(g, np.float32).reshape(nk, 128).T)


def host_inputs(inputs, core):
    b, s = core // 2, core % 2
    x = inputs["x"][b]
    meta = inputs["meta_tokens"]
    if s == 0:
        xloc = x[0:2048]
        xpre = np.concatenate([np.zeros((NEUT, D), np.float32), meta], axis=0)
    else:
        xloc = x[2048:4096]
        xpre = np.concatenate([meta, x[0:2048]], axis=0)
    m = {
        "xloc": np.ascontiguousarray(xloc, dtype=np.float32), "xpre": np.ascontiguousarray(xpre, dtype=np.float32),
        "w_in": inputs["a_w_in"][0], "w_out": inputs["a_w_out"][0],
        "w_gu0": inputs["ffn_w_gate_up"][0], "w_gu1": inputs["ffn_w_gate_up"][1],
        "w_dn0": inputs["ffn_w_down"][0], "w_dn1": inputs["ffn_w_down"][1],
        "kv_w_down": inputs["kv_w_down"], "kv_w_up": inputs["kv_w_up"],
        "w_dq": inputs["b_w_dq"][0], "w_uq": inputs["b_w_uq"][0], "w_o": inputs["b_w_o"][0],
        "a_norm_g": lay(inputs["a_norm_g"][0], 32), "ffn_g0": lay(inputs["ffn_norm_g"][0], 32),
        "ffn_g1": lay(inputs["ffn_norm_g"][1], 32), "kv_norm_g": lay(inputs["kv_norm_g"], 32),
        "b_norm_g": lay(inputs["b_norm_g"][0], 32), "kv_lat_g": lay(inputs["kv_latent_norm_g"], 4),
        "q_lat_g": lay(inputs["b_q_latent_norm_g"][0], 8), "gout_l": lay(inputs["a_out_norm_g"][0], 32),
        "mparams": np.stack([np.asarray(inputs["a_b_i"][0], np.float32), np.asarray(inputs["a_b_f"][0], np.float32),
                             np.full(8, float(s), np.float32)], axis=1),
    }
    kq = np.zeros((128, 8), np.float32)
    gk = np.asarray(inputs["k_norm_g"], np.float32)
    gq = np.asarray(inputs["q_norm_g"][0], np.float32)
    kq[:, 0] = gk[0:128]
    kq[0:32, 1] = gk[128:160]
    kq[0:32, 2] = gk[160:192]
    kq[:, 3] = gq[0:128]
    kq[0:32, 4] = gq[128:160]
    kq[0:32, 5] = gq[160:192]
    m["kq_g"] = kq
    pos = np.asarray(inputs["positions"][b], np.int32)
    metap = np.arange(16, dtype=np.int32) - 16
    if s == 0:
        posf = np.concatenate([np.zeros(NEUT, np.int32), metap, pos[0:2048]])
    else:
        posf = np.concatenate([metap, pos[0:2032], pos[2032:4096]])
    m["posf"] = posf.reshape(1, TALL).astype(np.int32)
    af = np.full((128, 2), -10.0, np.float32)
    if s == 0:
        af[:, 1] = -30000.0
    m["aflag"] = af
    m.update(host_consts())
    return m


NH = 64
TT_ALL = TT_P + [(TPRE + i * 512, 512) for i in range(4)]
TWO_PI = 6.283185307179586
C1_2PI = 6.28125
C2_2PI = TWO_PI - C1_2PI

SCRATCH.update({
    "aT": ([576, T], F32), "apT": ([576, TPRE], F32), "ckvnT": ([512, TALL], BF16),
    "kTh": ([NH * 192, TALL], BF16), "vtok": ([TALL, NH * 128], BF16),
    "cqT": ([1024, T], F32), "cqnT": ([1024, T], BF16), "qTh": ([NH * 192, T], BF16),
    "oT": ([NH * 128, T], BF16),
})


def kv_down(fw, cs, P):
    for sg, dst, b_dst in ((seg_L(P), P.aT, P.b_aT), (seg_P(P), P.apT, P.b_apT)):
        norm_fm(fw, cs, sg.hT, sg.b_hT, P.kv_norm_g, P.b_in, sg.xnT, sg.b_xnT, sg.Tn, D)
        with Scope(fw) as sc:
            ar = Arena(sc, Tn=sg.Tn)
            routes = [(0, 576, dst, 0, 0, None, 1.0, F32, b_dst)]
            gemm(fw, ar, sg.xnT, sg.b_xnT, D, P.kv_w_down, P.b_in, cb_range(0, 512) + [[(512, 32), (544, 32)]],
                 store_epilogue(fw, sc, routes), tts=sg.tts, Tn=sg.Tn)


def rope_tables(fw, P, cosT, sinT, b_tab, sc):
    nc = fw.nc
    V = nc.vector
    A = nc.scalar
    pi_, b_pi = sc.sb([32, TALL], I32, "rp_i")
    ang, b_ang = sc.sb([32, TALL], F32, "rp_a")
    kk, b_kk = sc.sb([32, TALL], F32, "rp_k")
    inv, b_inv = sc.sb([32, 1], F32, "rp_inv")
    fw.dma("sp", pi_[:, :], P.posf.partition_broadcast(32), P.b_in, b_pi, "in")
    fw.dma("sp", inv[:, :], P.invf, P.b_in, b_inv, "in")
    fw.op("dve", lambda: V.tensor_copy(ang[:, :], pi_[:, :]), reads=[b_pi], writes=[b_ang])
    fw.op("dve", lambda: V.tensor_scalar(ang[:, :], ang[:, :], 16.0, inv[:, 0:1], ALU.add, ALU.mult),
          reads=[b_ang, b_inv], writes=[b_ang])
    fw.op("dve", lambda: V.tensor_scalar(kk[:, :], ang[:, :], 1.0 / TWO_PI, 12582912.0, ALU.mult, ALU.add),
          reads=[b_ang], writes=[b_kk])
    fw.op("dve", lambda: V.tensor_scalar(kk[:, :], kk[:, :], 12582912.0, None, ALU.subtract), reads=[b_kk], writes=[b_kk])
    fw.op("dve", lambda: V.scalar_tensor_tensor(ang[:, :], kk[:, :], -C1_2PI, ang[:, :], ALU.mult, ALU.add),
          reads=[b_kk, b_ang], writes=[b_ang])
    fw.op("dve", lambda: V.scalar_tensor_tensor(ang[:, :], kk[:, :], -C2_2PI, ang[:, :], ALU.mult, ALU.add),
          reads=[b_kk, b_ang], writes=[b_ang])
    fw.op("dve", lambda: V.tensor_scalar(ang[:, :], ang[:, :], 3.1415925, -3.1415925, ALU.min, ALU.max),
          reads=[b_ang], writes=[b_ang])
    fw.op("act", lambda: A.activation(sinT[:, :], ang[:, :], AF.Sin), reads=[b_ang], writes=[b_tab])
    fw.op("act", lambda: A.activation(kk[:, :], ang[:, :], AF.Abs), reads=[b_ang], writes=[b_kk])
    fw.op("act", lambda: A.activation(cosT[:, :], kk[:, :], AF.Sin, bias=1.5707963, scale=-1.0), reads=[b_kk], writes=[b_tab])


def a_src(P, t0):
    return (P.apT, P.b_apT, t0) if t0 < TPRE else (P.aT, P.b_aT, t0 - TPRE)


def kv_k(fw, cs, P):
    nc = fw.nc
    V, A, G = nc.vector, nc.scalar, nc.gpsimd
    norm_fm(fw, cs, P.apT[0:512, :], P.b_apT, P.kv_lat_g, P.b_in, P.ckvnT[:, 0:TPRE], P.b_ckvnT, TPRE, 512)
    norm_fm(fw, cs, P.aT[0:512, :], P.b_aT, P.kv_lat_g, P.b_in, P.ckvnT[:, TPRE:TALL], P.b_ckvnT, T, 512)
    with Scope(fw) as sc:
        cosT, _ = sc.sb([32, TALL], F32, "kk_cos")
        sinT, _ = sc.sb([32, TALL], F32, "kk_sin")
        b_tab = sc.track(Buf("kk_tab", multi=True))
        kr1, b_kr1 = sc.sb([32, TALL], F32, "kk_kr1", multi=True)
        kr2, b_kr2 = sc.sb([32, TALL], F32, "kk_kr2", multi=True)
        ssr, b_ssr = sc.sb([128, TALL], F32, "kk_ssr", multi=True)
        gk, b_gk = sc.sb([128, 8], F32, "kk_g")
        fw.dma("sp", gk[:, :], P.kq_g, P.b_in, b_gk, "in")
        with Scope(fw) as s2:
            rope_tables(fw, P, cosT, sinT, b_tab, s2)
        with Scope(fw) as s3:
            t1s = [s3.sb([32, 512], F32, f"kk_t1{i}") for i in range(2)]
            t2s = [s3.sb([32, 512], F32, f"kk_t2{i}") for i in range(2)]
            sqs = [s3.sb([32, 1024], BF16, f"kk_sq{i}") for i in range(2)]
            m1s = [s3.sb([32, 512], F32, f"kk_m1{i}") for i in range(2)]
            m2s = [s3.sb([32, 512], F32, f"kk_m2{i}") for i in range(2)]
            pss = [s3.ps(name=f"kk_ps{i}") for i in range(2)]
            for i, (t0, tn) in enumerate(TT_ALL):
                src, b_src, l0 = a_src(P, t0)
                (t1, b1), (t2, b2), (sq, bsq), (m1, bm1), (m2, bm2), (ps, bps) = \
                    t1s[i % 2], t2s[i % 2], sqs[i % 2], m1s[i % 2], m2s[i % 2], pss[i % 2]
                fw.dma("sp", t1[:, 0:tn], src[512:544, l0:l0 + tn], b_src, b1, "in")
                fw.dma("sp", t2[:, 0:tn], src[544:576, l0:l0 + tn], b_src, b2, "in")
                fw.op("act", lambda: A.activation(sq[:, 0:tn], t1[:, 0:tn], AF.Square), reads=[b1], writes=[bsq])
                fw.op("act", lambda: A.activation(sq[:, 512:512 + tn], t2[:, 0:tn], AF.Square), reads=[b2], writes=[bsq])
                fw.op("pe", lambda: nc.tensor.matmul(ps[:, 0:tn], cs.onesb[0:32, :], sq[:, 0:tn], start=True, stop=False),
                      reads=[bsq, cs.b], writes=[bps], signal=False)
                fw.op("pe", lambda: nc.tensor.matmul(ps[:, 0:tn], cs.onesb[0:32, :], sq[:, 512:512 + tn], start=False, stop=True),
                      reads=[bsq, cs.b], writes=[bps])
                fw.op("act", lambda: A.copy(ssr[:, t0:t0 + tn], ps[:, 0:tn]), reads=[bps], writes=[b_ssr])
                fw.op("dve", lambda: V.tensor_scalar(t1[:, 0:tn], t1[:, 0:tn], gk[0:32, 1:2], None, ALU.mult), reads=[b1, b_gk], writes=[b1])
                fw.op("dve", lambda: V.tensor_scalar(t2[:, 0:tn], t2[:, 0:tn], gk[0:32, 2:3], None, ALU.mult), reads=[b2, b_gk], writes=[b2])
                rope_pair(fw, t1[:, 0:tn], b1, t2[:, 0:tn], b2, cosT[:, t0:t0 + tn], sinT[:, t0:t0 + tn], b_tab,
                          m1[:, 0:tn], bm1, m2[:, 0:tn], bm2, kr1[:, t0:t0 + tn], b_kr1, kr2[:, t0:t0 + tn], b_kr2)
        with Scope(fw) as s4:
            ar = Arena(s4, Tn=TALL, kmax=4, nps=6)
            sqb = [s4.sb([128, 512], BF16, f"ke_sq{i}") for i in range(2)]
            rsb = [s4.sb([128, 512], F32, f"ke_rs{i}") for i in range(2)]
            knb = [s4.sb([128, 512], BF16, f"ke_kn{i}") for i in range(3)]
            k1b = [s4.sb([32, 1024], BF16, f"ke_k1{i}") for i in range(3)]
            pss = [s4.ps(name=f"ke_ps{i}") for i in range(2)]
            cnt = [0]

            def k_epi(sbi, t0, tn, blocks, kp, nkp):
                for (c0, ncol, ps, b_ps) in blocks:
                    h = c0 // 256
                    i = cnt[0]
                    cnt[0] += 1
                    (sq, bsq), (rs, brs), (kn, bkn), (k1, bk1), (p2, bp2) = sqb[i % 2], rsb[i % 2], knb[i % 3], k1b[i % 3], pss[i % 2]
                    fw.op("act", lambda: A.activation(sq[:, 0:tn], ps[:, 0:tn], AF.Square), reads=[b_ps], writes=[bsq])
                    fw.op("pe", lambda: nc.tensor.matmul(p2[:, 0:tn], cs.onesb[:, :], sq[:, 0:tn], start=True, stop=True),
                          reads=[bsq, cs.b], writes=[bp2])
                    fw.op("dve", lambda: V.tensor_tensor(rs[:, 0:tn], p2[:, 0:tn], ssr[:, t0:t0 + tn], ALU.add),
                          reads=[bp2, b_ssr], writes=[brs])
                    rstd_from_sumsq(fw, rs[:, 0:tn], brs, rs[:, 0:tn], brs, 1.0 / 192.0)
                    fw.op("dve", lambda: V.scalar_tensor_tensor(kn[:, 0:tn], ps[:, 0:tn], gk[:, 0:1], rs[:, 0:tn], ALU.mult, ALU.mult),
                          reads=[b_ps, b_gk, brs], writes=[bkn])
                    fw.op("pool", lambda: G.tensor_tensor(k1[:, 0:tn], kr1[:, t0:t0 + tn], rs[0:32, 0:tn], ALU.mult),
                          reads=[b_kr1, brs], writes=[bk1])
                    fw.op("pool", lambda: G.tensor_tensor(k1[:, 512:512 + tn], kr2[:, t0:t0 + tn], rs[0:32, 0:tn], ALU.mult),
                          reads=[b_kr2, brs], writes=[bk1])
                    r0 = h * 192
                    fw.dma("pool", P.kTh[r0:r0 + 128, t0:t0 + tn], kn[:, 0:tn], bkn, P.b_kTh, "out")
                    fw.dma("pool", P.kTh[r0 + 128:r0 + 160, t0:t0 + tn], k1[:, 0:tn], bk1, P.b_kTh, "out")
                    fw.dma("pool", P.kTh[r0 + 160:r0 + 192, t0:t0 + tn], k1[:, 512:512 + tn], bk1, P.b_kTh, "out")

            sbs = [[(h * 256, 128), ((h + 1) * 256, 128)] for h in range(0, NH, 2)]
            gemm(fw, ar, P.ckvnT, P.b_ckvnT, 512, P.kv_w_up, P.b_in, sbs, k_epi, tts=TT_ALL, Tn=TALL)


def rope_pair(fw, u1, b1, u2, b2, cos, sin, b_tab, m1, bm1, m2, bm2, o1, bo1, o2, bo2):
    nc = fw.nc
    V, G = nc.vector, nc.gpsimd
    fw.op("pool", lambda: G.tensor_tensor(m1, u1, cos, ALU.mult), reads=[b1, b_tab], writes=[bm1])
    fw.op("pool", lambda: G.tensor_tensor(m2, u2, sin, ALU.mult), reads=[b2, b_tab], writes=[bm2])
    fw.op("dve", lambda: V.tensor_tensor(o1, m1, m2, ALU.subtract), reads=[bm1, bm2], writes=[bo1])
    fw.op("pool", lambda: G.tensor_tensor(m1, u1, sin, ALU.mult), reads=[b1, b_tab], writes=[bm1])
    fw.op("pool", lambda: G.tensor_tensor(m2, u2, cos, ALU.mult), reads=[b2, b_tab], writes=[bm2])
    fw.op("dve", lambda: V.tensor_tensor(o2, m1, m2, ALU.add), reads=[bm1, bm2], writes=[bo2])


def kv_v(fw, cs, P):
    nc = fw.nc
    V, A, G = nc.vector, nc.scalar, nc.gpsimd
    wv = P.kv_w_up.rearrange("k (h two c) -> k h two c", two=2, c=128)
    with Scope(fw) as sc:
        act, b_act = sc.sb([128, 4, TALL], BF16, "vv_act")
        fw.dma("sp", act[:, :, :], P.ckvnT.rearrange("(c p) t -> p c t", p=128), P.b_ckvnT, b_act, "in")
        wss = [sc.sb([128, 4, 4, 128], F32, f"vv_ws{i}") for i in range(2)]
        wbs = [sc.sb([128, 4, 512], BF16, f"vv_wb{i}") for i in range(2)]
        evs = [sc.sb([128, 512], BF16, f"vv_ev{i}") for i in range(3)]
        pss = [sc.ps(name=f"vv_ps{i}") for i in range(4)]
        n = 0
        for hg in range(NH // 4):
            (ws, bws), (wb, bwb) = wss[hg % 2], wbs[hg % 2]
            for c in range(4):
                fw.dma("sp", ws[:, c, :, :], wv[c * 128:(c + 1) * 128, hg * 4:hg * 4 + 4, 1, :], P.b_in, bws, "in")
            fw.op("pool", lambda: G.tensor_copy(wb[:, :, :], ws[:, :, :, :].rearrange("p c h d -> p c (h d)")),
                  reads=[bws], writes=[bwb])
            for (t0, tn) in tok_blocks(TALL, 128):
                ps, bps = pss[n % 4]
                ev, bev = evs[n % 3]
                n += 1
                for c in range(4):
                    fw.op("pe", lambda: nc.tensor.matmul(ps[0:tn, :], act[:, c, t0:t0 + tn], wb[:, c, :], start=(c == 0), stop=(c == 3)),
                          reads=[b_act, bwb], writes=[bps], signal=(c == 3))
                evac_copy(fw, ev[0:tn, :], bev, ps[0:tn, :], bps, n)
                fw.dma("pool", P.vtok[t0:t0 + tn, hg * 512:(hg + 1) * 512], ev[0:tn, :], bev, P.b_vtok, "out")


def q_proj(fw, cs, P):
    nc = fw.nc
    V, A, G = nc.vector, nc.scalar, nc.gpsimd
    norm_fm(fw, cs, P.hT, P.b_hT, P.b_norm_g, P.b_in, P.xnT, P.b_xnT, T, D)
    with Scope(fw) as sc:
        ar = Arena(sc)
        routes = [(0, 1024, P.cqT, 0, 0, None, 1.0, F32, P.b_cqT)]
        gemm(fw, ar, P.xnT, P.b_xnT, D, P.w_dq, P.b_in, cb_range(0, 1024), store_epilogue(fw, sc, routes))
    norm_fm(fw, cs, P.cqT, P.b_cqT, P.q_lat_g, P.b_in, P.cqnT, P.b_cqnT, T, 1024)
    with Scope(fw) as sc:
        cosT, _ = sc.sb([32, TALL], F32, "qq_cos")
        sinT, _ = sc.sb([32, TALL], F32, "qq_sin")
        b_tab = sc.track(Buf("qq_tab", multi=True))
        gq, b_gq = sc.sb([128, 8], F32, "qq_g")
        fw.dma("sp", gq[:, :], P.kq_g, P.b_in, b_gq, "in")
        with Scope(fw) as s2:
            rope_tables(fw, P, cosT, sinT, b_tab, s2)
        with Scope(fw) as s4:
            ar = Arena(s4, Tn=T, kmax=8, nps=6)
            sqb = [s4.sb([128, 512], BF16, f"qe_sq{i}") for i in range(2)]
            sqa = [s4.sb([32, 1024], BF16, f"qe_sa{i}") for i in range(2)]
            rsb = [s4.sb([128, 512], F32, f"qe_rs{i}") for i in range(2)]
            qnb = [s4.sb([128, 512], BF16, f"qe_qn{i}") for i in range(3)]
            uab = [s4.sb([32, 1024], F32, f"qe_u{i}") for i in range(2)]
            m1b = [s4.sb([32, 512], F32, f"qe_m1{i}") for i in range(2)]
            m2b = [s4.sb([32, 512], F32, f"qe_m2{i}") for i in range(2)]
            o1b = [s4.sb([32, 1024], BF16, f"qe_o{i}", multi=True) for i in range(3)]
            pss = [s4.ps(name=f"qe_ps{i}") for i in range(2)]
            cnt = [0]

            def q_epi(sbi, t0, tn, blocks, kp, nkp):
                (cn, _, psn, bpn), (ca, _, psa, bpa), (cb_, _, psb, bpb) = blocks
                h = cn // 192
                i = cnt[0]
                cnt[0] += 1
                (sq, bsq), (sa, bsa), (rs, brs), (qn, bqn), (ua, bua) = sqb[i % 2], sqa[i % 2], rsb[i % 2], qnb[i % 3], uab[i % 2]
                (m1, bm1), (m2, bm2), (o1, bo1), (p2, bp2) = m1b[i % 2], m2b[i % 2], o1b[i % 3], pss[i % 2]
                fw.op("act", lambda: A.activation(sq[:, 0:tn], psn[:, 0:tn], AF.Square), reads=[bpn], writes=[bsq])
                fw.op("act", lambda: A.activation(sa[:, 0:tn], psa[0:32, 0:tn], AF.Square), reads=[bpa], writes=[bsa])
                fw.op("act", lambda: A.activation(sa[:, 512:512 + tn], psb[0:32, 0:tn], AF.Square), reads=[bpb], writes=[bsa])
                fw.op("pe", lambda: nc.tensor.matmul(p2[:, 0:tn], cs.onesb[:, :], sq[:, 0:tn], start=True, stop=False),
                      reads=[bsq, cs.b], writes=[bp2], signal=False)
                fw.op("pe", lambda: nc.tensor.matmul(p2[:, 0:tn], cs.onesb[0:32, :], sa[:, 0:tn], start=False, stop=False),
                      reads=[bsa, cs.b], writes=[bp2], signal=False)
                fw.op("pe", lambda: nc.tensor.matmul(p2[:, 0:tn], cs.onesb[0:32, :], sa[:, 512:512 + tn], start=False, stop=True),
                      reads=[bsa, cs.b], writes=[bp2])
                rstd_from_sumsq(fw, p2[:, 0:tn], bp2, rs[:, 0:tn], brs, 1.0 / 192.0)
                fw.op("dve", lambda: V.scalar_tensor_tensor(qn[:, 0:tn], psn[:, 0:tn], gq[:, 3:4], rs[:, 0:tn], ALU.mult, ALU.mult),
                      reads=[bpn, b_gq, brs], writes=[bqn])
                fw.op("dve", lambda: V.scalar_tensor_tensor(ua[:, 0:tn], psa[0:32, 0:tn], gq[0:32, 4:5], rs[0:32, 0:tn], ALU.mult, ALU.mult),
                      reads=[bpa, b_gq, brs], writes=[bua])
                fw.op("dve", lambda: V.scalar_tensor_tensor(ua[:, 512:512 + tn], psb[0:32, 0:tn], gq[0:32, 5:6], rs[0:32, 0:tn], ALU.mult, ALU.mult),
                      reads=[bpb, b_gq, brs], writes=[bua])
                tt0 = TPRE + t0
                rope_pair(fw, ua[:, 0:tn], bua, ua[:, 512:512 + tn], bua, cosT[:, tt0:tt0 + tn], sinT[:, tt0:tt0 + tn], b_tab,
                          m1[:, 0:tn], bm1, m2[:, 0:tn], bm2, o1[:, 0:tn], bo1, o1[:, 512:512 + tn], bo1)
                r0 = h * 192
                fw.dma("pool", P.qTh[r0:r0 + 128, t0:t0 + tn], qn[:, 0:tn], bqn, P.b_qTh, "out")
                fw.dma("pool", P.qTh[r0 + 128:r0 + 160, t0:t0 + tn], o1[:, 0:tn], bo1, P.b_qTh, "out")
                fw.dma("pool", P.qTh[r0 + 160:r0 + 192, t0:t0 + tn], o1[:, 512:512 + tn], bo1, P.b_qTh, "out")

            sbs = [[(h * 192, 128), (h * 192 + 128, 32), (h * 192 + 160, 32)] for h in range(NH)]
            gemm(fw, ar, P.cqnT, P.b_cqnT, 1024, P.w_uq, P.b_in, sbs, q_epi, kpass=8)


KB = [(128 * i, 128) for i in range(16)] + [(NEUT, 16)] + [(TPRE + 128 * i, 128) for i in range(16)]
ATT_SCALE = 192.0 ** -0.5


def attention(fw, cs, P):
    nc = fw.nc
    V, A, G = nc.vector, nc.scalar, nc.gpsimd
    with Scope(fw) as sc:
        tri, b_tri = sc.sb([128, 128], BF16, "at_tri")
        bias, b_bias = sc.sb([128, 2], F32, "at_bias")
        fw.dma("sp", tri[:, :], P.maskT, P.b_in, b_tri, "in")
        fw.dma("sp", bias[:, :], P.aflag, P.b_in, b_bias, "in")
        qnb = [sc.sb([128, T], BF16, f"at_qn{i}") for i in range(2)]
        qrb = [sc.sb([64, T], BF16, f"at_qr{i}") for i in range(2)]
        knb = [sc.sb([128, TALL], BF16, f"at_kn{i}") for i in range(2)]
        krb = [sc.sb([64, TALL], BF16, f"at_kr{i}") for i in range(2)]
        vgb = [sc.sb([128, 33, 512], BF16, f"at_vg{i}", multi=True) for i in range(2)]
        ptb = [sc.sb([128, 512], BF16, f"at_pt{i}") for i in range(4)]
        rcb = [sc.sb([128, 512], F32, f"at_rc{i}") for i in range(2)]
        accb = [sc.sb([128, 512], F32, f"at_acc{i}") for i in range(2)]
        onesf, b_onesf = sc.sb([128, 128], F32, "at_onesf")
        fw.op("pool", lambda: G.memset(onesf[:, :], 1.0), writes=[b_onesf])
        obb = [sc.sb([128, 512], BF16, f"at_ob{i}") for i in range(2)]
        Sb = [sc.ps(name=f"at_S{i}") for i in range(2)]
        OTb = [sc.ps(name=f"at_O{i}") for i in range(2)]
        SMb = [sc.ps(name=f"at_M{i}") for i in range(2)]

        def load_head(h):
            if h >= NH:
                return
            r0 = h * 192
            fw.dma("sp", qnb[h % 2][0][:, :], P.qTh[r0:r0 + 128, :], P.b_qTh, qnb[h % 2][1], "in")
            fw.dma("sp", qrb[h % 2][0][:, :], P.qTh[r0 + 128:r0 + 192, :], P.b_qTh, qrb[h % 2][1], "in")
            fw.dma("sp", knb[h % 2][0][:, :], P.kTh[r0:r0 + 128, :], P.b_kTh, knb[h % 2][1], "in")
            fw.dma("sp", krb[h % 2][0][:, :], P.kTh[r0 + 128:r0 + 192, :], P.b_kTh, krb[h % 2][1], "in")
            if h % 4 == 0:
                g = h // 4
                vt, bvt = vgb[g % 2]
                fw.dma("sp", vt[:, 0:16, :], P.vtok[0:NEUT, g * 512:(g + 1) * 512].rearrange("(b p) c -> p b c", p=128),
                       P.b_vtok, bvt, "in")
                fw.dma("sp", vt[0:16, 16, :], P.vtok[NEUT:TPRE, g * 512:(g + 1) * 512], P.b_vtok, bvt, "in")
                fw.dma("sp", vt[:, 17:33, :], P.vtok[TPRE:TALL, g * 512:(g + 1) * 512].rearrange("(b p) c -> p b c", p=128),
                       P.b_vtok, bvt, "in")

        units = []
        ti = 0
        for h in range(NH):
            for (t0, tn) in TT:
                us = []
                for kb, (k0, kn_) in enumerate(KB):
                    if k0 < TPRE:
                        us.append([h, ti, t0, tn, kb, k0, kn_, 0, False, k0 < NEUT])
                    else:
                        lk0 = k0 - TPRE
                        if lk0 > t0 + tn - 1:
                            continue
                        if lk0 + kn_ - 1 <= t0:
                            us.append([h, ti, t0, tn, kb, k0, kn_, 0, False, False])
                        else:
                            us.append([h, ti, t0, tn, kb, k0, kn_, lk0 - t0, True, False])
                for i, u in enumerate(us):
                    u.append(i == 0)
                    u.append(i == len(us) - 1)
                units += us
                ti += 1

        def emit_S(ui):
            h, ti, t0, tn, kb, k0, kn_, c0, diag, pre, first, last = units[ui]
            S, bS = Sb[ui % 2]
            pt, bpt = ptb[ui % 4]
            ncol = tn - c0
            (qn, bqn), (qr, bqr), (kn, bkn), (kr, bkr) = qnb[h % 2], qrb[h % 2], knb[h % 2], krb[h % 2]
            fw.op("pe", lambda: nc.tensor.matmul(S[0:kn_, 0:ncol], kn[:, k0:k0 + kn_], qn[:, t0 + c0:t0 + tn], start=True, stop=False),
                  reads=[bkn, bqn], writes=[bS], signal=False)
            fw.op("pe", lambda: nc.tensor.matmul(S[0:kn_, 0:ncol], kr[:, k0:k0 + kn_], qr[:, t0 + c0:t0 + tn], start=False, stop=True),
                  reads=[bkr, bqr], writes=[bS])
            bcol = bias[0:kn_, 1:2] if pre else bias[0:kn_, 0:1]
            fw.op("act", lambda: A.activation(pt[0:kn_, 0:ncol], S[0:kn_, 0:ncol], AF.Exp, bias=bcol, scale=ATT_SCALE),
                  reads=[bS, b_bias], writes=[bpt])
            if diag:
                fw.op("pool", lambda: G.tensor_tensor(pt[0:kn_, 0:kn_], pt[0:kn_, 0:kn_], tri[0:kn_, 0:kn_], ALU.mult),
                      reads=[bpt, b_tri], writes=[bpt])

        def emit_PV(ui):
            h, ti, t0, tn, kb, k0, kn_, c0, diag, pre, first, last = units[ui]
            pt, bpt = ptb[ui % 4]
            OT, bOT = OTb[ti % 2]
            SM, bSM = SMb[ti % 2]
            acc, bacc = accb[ti % 2]
            vt, bvt = vgb[(h // 4) % 2]
            hh = h % 4
            fw.op("pe", lambda: nc.tensor.matmul(OT[:, c0:tn], vt[0:kn_, kb, hh * 128:(hh + 1) * 128], pt[0:kn_, 0:tn - c0],
                                                 start=first, stop=last),
                  reads=[bvt, bpt], writes=[bOT], signal=False)
            fw.op("pe", lambda: nc.tensor.matmul(SM[:, c0:tn], cs.onesb[0:kn_, :], pt[0:kn_, 0:tn - c0], start=first, stop=last),
                  reads=[cs.b, bpt], writes=[bSM])
            if last:
                rc, brc = rcb[ti % 2]
                ob, bob = obb[ti % 2]
                fw.op("dve", lambda: V.reciprocal(rc[:, 0:tn], SM[:, 0:tn]), reads=[bSM], writes=[brc])
                fw.op("dve", lambda: V.tensor_tensor(ob[:, 0:tn], OT[:, 0:tn], rc[:, 0:tn], ALU.mult), reads=[bOT, brc], writes=[bob])
                fw.dma("pool", P.oT[h * 128:(h + 1) * 128, t0:t0 + tn], ob[:, 0:tn], bob, P.b_oT, "out")

        load_head(0)
        n = len(units)
        import os
        if os.environ.get("ATT_UNITS"):
            n = int(os.environ["ATT_UNITS"])
        for ui in range(n + 1):
            if ui < n:
                emit_S(ui)
            if ui >= 1:
                emit_PV(ui - 1)
            if ui < n and (ui == 0 or units[ui][0] != units[ui - 1][0]):
                load_head(units[ui][0] + 1)


def o_proj(fw, cs, P):
    with Scope(fw) as sc:
        ar = Arena(sc)
        gemm(fw, ar, P.oT, P.b_oT, NH * 128, P.w_o, P.b_in, cb_range(0, D), resid_epilogue(fw, sc, P.hT, P.b_hT))


def transpose_out(fw, cs, P, out, b_out):
    nc = fw.nc
    with Scope(fw) as sc:
        hin = [sc.sb([128, 32, 512], F32, f"to_h{i}") for i in range(2)]
        xo = [sc.sb([128, D], F32, f"to_x{i}") for i in range(2)]
        pss = [sc.ps(name=f"to_ps{i}") for i in range(4)]
        ips = 0
        nb = 0
        for gi in range(4):
            g0 = gi * 512
            h_t, h_b = hin[gi % 2]
            fw.dma("sp", h_t[:, :, :], P.hT[:, g0:g0 + 512].rearrange("(c p) t -> p c t", p=128), P.b_hT, h_b, "in")
            for bi in range(4):
                x_t, x_b = xo[nb % 2]
                nb += 1
                for c4 in range(0, 32, 4):
                    ps_t, ps_b = pss[ips % 4]
                    ips += 1
                    for j in range(4):
                        c = c4 + j
                        fw.op("pe", lambda: nc.tensor.transpose(ps_t[:, j * 128:(j + 1) * 128], h_t[:, c, bi * 128:(bi + 1) * 128],
                                                                cs.ident[:, :]),
                              reads=[h_b, cs.b], writes=[ps_b], signal=(j == 3))
                    if (c4 // 4) % 2 == 0:
                        fw.op("dve", lambda: nc.vector.tensor_copy(x_t[:, c4 * 128:(c4 + 4) * 128], ps_t[:, :]), reads=[ps_b], writes=[x_b])
                    else:
                        fw.op("act", lambda: nc.scalar.copy(x_t[:, c4 * 128:(c4 + 4) * 128], ps_t[:, :]), reads=[ps_b], writes=[x_b])
                r0 = gi * 512 + bi * 128
                fw.dma("pool", out[r0:r0 + 128, :], x_t[:, :], x_b, b_out, "out")


INPUTS.update({"hT_in": ([D, T], F32), "aT_in": ([576, T], F32), "apT_in": ([576, TPRE], F32)})
SCRATCH.update({"out": ([T, D], F32)})


def copy_dram(fw, dst, b_dst, src, b_src, rows, tmpname="cp"):
    b_tmp = Buf(tmpname)
    step = 512
    for r0 in range(0, rows, step):
        r1 = min(rows, r0 + step)
        fw.dma("sp", dst[r0:r1, :], src[r0:r1, :], b_src, b_dst, "in")


def build_layer1(fw, cs, P, out, b_out):
    kv_k(fw, cs, P)
    kv_v(fw, cs, P)
    q_proj(fw, cs, P)
    attention(fw, cs, P)
    o_proj(fw, cs, P)
    ffn(fw, cs, P, 1)
    transpose_out(fw, cs, P, out, b_out)


NEED_A = ["xloc", "xpre", "w_in", "w_out", "a_norm_g", "gout_l", "mparams", "sel8", "maskT", "ident_f", "ident_b", "ones_b",
          "w_gu0", "w_dn0", "ffn_g0", "kv_norm_g", "kv_w_down"]
NEED_B = ["hT_in", "aT_in", "apT_in", "kv_w_up", "kv_lat_g", "kq_g", "posf", "invf", "aflag", "maskT", "ident_f", "ident_b", "ones_b",
          "b_norm_g", "w_dq", "q_lat_g", "w_uq", "w_o", "ffn_g1", "w_gu1", "w_dn1"]


def build_A():
    nc = bass.Bass("TRN2", target_bir_lowering=False)
    P = make_P(nc, NEED_A, ["hT", "aT"])
    fw = FW(nc)
    cs = Consts(fw, P.ident_f, P.ident_b, P.ones_b)
    build_layer0(nc, P, fw, cs)
    kv_down(fw, cs, P)
    fw.drain([P.b_hT, P.b_aT])
    return nc


def build_B():
    nc = bass.Bass("TRN2", target_bir_lowering=False)
    P = make_P(nc, NEED_B, ["out"])
    fw = FW(nc)
    cs = Consts(fw, P.ident_f, P.ident_b, P.ones_b)
    copy_dram(fw, P.hT, P.b_hT, P.hT_in, P.b_in, D)
    P.aT, P.b_aT = P.aT_in, P.b_in
    P.apT, P.b_apT = P.apT_in, P.b_in
    build_layer1(fw, cs, P, P.out, P.b_out)
    fw.drain([P.b_out])
    return nc


def kernel_2launch(**inputs):
    inputs = {k: np.asarray(v) for k, v in inputs.items()}
    maps = [host_inputs(inputs, c) for c in range(NCORES)]
    ncA = build_A()
    resA = run_bass_kernel_spmd(ncA, [{k: m[k] for k in NEED_A} for m in maps], core_ids=list(range(NCORES)))
    hTs = [np.asarray(r["hT"]) for r in resA.results]
    aTs = [np.asarray(r["aT"]) for r in resA.results]
    del resA
    for c in range(NCORES):
        maps[c]["hT_in"] = hTs[c]
        maps[c]["aT_in"] = aTs[c]
        if c % 2 == 1:
            maps[c]["apT_in"] = np.ascontiguousarray(aTs[c - 1][:, 0:TPRE])
        else:
            maps[c]["apT_in"] = np.zeros((576, TPRE), np.float32)
    ncB = build_B()
    resB = run_bass_kernel_spmd(ncB, [{k: m[k] for k in NEED_B} for m in maps], core_ids=list(range(NCORES)))
    out = np.empty((4, 4096, D), np.float32)
    for c in range(NCORES):
        b, s = c // 2, c % 2
        out[b, s * 2048:(s + 1) * 2048] = np.asarray(resB.results[c]["out"])
    return out


SCRATCH_UNUSED = {"agT": ([NCORES * 576, T], F32)}
INPUTS.update({"selw": ([128, NCORES], F32)})


def exchange_latent(fw, cs, P):
    nc = fw.nc
    V = nc.vector
    sem = fw.get_sem()
    fw._deps("pool", [P.b_aT], [P.b_agT])
    inst = nc.gpsimd.collective_compute("AllGather", mybir.AluOpType.bypass,
                                        replica_groups=[list(range(NCORES))],
                                        ins=[P.aT[:, :]], outs=[P.agT[:, :]])
    sem.n += 16
    inst.then_inc(sem.h, 16)
    P.b_agT.w[sem] = sem.n
    P.b_aT.r[sem] = sem.n
    with Scope(fw) as sc:
        sw, b_sw = sc.sb([128, NCORES], F32, "ex_w")
        fw.dma("sp", sw[:, :], P.selw, P.b_in, b_sw, "in")
        ins_ = [sc.sb([128, 512], F32, f"ex_i{i}") for i in range(4)]
        accs = [sc.sb([128, 512], F32, f"ex_a{i}") for i in range(2)]
        n = 0
        k = 0
        for (r0, rn) in [(0, 128), (128, 128), (256, 128), (384, 128), (512, 64)]:
            for t0 in range(0, TPRE, 512):
                acc, b_acc = accs[n % 2]
                n += 1
                for r in range(NCORES):
                    it, b_it = ins_[k % 4]
                    k += 1
                    fw.dma("sp", it[0:rn, :], P.agT[r * 576 + r0:r * 576 + r0 + rn, t0:t0 + 512], P.b_agT, b_it, "in")
                    if r == 0:
                        fw.op("dve", lambda: V.tensor_scalar(acc[0:rn, :], it[0:rn, :], sw[0:rn, 0:1], None, ALU.mult),
                              reads=[b_it, b_sw], writes=[b_acc])
                    else:
                        fw.op("dve", lambda: V.scalar_tensor_tensor(acc[0:rn, :], it[0:rn, :], sw[0:rn, r:r + 1], acc[0:rn, :], ALU.mult, ALU.add),
                              reads=[b_it, b_sw, b_acc], writes=[b_acc])
                fw.dma("pool", P.apT[r0:r0 + rn, t0:t0 + 512], acc[0:rn, :], b_acc, P.b_apT, "out")
    fw.put_sems([])
    fw.sem_pool.append(sem)


NEED_F = sorted(set(NEED_A + [k for k in NEED_B if k not in ("hT_in", "aT_in", "apT_in")]))


def build_fused():
    nc = bass.Bass("TRN2", target_bir_lowering=False)
    P = make_P(nc, NEED_F, ["out"])
    fw = FW(nc)
    cs = Consts(fw, P.ident_f, P.ident_b, P.ones_b)
    build_layer0(nc, P, fw, cs)
    kv_down(fw, cs, P)
    build_layer1(fw, cs, P, P.out, P.b_out)
    fw.drain([P.b_out])
    return nc


def kernel(**inputs):
    inputs = {k: np.asarray(v) for k, v in inputs.items()}
    maps = [host_inputs(inputs, c) for c in range(NCORES)]
    nc = build_fused()
    res = run_bass_kernel_spmd(nc, [{k: m[k] for k in NEED_F} for m in maps], core_ids=list(range(NCORES)))
    out = np.empty((4, 4096, D), np.float32)
    for c in range(NCORES):
        b, s = c // 2, c % 2
        out[b, s * 2048:(s + 1) * 2048] = np.asarray(res.results[c]["out"])
    return out
```

```python
import numpy as np
import concourse.bass as bass
import concourse.mybir as mybir
from concourse.bass_utils import run_bass_kernel_spmd

F32 = mybir.dt.float32
BF16 = mybir.dt.bfloat16
I32 = mybir.dt.int32
AF = mybir.ActivationFunctionType
ALU = mybir.AluOpType
AX = mybir.AxisListType

NCORES = 8
D = 4096
T = 2048
TPRE = 2064
NEUT = 2048
TT = [(i * 512, 512) for i in range(4)]
TT_P = [(0, 512), (512, 512), (1024, 512), (1536, 512), (2048, 16)]


class Sem:
    _k = 0

    def __init__(self, nc, name):
        Sem._k += 1
        self.h = nc.alloc_semaphore(f"{name}_{Sem._k}")
        self.n = 0


class Buf:
    _id = 0

    def __init__(self, name=None, multi=False):
        Buf._id += 1
        self.name = name or f"b{Buf._id}"
        self.w = {}
        self.r = {}
        self.sem_in = None
        self.sem_out = None
        self.multi = multi


class FW:
    ENG = ("pe", "act", "dve", "pool", "sp")

    def __init__(self, nc):
        self.nc = nc
        self.eng = {"pe": nc.tensor, "act": nc.scalar, "dve": nc.vector,
                    "pool": nc.gpsimd, "sp": nc.sync}
        self.prog = {e: Sem(nc, f"prog_{e}") for e in self.ENG}
        self.waited = {e: {} for e in self.ENG}
        self.ninst = 0
        self.sem_pool = []

    def get_sem(self):
        if self.sem_pool:
            return self.sem_pool.pop()
        return Sem(self.nc, "dq")

    def put_sems(self, bufs):
        for b in bufs:
            for a in ("sem_in", "sem_out"):
                sm = getattr(b, a)
                if sm is not None:
                    self.sem_pool.append(sm)
                    setattr(b, a, None)

    def _wait(self, e, sem, cnt):
        if cnt <= 0:
            return
        w = self.waited[e]
        if w.get(sem, 0) >= cnt:
            return
        if sem is self.prog[e] and cnt > sem.n:
            return
        self.eng[e].wait_ge(sem.h, cnt)
        w[sem] = cnt

    def _deps(self, e, reads, writes, skip=None):
        pe = self.prog["pe"]
        for b in reads:
            for s, c in b.w.items():
                if (e == "pe" and s is pe) or s is skip:
                    continue
                self._wait(e, s, c)
        for b in writes:
            for s, c in b.r.items():
                if (e == "pe" and s is pe) or s is skip:
                    continue
                self._wait(e, s, c)
            if not b.multi:
                for s, c in b.w.items():
                    if (e == "pe" and s is pe) or s is skip:
                        continue
                    self._wait(e, s, c)

    def _mark(self, tok, reads, writes):
        s, c = tok
        for b in reads:
            if b.r.get(s, 0) < c:
                b.r[s] = c
        for b in writes:
            if b.multi:
                if b.w.get(s, 0) < c:
                    b.w[s] = c
            else:
                b.w = {s: c}
                b.r = {}

    def op(self, e, fn, reads=(), writes=(), signal=True):
        self._deps(e, reads, writes)
        inst = fn()
        self.ninst += 1
        p = self.prog[e]
        if signal:
            p.n += 1
            inst.then_inc(p.h, 1)
            tok = (p, p.n)
        else:
            tok = (p, p.n + 1)
        self._mark(tok, reads, writes)
        self.last = inst
        return tok

    def no_ldweights(self):
        old = self.last.ins
        new = mybir.InstMatmult(
            name=old.name, opcode=old.opcode, engine=old.engine, debug=old.debug, ins=old.ins, outs=old.outs,
            sync_info=old.sync_info, start_tensor_calc=old.start_tensor_calc, stop_tensor_calc=old.stop_tensor_calc,
            is_transpose=old.is_transpose, tile_size=old.tile_size, tile_position=old.tile_position,
            perf_mode=old.perf_mode, bass_skip_group_check=old.bass_skip_group_check, ldweights=False)
        self.nc.register_instruction(new, overwrite=True)

    def dma(self, q, out_ap, in_ap, src, dst, side, **kw):
        if side == "in":
            if dst.sem_in is None:
                dst.sem_in = self.get_sem()
            sem = dst.sem_in
        else:
            if src.sem_out is None:
                src.sem_out = self.get_sem()
            sem = src.sem_out
        self._deps(q, [src], [dst], skip=sem)
        inst = self.eng[q].dma_start(out=out_ap, in_=in_ap, **kw)
        sem.n += 16
        inst.then_inc(sem.h, 16)
        self.ninst += 1
        s, c = sem, sem.n
        if src.r.get(s, 0) < c:
            src.r[s] = c
        if dst.multi:
            dst.w[s] = c
        else:
            dst.w = {s: c}
            dst.r = {}
        return (s, c)

    def drain(self, bufs):
        for b in bufs:
            for s, c in list(b.w.items()) + list(b.r.items()):
                self._wait("sp", s, c)
        for e in self.ENG:
            if e != "sp":
                self._wait("sp", self.prog[e], self.prog[e].n)
        self.nc.all_engine_barrier()


import os as _os
GEMM_KOUTER = bool(int(_os.environ.get('GEMM_KOUTER', '0')))
GEMM_ROWTILE = bool(int(_os.environ.get('GEMM_ROWTILE', '0')))
GEMM_NOLDW = bool(int(_os.environ.get('GEMM_NOLDW', '0')))


class Arena:
    def __init__(self, sc, Tn=T, kmax=32, nps=8):
        self.act, _ = sc.sb([128, kmax, Tn], BF16, "g_act")
        self.b_act = [sc.track(Buf(f"act{k}")) for k in range(kmax)]
        self.wb = [sc.sb([128, kmax, 256], BF16, f"g_wb{i}", multi=True) for i in range(2)]
        self.ws = [sc.sb([128, 8, 256], F32, f"g_ws{i}", multi=True) for i in range(4)]
        self.ps = [sc.ps(name=f"g_ps{i}") for i in range(nps)]
        self.nps = nps
        self.ips = 0
        self.iws = 0
        self.iwb = 0


def gemm(fw, ar, actT, b_actT, K, W, b_W, superblocks, epilogue, tts=TT, kpass=32, Tn=T):
    nc = fw.nc
    nkc = K // 128
    npass = -(-nkc // kpass)
    base, rem = nkc // npass, nkc % npass
    passes = []
    k0 = 0
    for p in range(npass):
        n = base + (1 if p < rem else 0)
        passes.append((k0, n))
        k0 += n
    for kp, (kc0, kc) in enumerate(passes):
        for k in range(kc):
            r0 = (kc0 + k) * 128
            fw.dma("sp", ar.act[:, k, 0:Tn], actT[r0:r0 + 128, 0:Tn], b_actT, ar.b_act[k], "in")
        def sb_geom(sb):
            offs = []
            o = 0
            for (c0, n) in sb:
                offs.append(o)
                o += n
            assert o <= 256
            segs = []
            for (c0, n), oo in zip(sb, offs):
                if segs and segs[-1][0] + segs[-1][1] == c0 and segs[-1][2] + segs[-1][1] == oo:
                    segs[-1][1] += n
                else:
                    segs.append([c0, n, oo])
            return offs, o, segs

        def issue_dma(sbi):
            offs, o, segs = sb_geom(superblocks[sbi])
            for gi, g0 in enumerate(range(0, kc, 8)):
                gn = min(8, kc - g0)
                ws, b_ws = ar.ws[gi]
                r0 = (kc0 + g0) * 128
                for (c0, n, oo) in segs:
                    src = W[r0:r0 + gn * 128, c0:c0 + n].rearrange("(c p) j -> p c j", p=128)
                    fw.dma("sp", ws[:, 0:gn, oo:oo + n], src, b_W, b_ws, "in")

        def issue_cast(sbi):
            offs, o, segs = sb_geom(superblocks[sbi])
            wb, b_wb = ar.wb[sbi % 2]
            for gi, g0 in enumerate(range(0, kc, 8)):
                gn = min(8, kc - g0)
                ws, b_ws = ar.ws[gi]
                if gi % 2 == 0:
                    fw.op("dve", lambda: nc.vector.tensor_copy(wb[:, g0:g0 + gn, 0:o], ws[:, 0:gn, 0:o]),
                          reads=[b_ws], writes=[b_wb])
                else:
                    fw.op("act", lambda: nc.scalar.copy(wb[:, g0:g0 + gn, 0:o], ws[:, 0:gn, 0:o]),
                          reads=[b_ws], writes=[b_wb])

        nsb = len(superblocks)
        issue_dma(0)
        issue_cast(0)
        for sbi, sb in enumerate(superblocks):
            offs, o, segs = sb_geom(sb)
            wb, b_wb = ar.wb[sbi % 2]
            if sbi + 1 < nsb:
                issue_dma(sbi + 1)
            if GEMM_KOUTER and len(tts) <= ar.nps:
                for (c0, ncol), oo in zip(sb, offs):
                    pss = []
                    for _ in tts:
                        pss.append(ar.ps[ar.ips % ar.nps])
                        ar.ips += 1
                    for k in range(kc):
                        for ti_, ((t0, tn), (ps, b_ps)) in enumerate(zip(tts, pss)):
                            fw.op("pe", lambda: nc.tensor.matmul(ps[0:ncol, 0:tn], wb[:, k, oo:oo + ncol],
                                                                 ar.act[:, k, t0:t0 + tn],
                                                                 start=(k == 0), stop=(k == kc - 1)),
                                  reads=[b_wb, ar.b_act[k]], writes=[b_ps], signal=(k == kc - 1))
                            if ti_ > 0 and GEMM_NOLDW:
                                fw.no_ldweights()
                    for (t0, tn), (ps, b_ps) in zip(tts, pss):
                        epilogue(sbi, t0, tn, [(c0, ncol, ps, b_ps)], kp, len(passes))
                continue
            if GEMM_ROWTILE:
                for (t0, tn) in tts:
                    blocks = []
                    for (c0, ncol), oo in zip(sb, offs):
                        psA, b_psA = ar.ps[ar.ips % ar.nps]
                        psB, b_psB = ar.ps[(ar.ips + 1) % ar.nps]
                        ar.ips += 2
                        for k in range(kc):
                            fw.op("pe", lambda: nc.tensor.matmul(psA[0:ncol, 0:tn], wb[0:64, k, oo:oo + ncol],
                                                                 ar.act[0:64, k, t0:t0 + tn],
                                                                 start=(k == 0), stop=(k == kc - 1)),
                                  reads=[b_wb, ar.b_act[k]], writes=[b_psA], signal=(k == kc - 1))
                            fw.op("pe", lambda: nc.tensor.matmul(psB[0:ncol, 0:tn], wb[64:128, k, oo:oo + ncol],
                                                                 ar.act[64:128, k, t0:t0 + tn],
                                                                 start=(k == 0), stop=(k == kc - 1)),
                                  reads=[b_wb, ar.b_act[k]], writes=[b_psB], signal=(k == kc - 1))
                        blocks.append((c0, ncol, psA, b_psA, psB, b_psB))
                    epilogue(sbi, t0, tn, blocks, kp, len(passes))
                continue
            for ti_, (t0, tn) in enumerate(tts):
                if ti_ == len(tts) - 1 and sbi + 1 < nsb:
                    issue_cast(sbi + 1)
                blocks = []
                for (c0, ncol), oo in zip(sb, offs):
                    ps, b_ps = ar.ps[ar.ips % ar.nps]
                    ar.ips += 1
                    for k in range(kc):
                        fw.op("pe", lambda: nc.tensor.matmul(ps[0:ncol, 0:tn], wb[:, k, oo:oo + ncol],
                                                             ar.act[:, k, t0:t0 + tn],
                                                             start=(k == 0), stop=(k == kc - 1)),
                              reads=[b_wb, ar.b_act[k]], writes=[b_ps], signal=(k == kc - 1))
                    blocks.append((c0, ncol, ps, b_ps))
                epilogue(sbi, t0, tn, blocks, kp, len(passes))


def cb_range(c0, c1):
    out = []
    c = c0
    while c < c1:
        sb = []
        e = min(c + 256, c1)
        cc = c
        while cc < e:
            n = min(128, e - cc)
            sb.append((cc, n))
            cc += n
        out.append(sb)
        c = e
    return out


import contextlib


class Scope:
    def __init__(self, fw):
        self.fw = fw
        self.nc = fw.nc
        self.stack = contextlib.ExitStack()
        self.bufs = []

    def __enter__(self):
        self.stack.__enter__()
        return self

    def __exit__(self, *a):
        if a[0] is None:
            self.fw.drain(self.bufs)
            self.fw.put_sems(self.bufs)
        return self.stack.__exit__(*a)

    def sb(self, shape, dtype, name=None, multi=False):
        t = self.stack.enter_context(self.nc.sbuf_tensor(list(shape), dtype))
        b = Buf(name, multi=multi)
        self.bufs.append(b)
        return t, b

    def ps(self, shape=(128, 512), dtype=F32, name=None):
        t = self.stack.enter_context(self.nc.psum_tensor(list(shape), dtype))
        b = Buf(name)
        self.bufs.append(b)
        return t, b

    def track(self, b):
        self.bufs.append(b)
        return b


class Consts:
    def __init__(self, fw, ident_f32, ident_bf16, ones_bf16):
        nc = fw.nc
        self.b = Buf("consts", multi=True)
        self.ident = nc.alloc_sbuf_tensor("c_ident", [128, 128], F32)
        self.identb = nc.alloc_sbuf_tensor("c_identb", [128, 128], BF16)
        self.onesb = nc.alloc_sbuf_tensor("c_onesb", [128, 128], BF16)
        bd = Buf("cdram", multi=True)
        fw.dma("sp", self.ident[:, :], ident_f32, bd, self.b, "in")
        fw.dma("sp", self.identb[:, :], ident_bf16, bd, self.b, "in")
        fw.dma("sp", self.onesb[:, :], ones_bf16, bd, self.b, "in")


def tok_blocks(n, bs=128):
    out = []
    t = 0
    while t < n:
        out.append((t, min(bs, n - t)))
        t += bs
    return out


def transpose_in(fw, cs, x, b_x, hT, b_hT, Tn, Dn=D):
    nc = fw.nc
    nk = Dn // 128
    with Scope(fw) as sc:
        xin = [sc.sb([128, Dn], F32, f"ti_x{i}") for i in range(2)]
        ho = [sc.sb([128, nk, 512], F32, f"ti_h{i}") for i in range(2)]
        pss = [sc.ps(name=f"ti_ps{i}") for i in range(4)]
        ips = 0
        groups = tok_blocks(Tn, 512)
        for gi, (g0, gn) in enumerate(groups):
            h_t, h_b = ho[gi % 2]
            for bi, (t0, tn) in enumerate(tok_blocks(gn, 128)):
                x_t, x_b = xin[bi % 2]
                fw.dma("sp", x_t[0:tn, :], x[g0 + t0:g0 + t0 + tn, :], b_x, x_b, "in")
                for c4 in range(0, nk, 4):
                    ps_t, ps_b = pss[ips % 4]
                    ips += 1
                    for j in range(4):
                        c = c4 + j
                        fw.op("pe", lambda: nc.tensor.transpose(ps_t[:, j * 128:j * 128 + tn],
                                                                x_t[0:tn, c * 128:(c + 1) * 128],
                                                                cs.ident[0:tn, 0:tn]),
                              reads=[x_b, cs.b], writes=[ps_b], signal=(j == 3))
                    e = "dve" if (c4 // 4) % 2 == 0 else "act"
                    src = ps_t[:, :].rearrange("p (j t) -> p j t", j=4)[:, :, 0:tn]
                    dst = h_t[:, c4:c4 + 4, t0:t0 + tn]
                    if e == "dve":
                        fw.op("dve", lambda: nc.vector.tensor_copy(dst, src), reads=[ps_b], writes=[h_b])
                    else:
                        fw.op("act", lambda: nc.scalar.copy(dst, src), reads=[ps_b], writes=[h_b])
            fw.dma("pool", hT[:, g0:g0 + gn].rearrange("(c p) t -> p c t", p=128), h_t[:, :, 0:gn],
                   h_b, b_hT, "out")


def norm_fm(fw, cs, src, b_src, g_l, b_g, dst, b_dst, Tn, Kd, tw=256):
    nc = fw.nc
    nk = Kd // 128
    with Scope(fw) as sc:
        g_t, g_b = sc.sb([128, nk], F32, "nf_g")
        fw.dma("sp", g_t[:, :], g_l, b_g, g_b, "in")
        xs = [sc.sb([128, nk, tw], F32, f"nf_x{i}") for i in range(2)]
        sq = [sc.sb([128, nk, tw], BF16, f"nf_sq{i}") for i in range(2)]
        ys = [sc.sb([128, nk, tw], BF16, f"nf_y{i}") for i in range(2)]
        rs = [sc.sb([128, tw], F32, f"nf_r{i}") for i in range(2)]
        pss = [sc.ps(name=f"nf_ps{i}") for i in range(2)]
        for i, (t0, tn) in enumerate(tok_blocks(Tn, tw)):
            x_t, x_b = xs[i % 2]
            s_t, s_b = sq[i % 2]
            y_t, y_b = ys[i % 2]
            r_t, r_b = rs[i % 2]
            ps_t, ps_b = pss[i % 2]
            fw.dma("sp", x_t[:, :, 0:tn], src[:, t0:t0 + tn].rearrange("(c p) t -> p c t", p=128),
                   b_src, x_b, "in")
            fw.op("act", lambda: nc.scalar.activation(s_t[:, :, 0:tn], x_t[:, :, 0:tn], AF.Square),
                  reads=[x_b], writes=[s_b])
            for c in range(nk):
                fw.op("pe", lambda: nc.tensor.matmul(ps_t[:, 0:tn], cs.onesb[:, :], s_t[:, c, 0:tn],
                                                     start=(c == 0), stop=(c == nk - 1)),
                      reads=[s_b, cs.b], writes=[ps_b], signal=(c == nk - 1))
            rstd_from_sumsq(fw, ps_t[:, 0:tn], ps_b, r_t[:, 0:tn], r_b, 1.0 / Kd)
            for c in range(nk):
                fw.op("dve", lambda: nc.vector.scalar_tensor_tensor(
                    y_t[:, c, 0:tn], x_t[:, c, 0:tn], g_t[:, c:c + 1], r_t[:, 0:tn],
                    ALU.mult, ALU.mult), reads=[x_b, g_b, r_b], writes=[y_b], signal=(c == nk - 1))
            fw.dma("pool", dst[:, t0:t0 + tn].rearrange("(c p) t -> p c t", p=128), y_t[:, :, 0:tn],
                   y_b, b_dst, "out")


EPS = 1e-6


def rstd_from_sumsq(fw, ss_ap, ss_b, out_ap, out_b, inv_n):
    nc = fw.nc
    fw.op("act", lambda: nc.scalar.activation(out_ap, ss_ap, AF.Ln, bias=EPS, scale=inv_n),
          reads=[ss_b], writes=[out_b])
    fw.op("act", lambda: nc.scalar.activation(out_ap, out_ap, AF.Exp, scale=-0.5),
          reads=[out_b], writes=[out_b])


class Evac:
    def __init__(self, sc, n=4, dtype=BF16, w=512, name="ev"):
        self.bufs = [sc.sb([128, w], dtype, f"{name}{i}") for i in range(n)]
        self.i = 0

    def next(self):
        t = self.bufs[self.i % len(self.bufs)]
        self.i += 1
        return t


def evac_copy(fw, dst_ap, dst_b, src_ap, src_b, idx, func=None, scale=1.0):
    nc = fw.nc
    if func is not None or idx % 2 == 1:
        f = func if func is not None else AF.Copy
        fw.op("act", lambda: nc.scalar.activation(dst_ap, src_ap, f, scale=scale), reads=[src_b], writes=[dst_b])
    else:
        if scale == 1.0:
            fw.op("dve", lambda: nc.vector.tensor_copy(dst_ap, src_ap), reads=[src_b], writes=[dst_b])
        else:
            fw.op("dve", lambda: nc.vector.tensor_scalar(dst_ap, src_ap, scale, None, ALU.mult),
                  reads=[src_b], writes=[dst_b])


def store_epilogue(fw, sc, routes):
    evb = Evac(sc, 3, BF16, name="evb")
    evf = Evac(sc, 2, F32, name="evf")
    cnt = [0]

    def epi(sbi, t0, tn, blocks, kp, nkp):
        for (c0, ncol, ps, b_ps) in blocks:
            for (lo, hi, dst, roff, toff, func, scale, dt, b_dst) in routes:
                if lo <= c0 < hi:
                    break
            else:
                raise AssertionError(c0)
            ev, b_ev = (evb if dt == BF16 else evf).next()
            evac_copy(fw, ev[0:ncol, 0:tn], b_ev, ps[0:ncol, 0:tn], b_ps, cnt[0], func, scale)
            cnt[0] += 1
            r = roff + c0 - lo
            fw.dma("pool", dst[r:r + ncol, toff + t0:toff + t0 + tn], ev[0:ncol, 0:tn], b_ev, b_dst, "out")
    return epi


def resid_epilogue(fw, sc, hT, b_hT):
    nc = fw.nc
    hb = Evac(sc, 3, F32, name="rh")

    def epi(sbi, t0, tn, blocks, kp, nkp):
        for (c0, ncol, ps, b_ps) in blocks:
            h, b_h = hb.next()
            fw.dma("sp", h[0:ncol, 0:tn], hT[c0:c0 + ncol, t0:t0 + tn], b_hT, b_h, "in")
            fw.op("dve", lambda: nc.vector.tensor_tensor(h[0:ncol, 0:tn], ps[0:ncol, 0:tn], h[0:ncol, 0:tn], ALU.add),
                  reads=[b_ps, b_h], writes=[b_h])
            fw.dma("pool", hT[c0:c0 + ncol, t0:t0 + tn], h[0:ncol, 0:tn], b_h, b_hT, "out")
    return epi


def swiglu_epilogue(fw, sc, hidT, b_hid, dff):
    nc = fw.nc
    sg = Evac(sc, 2, F32, name="sg")
    hb = Evac(sc, 3, BF16, name="hb")

    def epi(sbi, t0, tn, blocks, kp, nkp):
        (cg, ncol, psg, b_psg), (cu, ncol2, psu, b_psu) = blocks
        assert cu == cg + dff and ncol == ncol2
        s, b_s = sg.next()
        h, b_h = hb.next()
        fw.op("act", lambda: nc.scalar.activation(s[0:ncol, 0:tn], psg[0:ncol, 0:tn], AF.Silu),
              reads=[b_psg], writes=[b_s])
        fw.op("dve", lambda: nc.vector.tensor_tensor(h[0:ncol, 0:tn], s[0:ncol, 0:tn], psu[0:ncol, 0:tn], ALU.mult),
              reads=[b_s, b_psu], writes=[b_h])
        fw.dma("pool", hidT[cg:cg + ncol, t0:t0 + tn], h[0:ncol, 0:tn], b_h, b_hid, "out")
    return epi


DFF = 11008


class Seg:
    def __init__(self, hT, b_hT, xnT, b_xnT, Tn, tts, off):
        self.hT, self.b_hT, self.xnT, self.b_xnT, self.Tn, self.tts, self.off = hT, b_hT, xnT, b_xnT, Tn, tts, off


def seg_L(P):
    return Seg(P.hT, P.b_hT, P.xnT, P.b_xnT, T, TT, TPRE)


def seg_P(P):
    return Seg(P.hpT, P.b_hpT, P.xnpT, P.b_xnpT, TPRE, TT_P, 0)


def ffn(fw, cs, P, layer, sg=None):
    sg = sg or seg_L(P)
    norm_fm(fw, cs, sg.hT, sg.b_hT, P.ffn_g[layer], P.b_in, sg.xnT, sg.b_xnT, sg.Tn, D)
    hid = P.hidT[:, 0:sg.Tn]
    with Scope(fw) as sc:
        ar = Arena(sc, Tn=sg.Tn)
        sbs = [[(j * 128, 128), (DFF + j * 128, 128)] for j in range(DFF // 128)]
        gemm(fw, ar, sg.xnT, sg.b_xnT, D, P.w_gu[layer], P.b_in, sbs,
             swiglu_epilogue(fw, sc, hid, P.b_hidT, DFF), tts=sg.tts, Tn=sg.Tn)
    with Scope(fw) as sc:
        ar = Arena(sc, Tn=sg.Tn, kmax=29)
        gemm(fw, ar, hid, P.b_hidT, DFF, P.w_dn[layer], P.b_in, cb_range(0, D),
             resid_epilogue(fw, sc, sg.hT, sg.b_hT), kpass=29, tts=sg.tts, Tn=sg.Tn)


TALL = TPRE + T
MH = 8
MDK = 256
MDV = 512
CHUNKS = [(128 * c, 128, True) for c in range(16)] + [(NEUT, 16, True)] + \
         [(NEUT + 16 + 128 * c, 128, True) for c in range(16)]
NCH = len(CHUNKS)


def l0_proj(fw, cs, P):
    for (xn, b_xn, Tn, tts, off) in ((P.xnT, P.b_xnT, T, TT, TPRE), (P.xnpT, P.b_xnpT, TPRE, TT_P, 0)):
        with Scope(fw) as sc:
            ar = Arena(sc, Tn=Tn)
            routes = [
                (0, 2048, P.qT, 0, off, None, 1.0, BF16, P.b_qT),
                (2048, 4096, P.kT, 0, off, None, 1.0 / 16.0, BF16, P.b_kT),
                (4096, 8192, P.vT, 0, off, None, 1.0, BF16, P.b_vT),
                (8192, 12288, P.ogT, 0, off, AF.Sigmoid, 1.0, BF16, P.b_ogT),
                (12288, 12296, P.graw, 0, off, None, 1.0, F32, P.b_graw),
                (12296, 12304, P.graw, 8, off, None, 1.0, F32, P.b_graw),
            ]
            gemm(fw, ar, xn, b_xn, D, P.w_in, P.b_in, cb_range(0, 12288) + [[(12288, 8), (12296, 8)]],
                 store_epilogue(fw, sc, routes), tts=tts, Tn=Tn)


def mlstm_gates(fw, cs, P, efc, b_efc, lam, b_lam):
    nc = fw.nc
    with Scope(fw) as sc:
        A1, b1 = sc.sb([8, TALL], F32, "mg1")
        A2, b2 = sc.sb([8, TALL], F32, "mg2")
        A3, b3 = sc.sb([8, TALL], F32, "mg3")
        A4, b4 = sc.sb([8, TALL + 1], F32, "mg4")
        sm, bsm = sc.sb([8, 8], F32, "mgs")
        sel, bsel = sc.sb([8, 8, 128], F32, "mgsel")
        zb, bzb = sc.sb([128, 8, 34], F32, "mgzb")
        ps1, bp1 = sc.ps(name="mgp1")
        ps2, bp2 = sc.ps(name="mgp2")
        fw.dma("sp", A1[:, :], P.graw[0:8, :], P.b_graw, b1, "in")
        fw.dma("sp", A2[:, :], P.graw[8:16, :], P.b_graw, b2, "in")
        fw.dma("sp", sm[:, 0:3], P.mparams, P.b_in, bsm, "in")
        fw.dma("sp", sel[:, :, :], P.sel8, P.b_in, bsel, "in")
        V = nc.vector
        fw.op("dve", lambda: V.tensor_scalar(sm[:, 3:5], sm[:, 0:2], 1.0 / 15.0, None, ALU.mult), reads=[bsm], writes=[bsm])
        fw.op("dve", lambda: V.tensor_scalar(sm[:, 5:6], sm[:, 2:3], -1.0, 30000.0, ALU.add, ALU.mult), reads=[bsm], writes=[bsm])
        fw.op("act", lambda: nc.scalar.activation(A1[:, :], A1[:, :], AF.Tanh, bias=sm[:, 3:4], scale=1.0 / 15.0),
              reads=[b1, bsm], writes=[b1])
        fw.op("dve", lambda: V.tensor_scalar(A1[:, :], A1[:, :], 15.0, None, ALU.mult), reads=[b1], writes=[b1])
        fw.op("act", lambda: nc.scalar.activation(A2[:, :], A2[:, :], AF.Tanh, bias=sm[:, 4:5], scale=1.0 / 15.0),
              reads=[b2, bsm], writes=[b2])
        fw.op("act", lambda: nc.scalar.activation(A2[:, :], A2[:, :], AF.Exp, scale=-15.0), reads=[b2], writes=[b2])
        fw.op("act", lambda: nc.scalar.activation(A2[:, :], A2[:, :], AF.Ln, bias=1.0, scale=1.0), reads=[b2], writes=[b2])
        fw.op("dve", lambda: V.tensor_scalar(A2[:, :], A2[:, :], -1.0, None, ALU.mult), reads=[b2], writes=[b2])
        fw.op("dve", lambda: V.tensor_scalar(A2[:, 0:NEUT], A2[:, 0:NEUT], sm[:, 2:3], None, ALU.mult),
              reads=[b2, bsm], writes=[b2])
        fw.op("dve", lambda: V.tensor_scalar(A1[:, 0:NEUT], A1[:, 0:NEUT], sm[:, 2:3], sm[:, 5:6], ALU.mult, ALU.add),
              reads=[b1, bsm], writes=[b1])
        fw.op("pool", lambda: nc.gpsimd.memset(A4[:, :], 0.0), writes=[b4])
        fw.op("dve", lambda: V.tensor_tensor_scan(A3[:, :], A2[:, :], A4[:, 0:TALL], 0.0, ALU.add, ALU.add),
              reads=[b2, b4], writes=[b3])
        fw.op("dve", lambda: V.tensor_tensor(A1[:, :], A1[:, :], A3[:, :], ALU.subtract), reads=[b1, b3], writes=[b1])
        fw.op("dve", lambda: V.tensor_tensor_scan(A4[:, 1:TALL + 1], A1[:, :], A1[:, :], 0.0, ALU.max, ALU.max),
              reads=[b1], writes=[b4])
        fw.op("dve", lambda: V.tensor_scalar(A4[:, 1:TALL + 1], A4[:, 1:TALL + 1], -1.0, None, ALU.mult),
              reads=[b4], writes=[b4])
        for c, (t0, L, _) in enumerate(CHUNKS):
            fw.op("act", lambda: nc.scalar.activation(A1[:, t0:t0 + L], A1[:, t0:t0 + L], AF.Exp,
                                                      bias=A4[:, t0:t0 + 1], scale=1.0),
                  reads=[b1, b4], writes=[b1], signal=False)
            fw.op("act", lambda: nc.scalar.activation(A3[:, t0:t0 + L], A3[:, t0:t0 + L], AF.Exp,
                                                      bias=A4[:, t0:t0 + 1], scale=-1.0),
                  reads=[b3, b4], writes=[b3], signal=(c == NCH - 1))
        for c, (t0, L, _) in enumerate(CHUNKS):
            pst, bpt = (ps1, bp1) if c % 2 == 0 else (ps2, bp2)
            fw.op("pe", lambda: nc.tensor.transpose(pst[0:L, 0:8], A1[:, t0:t0 + L], cs.ident[0:8, 0:8]),
                  reads=[b1, cs.b], writes=[bpt], signal=False)
            fw.op("pe", lambda: nc.tensor.transpose(pst[0:L, 8:16], A3[:, t0:t0 + L], cs.ident[0:8, 0:8]),
                  reads=[b3, cs.b], writes=[bpt])
            fw.op("dve", lambda: V.tensor_copy(efc[0:L, c, :], pst[0:L, 0:16]), reads=[bpt], writes=[b_efc])
        for h in range(8):
            pst, bpt = (ps1, bp1) if h % 2 == 0 else (ps2, bp2)
            fw.op("pe", lambda: nc.tensor.matmul(pst[:, 0:17], sel[:, h, :], A4[:, 0:NEUT + 1:128], start=True, stop=True),
                  reads=[bsel, b4], writes=[bpt], signal=False)
            fw.op("pe", lambda: nc.tensor.matmul(pst[:, 17:34], sel[:, h, :], A4[:, NEUT + 16:TALL + 1:128], start=True, stop=True),
                  reads=[bsel, b4], writes=[bpt])
            fw.op("dve", lambda: V.tensor_copy(zb[:, h, :], pst[:, 0:34]), reads=[bpt], writes=[bzb])
        fw.op("dve", lambda: V.tensor_tensor(lam[:, :, :], zb[:, :, 1:34], zb[:, :, 0:33], ALU.subtract),
              reads=[bzb], writes=[b_lam])
        fw.op("act", lambda: nc.scalar.activation(lam[:, :, :], lam[:, :, :], AF.Exp), reads=[b_lam], writes=[b_lam])


def mlstm(fw, cs, P):
    import os
    nc = fw.nc
    V = nc.vector
    A = nc.scalar
    G = nc.gpsimd
    efc = nc.alloc_sbuf_tensor("m_efc", [128, NCH, 16], F32)
    lam = nc.alloc_sbuf_tensor("m_lam", [128, 8, NCH], F32)
    b_efc, b_lam = Buf("efc"), Buf("lam")
    mlstm_gates(fw, cs, P, efc, b_efc, lam, b_lam)
    if getattr(P, "debug_gates", False):
        fw.dma("sp", P.dbg_efc, efc[:, :, :].rearrange("p a b -> p (a b)"), b_efc, P.b_dbg_efc, "out")
        fw.dma("sp", P.dbg_lam, lam[:, :, :].rearrange("p a b -> p (a b)"), b_lam, P.b_dbg_lam, "out")
        return
    groups = [(256 * g, 256, [2 * g, 2 * g + 1]) for g in range(8)] + [(NEUT, 16, [16])] + \
             [(NEUT + 16 + 256 * g, 256, [17 + 2 * g, 18 + 2 * g]) for g in range(8)]
    with Scope(fw) as sc:
        kTg = [sc.sb([128, 16, 256], BF16, f"m_k{i}") for i in range(2)]
        vTg = [sc.sb([128, 32, 256], BF16, f"m_v{i}") for i in range(2)]
        qTg = [sc.sb([128, 16, 256], BF16, f"m_q{i}") for i in range(2)]
        ogg = [sc.sb([128, 32, 256], BF16, f"m_o{i}", multi=True) for i in range(2)]
        Cst = [sc.sb([128, 2, 513], F32, f"m_C{h}") for h in range(8)]
        Cbf = [sc.sb([128, 2, 513], BF16, f"m_Cb{h}") for h in range(8)]
        Clt = [sc.sb([128, 513], F32, f"m_Cl{i}") for i in range(2)]
        NR = 4
        kE = [sc.sb([128, 256], BF16, f"m_kE{i}") for i in range(NR)]
        vx = [sc.sb([128, 513], BF16, f"m_vx{i}", multi=True) for i in range(NR)]
        Sp = [sc.sb([128, 128], BF16, f"m_Sp{i}") for i in range(NR)]
        junk = [sc.sb([128, 512], BF16, f"m_jk{i}") for i in range(2)]
        hn = [sc.sb([128, 512], BF16, f"m_hn{i}") for i in range(NR)]
        sml = [sc.sb([128, 8], F32, f"m_sm{i}") for i in range(NR)]
        recl = [sc.sb([128, 1], F32, f"m_rec{i}") for i in range(NR)]
        scal = [sc.sb([128, 1], F32, f"m_scl{i}") for i in range(NR)]
        mask, b_mask = sc.sb([128, 128], BF16, "m_mask")
        gout, b_gout = sc.sb([128, 32], F32, "m_gout")
        B0, b0 = sc.ps([128, 1024], BF16, "m_B0")
        B1, bb1 = sc.ps(name="m_B1")
        B2, bb2 = sc.ps(name="m_B2")
        B3, bb3 = sc.ps(name="m_B3")
        B4, bb4 = sc.ps([128, 1024], BF16, "m_B4")
        B5, bb5 = sc.ps(name="m_B5")
        B6, bb6 = sc.ps(name="m_B6")
        B7, bb7 = sc.ps(name="m_B7")
        fw.dma("sp", mask[:, :], P.maskT, P.b_in, b_mask, "in")
        fw.dma("sp", gout[:, :], P.gout_l, P.b_in, b_gout, "in")
        for h in range(8):
            fw.op("pool", lambda: G.memset(Cst[h][0][:, :, :], 0.0), writes=[Cst[h][1]])
            fw.op("pool", lambda: G.memset(Cbf[h][0][:, :, :], 0.0), writes=[Cbf[h][1]])
        for i in range(NR):
            fw.op("pool", lambda: G.memset(vx[i][0][:, 512:513], 1.0), writes=[vx[i][1]])

        items = []
        for gi, (g0, gn, chs) in enumerate(groups):
            for c in chs:
                for h in range(8):
                    items.append((gi, c, h))
        loaded = set()

        def load_group(gi):
            if gi in loaded or gi >= len(groups):
                return
            loaded.add(gi)
            g0, gn, chs = groups[gi]
            main = CHUNKS[chs[0]][2]
            kt, bk = kTg[gi % 2]
            vt, bv = vTg[gi % 2]
            fw.dma("sp", kt[:, :, 0:gn], P.kT[:, g0:g0 + gn].rearrange("(c p) t -> p c t", p=128), P.b_kT, bk, "in")
            fw.dma("sp", vt[:, :, 0:gn], P.vT[:, g0:g0 + gn].rearrange("(c p) t -> p c t", p=128), P.b_vT, bv, "in")
            if main:
                qt, bq = qTg[gi % 2]
                ot, bo = ogg[gi % 2]
                l0 = g0
                fw.dma("sp", qt[:, :, 0:gn], P.qT[:, l0:l0 + gn].rearrange("(c p) t -> p c t", p=128), P.b_qT, bq, "in")
                fw.dma("sp", ot[:, :, 0:gn], P.ogT[:, l0:l0 + gn].rearrange("(c p) t -> p c t", p=128), P.b_ogT, bo, "in")

        def ctx(idx):
            gi, c, h = items[idx]
            g0, gn, chs = groups[gi]
            t0, L, main = CHUNKS[c]
            return gi, c, h, t0 - g0, L, main

        def stage1(idx):
            gi, c, h, o, L, main = ctx(idx)
            r = idx % NR
            kt, bk = kTg[gi % 2]
            vt, bv = vTg[gi % 2]
            parts = os.environ.get("S1_PARTS", "12")
            if "1" in parts:
                for j in range(2):
                    fw.op("pe", lambda: nc.tensor.transpose(B0[0:L, j * 128:(j + 1) * 128], kt[:, 2 * h + j, o:o + L], cs.identb[:, :]),
                          reads=[bk, cs.b], writes=[b0], signal=(j == 1))
                fw.op("dve", lambda: V.tensor_scalar(kE[r][0][0:L, :], B0[0:L, 0:256], efc[0:L, c, h:h + 1], None, ALU.mult),
                      reads=[b0, b_efc], writes=[kE[r][1]])
            if "2" in parts:
                for j in range(4):
                    fw.op("pe", lambda: nc.tensor.transpose(B0[0:L, 256 + j * 128:256 + (j + 1) * 128], vt[:, 4 * h + j, o:o + L], cs.identb[:, :]),
                          reads=[bv, cs.b], writes=[b0], signal=(j == 3))
                fw.op("act", lambda: A.copy(vx[r][0][0:L, 0:512], B0[0:L, 256:768]), reads=[b0], writes=[vx[r][1]])
            if main:
                qt, bq = qTg[gi % 2]
                for j in range(2):
                    fw.op("pe", lambda: nc.tensor.matmul(B1[0:L, 0:L], kt[:, 2 * h + j, o:o + L], qt[:, 2 * h + j, o:o + L],
                                                         start=(j == 0), stop=(j == 1)),
                          reads=[bk, bq], writes=[bb1], signal=(j == 1))
                fw.op("dve", lambda: V.scalar_tensor_tensor(Sp[r][0][0:L, 0:L], B1[0:L, 0:L], efc[0:L, c, h:h + 1],
                                                            mask[0:L, 0:L], ALU.mult, ALU.mult),
                      reads=[bb1, b_efc, b_mask], writes=[Sp[r][1]])

        def stage2(idx):
            gi, c, h, o, L, main = ctx(idx)
            r = idx % NR
            Ct, bC = Cst[h]
            Cb, bCb = Cbf[h]
            if main:
                qt, bq = qTg[gi % 2]
                sm_t, b_sm = sml[r]
                for j in range(2):
                    fw.op("pe", lambda: nc.tensor.matmul(B2[0:L, 0:512], qt[:, 2 * h + j, o:o + L], Cb[:, j, 0:512],
                                                         start=(j == 0), stop=False),
                          reads=[bq, bCb], writes=[bb2], signal=False)
                fw.op("pe", lambda: nc.tensor.matmul(B2[0:L, 0:512], Sp[r][0][0:L, 0:L], vx[r][0][0:L, 0:512], start=False, stop=True),
                      reads=[Sp[r][1], vx[r][1]], writes=[bb2])
                for j in range(2):
                    fw.op("pe", lambda: nc.tensor.matmul(B3[0:L, 0:1], qt[:, 2 * h + j, o:o + L], Cb[:, j, 512:513],
                                                         start=(j == 0), stop=False),
                          reads=[bq, bCb], writes=[bb3], signal=False)
                fw.op("pe", lambda: nc.tensor.matmul(B3[0:L, 0:1], Sp[r][0][0:L, 0:L], vx[r][0][0:L, 512:513], start=False, stop=True),
                      reads=[Sp[r][1], vx[r][1]], writes=[bb3])
                fw.op("act", lambda: A.activation(sm_t[0:L, 0:1], B3[0:L, 0:1], AF.Abs), reads=[bb3], writes=[b_sm])
                fw.op("dve", lambda: V.tensor_tensor(sm_t[0:L, 0:1], sm_t[0:L, 0:1], efc[0:L, c, 8 + h:9 + h], ALU.max),
                      reads=[b_sm, b_efc], writes=[b_sm])
                rc_t, b_rc = recl[r]
                sl_t, b_sl = scal[r]
                fw.op("dve", lambda: V.reciprocal(rc_t[0:L, 0:1], sm_t[0:L, 0:1]), reads=[b_sm], writes=[b_rc])
                jk, b_jk = junk[idx % 2]
                fw.op("act", lambda: A.activation(jk[0:L, :], B2[0:L, 0:512], AF.Square, scale=rc_t[0:L, 0:1], accum_out=sm_t[0:L, 2:3]),
                      reads=[bb2, b_rc], writes=[b_jk, b_sm])
                fw.op("act", lambda: A.activation(sm_t[0:L, 3:4], sm_t[0:L, 2:3], AF.Ln, bias=EPS, scale=1.0 / MDV),
                      reads=[b_sm], writes=[b_sm])
                fw.op("act", lambda: A.activation(sm_t[0:L, 3:4], sm_t[0:L, 3:4], AF.Exp, scale=-0.5), reads=[b_sm], writes=[b_sm])
                fw.op("dve", lambda: V.tensor_tensor(sl_t[0:L, 0:1], sm_t[0:L, 3:4], rc_t[0:L, 0:1], ALU.mult), reads=[b_sm, b_rc], writes=[b_sl])
                fw.op("act", lambda: A.activation(hn[r][0][0:L, :], B2[0:L, 0:512], AF.Copy, scale=sl_t[0:L, 0:1]),
                      reads=[bb2, b_sl], writes=[hn[r][1]])
            lsc = lam[:, h, c:c + 1]
            for j, (Bj, bbj) in enumerate(((B5, bb5), (B6, bb6))):
                fw.op("pe", lambda: nc.tensor.matmul(Bj[:, 0:512], kE[r][0][0:L, j * 128:(j + 1) * 128], vx[r][0][0:L, 0:512], start=True, stop=True),
                      reads=[kE[r][1], vx[r][1]], writes=[bbj])
                fw.op("pe", lambda: nc.tensor.matmul(B7[:, j:j + 1], kE[r][0][0:L, j * 128:(j + 1) * 128], vx[r][0][0:L, 512:513], start=True, stop=True),
                      reads=[kE[r][1], vx[r][1]], writes=[bb7])
                cl, b_cl = Clt[j]
                fw.op("pool", lambda: G.tensor_scalar(cl[:, :], Ct[:, j, :], lsc, 0.0, ALU.mult, ALU.add), reads=[bC, b_lam], writes=[b_cl])
                fw.op("dve", lambda: V.scalar_tensor_tensor(Ct[:, j, 0:512], Bj[:, 0:512], lsc, cl[:, 0:512], ALU.mult, ALU.add),
                      reads=[bbj, b_lam, b_cl], writes=[bC])
                fw.op("dve", lambda: V.scalar_tensor_tensor(Ct[:, j, 512:513], B7[:, j:j + 1], lsc, cl[:, 512:513], ALU.mult, ALU.add),
                      reads=[bb7, b_lam, b_cl], writes=[bC])
                fw.op("act", lambda: A.copy(Cb[:, j, :], Ct[:, j, :]), reads=[bC], writes=[bCb])

        def stage3(idx):
            gi, c, h, o, L, main = ctx(idx)
            if not main:
                return
            r = idx % NR
            ot, bo = ogg[gi % 2]
            for j in range(4):
                fw.op("pe", lambda: nc.tensor.transpose(B4[:, j * 128:j * 128 + L], hn[r][0][0:L, j * 128:(j + 1) * 128], cs.identb[0:L, 0:L]),
                      reads=[hn[r][1], cs.b], writes=[bb4], signal=(j == 3))
            for j in range(4):
                fw.op("dve", lambda: V.scalar_tensor_tensor(ot[:, 4 * h + j, o:o + L], B4[:, j * 128:j * 128 + L], gout[:, 4 * h + j:4 * h + j + 1],
                                                            ot[:, 4 * h + j, o:o + L], ALU.mult, ALU.mult),
                      reads=[bb4, b_gout, bo], writes=[bo], signal=(j == 3))
            g0, gn, chs = groups[gi]
            if h == 7 and c == chs[-1]:
                l0 = g0
                fw.dma("pool", P.yT[:, l0:l0 + gn].rearrange("(c p) t -> p c t", p=128), ot[:, :, 0:gn], bo, P.b_yT, "out")

        load_group(0)
        load_group(1)
        import os
        n = len(items)
        if os.environ.get("MLSTM_ITEMS"):
            n = int(os.environ["MLSTM_ITEMS"])
        for it in range(n + 2):
            if it < n:
                stage1(it)
            if 0 <= it - 1 < n and not os.environ.get("MLSTM_SKIP2"):
                stage2(it - 1)
            if 0 <= it - 2 < n:
                stage3(it - 2)
                gi, c, h = items[it - 2]
                if h == 7 and c == groups[gi][2][-1]:
                    load_group(gi + 2)


def w_out_phase(fw, cs, P, sg=None):
    sg = sg or seg_L(P)
    with Scope(fw) as sc:
        ar = Arena(sc, Tn=sg.Tn)
        gemm(fw, ar, P.yT[:, sg.off:sg.off + sg.Tn], P.b_yT, D, P.w_out, P.b_in, cb_range(0, D),
             resid_epilogue(fw, sc, sg.hT, sg.b_hT), tts=sg.tts, Tn=sg.Tn)


class Params:
    pass


def declare(nc, name, shape, dtype, kind):
    return nc.dram_tensor(name, list(shape), dtype, kind=kind).ap()


def build_layer0(nc, P, fw, cs):
    transpose_in(fw, cs, P.xloc, P.b_in, P.hT, P.b_hT, T)
    transpose_in(fw, cs, P.xpre, P.b_in, P.hpT, P.b_hpT, TPRE)
    norm_fm(fw, cs, P.hT, P.b_hT, P.a_norm_g, P.b_in, P.xnT, P.b_xnT, T, D)
    norm_fm(fw, cs, P.hpT, P.b_hpT, P.a_norm_g, P.b_in, P.xnpT, P.b_xnpT, TPRE, D)
    l0_proj(fw, cs, P)
    mlstm(fw, cs, P)
    for sg in (seg_L(P), seg_P(P)):
        w_out_phase(fw, cs, P, sg)
        ffn(fw, cs, P, 0, sg)


INPUTS = {
    "xloc": ([T, D], F32), "xpre": ([TPRE, D], F32),
    "w_in": ([D, 12304], F32), "w_out": ([D, D], F32),
    "w_gu0": ([D, 2 * DFF], F32), "w_gu1": ([D, 2 * DFF], F32),
    "w_dn0": ([DFF, D], F32), "w_dn1": ([DFF, D], F32),
    "kv_w_down": ([D, 576], F32), "kv_w_up": ([512, 16384], F32),
    "w_dq": ([D, 1024], F32), "w_uq": ([1024, 12288], F32), "w_o": ([8192, D], F32),
    "a_norm_g": ([128, 32], F32), "ffn_g0": ([128, 32], F32), "ffn_g1": ([128, 32], F32),
    "kv_norm_g": ([128, 32], F32), "b_norm_g": ([128, 32], F32),
    "kv_lat_g": ([128, 4], F32), "q_lat_g": ([128, 8], F32), "gout_l": ([128, 32], F32),
    "kq_g": ([128, 8], F32),
    "mparams": ([8, 3], F32), "sel8": ([8, 8, 128], F32), "maskT": ([128, 128], BF16),
    "ident_f": ([128, 128], F32), "ident_b": ([128, 128], BF16), "ones_b": ([128, 128], BF16),
    "posf": ([1, TALL], I32), "invf": ([32, 1], F32), "aflag": ([128, 2], F32),
}
SCRATCH = {
    "hT": ([D, T], F32), "hpT": ([D, TPRE], F32), "xnT": ([D, T], BF16), "xnpT": ([D, TPRE], BF16),
    "qT": ([2048, TALL], BF16), "kT": ([2048, TALL], BF16), "vT": ([4096, TALL], BF16), "ogT": ([D, TALL], BF16),
    "graw": ([16, TALL], F32), "yT": ([D, TALL], BF16), "hidT": ([DFF, TPRE], BF16),
}


def make_P(nc, need, outputs=()):
    P = Params()
    P.b_in = Buf("inputs", multi=True)
    P.ffn_g, P.w_gu, P.w_dn = {}, {}, {}
    for name in need:
        shape, dt = INPUTS[name]
        ap = declare(nc, name, shape, dt, "ExternalInput")
        setattr(P, name, ap)
    for l in (0, 1):
        if f"ffn_g{l}" in need:
            P.ffn_g[l] = getattr(P, f"ffn_g{l}")
            P.w_gu[l] = getattr(P, f"w_gu{l}")
            P.w_dn[l] = getattr(P, f"w_dn{l}")
    for name, (shape, dt) in SCRATCH.items():
        kind = "ExternalOutput" if name in outputs else "Internal"
        setattr(P, name, declare(nc, name, shape, dt, kind))
        setattr(P, "b_" + name, Buf(name, multi=True))
    return P


def host_consts():
    import ml_dtypes
    bf = ml_dtypes.bfloat16
    sel8 = np.zeros((8, 8, 128), np.float32)
    for h in range(8):
        sel8[h, h, :] = 1.0
    s = np.arange(128)
    maskT = (s[:, None] <= s[None, :]).astype(np.float32).astype(bf)
    inv = (1.0 / (10000.0 ** (np.arange(0, 64, 2, dtype=np.float32) / 64.0))).astype(np.float32)
    return {
        "sel8": sel8, "maskT": maskT, "ident_f": np.eye(128, dtype=np.float32),
        "ident_b": np.eye(128, dtype=np.float32).astype(bf), "ones_b": np.ones((128, 128), np.float32).astype(bf),
        "invf": inv.reshape(32, 1),
    }


def lay(g, nk):
    return np.ascontiguousarray(np.asarray(g, np.float32).reshape(nk, 128).T)


def host_inputs(inputs, core):
    b, s = core // 2, core % 2
    x = inputs["x"][b]
    meta = inputs["meta_tokens"]
    if s == 0:
        xloc = x[0:2048]
        xpre = np.concatenate([np.zeros((NEUT, D), np.float32), meta], axis=0)
    else:
        xloc = x[2048:4096]
        xpre = np.concatenate([meta, x[0:2048]], axis=0)
    m = {
        "xloc": np.ascontiguousarray(xloc, dtype=np.float32), "xpre": np.ascontiguousarray(xpre, dtype=np.float32),
        "w_in": inputs["a_w_in"][0], "w_out": inputs["a_w_out"][0],
        "w_gu0": inputs["ffn_w_gate_up"][0], "w_gu1": inputs["ffn_w_gate_up"][1],
        "w_dn0": inputs["ffn_w_down"][0], "w_dn1": inputs["ffn_w_down"][1],
        "kv_w_down": inputs["kv_w_down"], "kv_w_up": inputs["kv_w_up"],
        "w_dq": inputs["b_w_dq"][0], "w_uq": inputs["b_w_uq"][0], "w_o": inputs["b_w_o"][0],
        "a_norm_g": lay(inputs["a_norm_g"][0], 32), "ffn_g0": lay(inputs["ffn_norm_g"][0], 32),
        "ffn_g1": lay(inputs["ffn_norm_g"][1], 32), "kv_norm_g": lay(inputs["kv_norm_g"], 32),
        "b_norm_g": lay(inputs["b_norm_g"][0], 32), "kv_lat_g": lay(inputs["kv_latent_norm_g"], 4),
        "q_lat_g": lay(inputs["b_q_latent_norm_g"][0], 8), "gout_l": lay(inputs["a_out_norm_g"][0], 32),
        "mparams": np.stack([np.asarray(inputs["a_b_i"][0], np.float32), np.asarray(inputs["a_b_f"][0], np.float32),
                             np.full(8, float(s), np.float32)], axis=1),
    }
    kq = np.zeros((128, 8), np.float32)
    gk = np.asarray(inputs["k_norm_g"], np.float32)
    gq = np.asarray(inputs["q_norm_g"][0], np.float32)
    kq[:, 0] = gk[0:128]
    kq[0:32, 1] = gk[128:160]
    kq[0:32, 2] = gk[160:192]
    kq[:, 3] = gq[0:128]
    kq[0:32, 4] = gq[128:160]
    kq[0:32, 5] = gq[160:192]
    m["kq_g"] = kq
    pos = np.asarray(inputs["positions"][b], np.int32)
    metap = np.arange(16, dtype=np.int32) - 16
    if s == 0:
        posf = np.concatenate([np.zeros(NEUT, np.int32), metap, pos[0:2048]])
    else:
        posf = np.concatenate([metap, pos[0:2032], pos[2032:4096]])
    m["posf"] = posf.reshape(1, TALL).astype(np.int32)
    af = np.full((128, 2), -10.0, np.float32)
    if s == 0:
        af[:, 1] = -30000.0
    m["aflag"] = af
    m.update(host_consts())
    return m


NH = 64
TT_ALL = TT_P + [(TPRE + i * 512, 512) for i in range(4)]
TWO_PI = 6.283185307179586
C1_2PI = 6.28125
C2_2PI = TWO_PI - C1_2PI

SCRATCH.update({
    "aT": ([576, T], F32), "apT": ([576, TPRE], F32), "ckvnT": ([512, TALL], BF16),
    "kTh": ([NH * 192, TALL], BF16), "vtok": ([TALL, NH * 128], BF16),
    "cqT": ([1024, T], F32), "cqnT": ([1024, T], BF16), "qTh": ([NH * 192, T], BF16),
    "oT": ([NH * 128, T], BF16),
})


def kv_down(fw, cs, P):
    for sg, dst, b_dst in ((seg_L(P), P.aT, P.b_aT), (seg_P(P), P.apT, P.b_apT)):
        norm_fm(fw, cs, sg.hT, sg.b_hT, P.kv_norm_g, P.b_in, sg.xnT, sg.b_xnT, sg.Tn, D)
        with Scope(fw) as sc:
            ar = Arena(sc, Tn=sg.Tn)
            routes = [(0, 576, dst, 0, 0, None, 1.0, F32, b_dst)]
            gemm(fw, ar, sg.xnT, sg.b_xnT, D, P.kv_w_down, P.b_in, cb_range(0, 512) + [[(512, 32), (544, 32)]],
                 store_epilogue(fw, sc, routes), tts=sg.tts, Tn=sg.Tn)


def rope_tables(fw, P, cosT, sinT, b_tab, sc):
    nc = fw.nc
    V = nc.vector
    A = nc.scalar
    pi_, b_pi = sc.sb([32, TALL], I32, "rp_i")
    ang, b_ang = sc.sb([32, TALL], F32, "rp_a")
    kk, b_kk = sc.sb([32, TALL], F32, "rp_k")
    inv, b_inv = sc.sb([32, 1], F32, "rp_inv")
    fw.dma("sp", pi_[:, :], P.posf.partition_broadcast(32), P.b_in, b_pi, "in")
    fw.dma("sp", inv[:, :], P.invf, P.b_in, b_inv, "in")
    fw.op("dve", lambda: V.tensor_copy(ang[:, :], pi_[:, :]), reads=[b_pi], writes=[b_ang])
    fw.op("dve", lambda: V.tensor_scalar(ang[:, :], ang[:, :], 16.0, inv[:, 0:1], ALU.add, ALU.mult),
          reads=[b_ang, b_inv], writes=[b_ang])
    fw.op("dve", lambda: V.tensor_scalar(kk[:, :], ang[:, :], 1.0 / TWO_PI, 12582912.0, ALU.mult, ALU.add),
          reads=[b_ang], writes=[b_kk])
    fw.op("dve", lambda: V.tensor_scalar(kk[:, :], kk[:, :], 12582912.0, None, ALU.subtract), reads=[b_kk], writes=[b_kk])
    fw.op("dve", lambda: V.scalar_tensor_tensor(ang[:, :], kk[:, :], -C1_2PI, ang[:, :], ALU.mult, ALU.add),
          reads=[b_kk, b_ang], writes=[b_ang])
    fw.op("dve", lambda: V.scalar_tensor_tensor(ang[:, :], kk[:, :], -C2_2PI, ang[:, :], ALU.mult, ALU.add),
          reads=[b_kk, b_ang], writes=[b_ang])
    fw.op("dve", lambda: V.tensor_scalar(ang[:, :], ang[:, :], 3.1415925, -3.1415925, ALU.min, ALU.max),
          reads=[b_ang], writes=[b_ang])
    fw.op("act", lambda: A.activation(sinT[:, :], ang[:, :], AF.Sin), reads=[b_ang], writes=[b_tab])
    fw.op("act", lambda: A.activation(kk[:, :], ang[:, :], AF.Abs), reads=[b_ang], writes=[b_kk])
    fw.op("act", lambda: A.activation(cosT[:, :], kk[:, :], AF.Sin, bias=1.5707963, scale=-1.0), reads=[b_kk], writes=[b_tab])


def a_src(P, t0):
    return (P.apT, P.b_apT, t0) if t0 < TPRE else (P.aT, P.b_aT, t0 - TPRE)


def kv_k(fw, cs, P):
    nc = fw.nc
    V, A, G = nc.vector, nc.scalar, nc.gpsimd
    norm_fm(fw, cs, P.apT[0:512, :], P.b_apT, P.kv_lat_g, P.b_in, P.ckvnT[:, 0:TPRE], P.b_ckvnT, TPRE, 512)
    norm_fm(fw, cs, P.aT[0:512, :], P.b_aT, P.kv_lat_g, P.b_in, P.ckvnT[:, TPRE:TALL], P.b_ckvnT, T, 512)
    with Scope(fw) as sc:
        cosT, _ = sc.sb([32, TALL], F32, "kk_cos")
        sinT, _ = sc.sb([32, TALL], F32, "kk_sin")
        b_tab = sc.track(Buf("kk_tab", multi=True))
        kr1, b_kr1 = sc.sb([32, TALL], F32, "kk_kr1", multi=True)
        kr2, b_kr2 = sc.sb([32, TALL], F32, "kk_kr2", multi=True)
        ssr, b_ssr = sc.sb([128, TALL], F32, "kk_ssr", multi=True)
        gk, b_gk = sc.sb([128, 8], F32, "kk_g")
        fw.dma("sp", gk[:, :], P.kq_g, P.b_in, b_gk, "in")
        with Scope(fw) as s2:
            rope_tables(fw, P, cosT, sinT, b_tab, s2)
        with Scope(fw) as s3:
            t1s = [s3.sb([32, 512], F32, f"kk_t1{i}") for i in range(2)]
            t2s = [s3.sb([32, 512], F32, f"kk_t2{i}") for i in range(2)]
            sqs = [s3.sb([32, 1024], BF16, f"kk_sq{i}") for i in range(2)]
            m1s = [s3.sb([32, 512], F32, f"kk_m1{i}") for i in range(2)]
            m2s = [s3.sb([32, 512], F32, f"kk_m2{i}") for i in range(2)]
            pss = [s3.ps(name=f"kk_ps{i}") for i in range(2)]
            for i, (t0, tn) in enumerate(TT_ALL):
                src, b_src, l0 = a_src(P, t0)
                (t1, b1), (t2, b2), (sq, bsq), (m1, bm1), (m2, bm2), (ps, bps) = \
                    t1s[i % 2], t2s[i % 2], sqs[i % 2], m1s[i % 2], m2s[i % 2], pss[i % 2]
                fw.dma("sp", t1[:, 0:tn], src[512:544, l0:l0 + tn], b_src, b1, "in")
                fw.dma("sp", t2[:, 0:tn], src[544:576, l0:l0 + tn], b_src, b2, "in")
                fw.op("act", lambda: A.activation(sq[:, 0:tn], t1[:, 0:tn], AF.Square), reads=[b1], writes=[bsq])
                fw.op("act", lambda: A.activation(sq[:, 512:512 + tn], t2[:, 0:tn], AF.Square), reads=[b2], writes=[bsq])
                fw.op("pe", lambda: nc.tensor.matmul(ps[:, 0:tn], cs.onesb[0:32, :], sq[:, 0:tn], start=True, stop=False),
                      reads=[bsq, cs.b], writes=[bps], signal=False)
                fw.op("pe", lambda: nc.tensor.matmul(ps[:, 0:tn], cs.onesb[0:32, :], sq[:, 512:512 + tn], start=False, stop=True),
                      reads=[bsq, cs.b], writes=[bps])
                fw.op("act", lambda: A.copy(ssr[:, t0:t0 + tn], ps[:, 0:tn]), reads=[bps], writes=[b_ssr])
                fw.op("dve", lambda: V.tensor_scalar(t1[:, 0:tn], t1[:, 0:tn], gk[0:32, 1:2], None, ALU.mult), reads=[b1, b_gk], writes=[b1])
                fw.op("dve", lambda: V.tensor_scalar(t2[:, 0:tn], t2[:, 0:tn], gk[0:32, 2:3], None, ALU.mult), reads=[b2, b_gk], writes=[b2])
                rope_pair(fw, t1[:, 0:tn], b1, t2[:, 0:tn], b2, cosT[:, t0:t0 + tn], sinT[:, t0:t0 + tn], b_tab,
                          m1[:, 0:tn], bm1, m2[:, 0:tn], bm2, kr1[:, t0:t0 + tn], b_kr1, kr2[:, t0:t0 + tn], b_kr2)
        with Scope(fw) as s4:
            ar = Arena(s4, Tn=TALL, kmax=4, nps=6)
            sqb = [s4.sb([128, 512], BF16, f"ke_sq{i}") for i in range(2)]
            rsb = [s4.sb([128, 512], F32, f"ke_rs{i}") for i in range(2)]
            knb = [s4.sb([128, 512], BF16, f"ke_kn{i}") for i in range(3)]
            k1b = [s4.sb([32, 1024], BF16, f"ke_k1{i}") for i in range(3)]
            pss = [s4.ps(name=f"ke_ps{i}") for i in range(2)]
            cnt = [0]

            def k_epi(sbi, t0, tn, blocks, kp, nkp):
                for (c0, ncol, ps, b_ps) in blocks:
                    h = c0 // 256
                    i = cnt[0]
                    cnt[0] += 1
                    (sq, bsq), (rs, brs), (kn, bkn), (k1, bk1), (p2, bp2) = sqb[i % 2], rsb[i % 2], knb[i % 3], k1b[i % 3], pss[i % 2]
                    fw.op("act", lambda: A.activation(sq[:, 0:tn], ps[:, 0:tn], AF.Square), reads=[b_ps], writes=[bsq])
                    fw.op("pe", lambda: nc.tensor.matmul(p2[:, 0:tn], cs.onesb[:, :], sq[:, 0:tn], start=True, stop=True),
                          reads=[bsq, cs.b], writes=[bp2])
                    fw.op("dve", lambda: V.tensor_tensor(rs[:, 0:tn], p2[:, 0:tn], ssr[:, t0:t0 + tn], ALU.add),
                          reads=[bp2, b_ssr], writes=[brs])
                    rstd_from_sumsq(fw, rs[:, 0:tn], brs, rs[:, 0:tn], brs, 1.0 / 192.0)
                    fw.op("dve", lambda: V.scalar_tensor_tensor(kn[:, 0:tn], ps[:, 0:tn], gk[:, 0:1], rs[:, 0:tn], ALU.mult, ALU.mult),
                          reads=[b_ps, b_gk, brs], writes=[bkn])
                    fw.op("pool", lambda: G.tensor_tensor(k1[:, 0:tn], kr1[:, t0:t0 + tn], rs[0:32, 0:tn], ALU.mult),
                          reads=[b_kr1, brs], writes=[bk1])
                    fw.op("pool", lambda: G.tensor_tensor(k1[:, 512:512 + tn], kr2[:, t0:t0 + tn], rs[0:32, 0:tn], ALU.mult),
                          reads=[b_kr2, brs], writes=[bk1])
                    r0 = h * 192
                    fw.dma("pool", P.kTh[r0:r0 + 128, t0:t0 + tn], kn[:, 0:tn], bkn, P.b_kTh, "out")
                    fw.dma("pool", P.kTh[r0 + 128:r0 + 160, t0:t0 + tn], k1[:, 0:tn], bk1, P.b_kTh, "out")
                    fw.dma("pool", P.kTh[r0 + 160:r0 + 192, t0:t0 + tn], k1[:, 512:512 + tn], bk1, P.b_kTh, "out")

            sbs = [[(h * 256, 128), ((h + 1) * 256, 128)] for h in range(0, NH, 2)]
            gemm(fw, ar, P.ckvnT, P.b_ckvnT, 512, P.kv_w_up, P.b_in, sbs, k_epi, tts=TT_ALL, Tn=TALL)


def rope_pair(fw, u1, b1, u2, b2, cos, sin, b_tab, m1, bm1, m2, bm2, o1, bo1, o2, bo2):
    nc = fw.nc
    V, G = nc.vector, nc.gpsimd
    fw.op("pool", lambda: G.tensor_tensor(m1, u1, cos, ALU.mult), reads=[b1, b_tab], writes=[bm1])
    fw.op("pool", lambda: G.tensor_tensor(m2, u2, sin, ALU.mult), reads=[b2, b_tab], writes=[bm2])
    fw.op("dve", lambda: V.tensor_tensor(o1, m1, m2, ALU.subtract), reads=[bm1, bm2], writes=[bo1])
    fw.op("pool", lambda: G.tensor_tensor(m1, u1, sin, ALU.mult), reads=[b1, b_tab], writes=[bm1])
    fw.op("pool", lambda: G.tensor_tensor(m2, u2, cos, ALU.mult), reads=[b2, b_tab], writes=[bm2])
    fw.op("dve", lambda: V.tensor_tensor(o2, m1, m2, ALU.add), reads=[bm1, bm2], writes=[bo2])


def kv_v(fw, cs, P):
    nc = fw.nc
    V, A, G = nc.vector, nc.scalar, nc.gpsimd
    wv = P.kv_w_up.rearrange("k (h two c) -> k h two c", two=2, c=128)
    with Scope(fw) as sc:
        act, b_act = sc.sb([128, 4, TALL], BF16, "vv_act")
        fw.dma("sp", act[:, :, :], P.ckvnT.rearrange("(c p) t -> p c t", p=128), P.b_ckvnT, b_act, "in")
        wss = [sc.sb([128, 4, 4, 128], F32, f"vv_ws{i}") for i in range(2)]
        wbs = [sc.sb([128, 4, 512], BF16, f"vv_wb{i}") for i in range(2)]
        evs = [sc.sb([128, 512], BF16, f"vv_ev{i}") for i in range(3)]
        pss = [sc.ps(name=f"vv_ps{i}") for i in range(4)]
        n = 0
        for hg in range(NH // 4):
            (ws, bws), (wb, bwb) = wss[hg % 2], wbs[hg % 2]
            for c in range(4):
                fw.dma("sp", ws[:, c, :, :], wv[c * 128:(c + 1) * 128, hg * 4:hg * 4 + 4, 1, :], P.b_in, bws, "in")
            fw.op("pool", lambda: G.tensor_copy(wb[:, :, :], ws[:, :, :, :].rearrange("p c h d -> p c (h d)")),
                  reads=[bws], writes=[bwb])
            for (t0, tn) in tok_blocks(TALL, 128):
                ps, bps = pss[n % 4]
                ev, bev = evs[n % 3]
                n += 1
                for c in range(4):
                    fw.op("pe", lambda: nc.tensor.matmul(ps[0:tn, :], act[:, c, t0:t0 + tn], wb[:, c, :], start=(c == 0), stop=(c == 3)),
                          reads=[b_act, bwb], writes=[bps], signal=(c == 3))
                evac_copy(fw, ev[0:tn, :], bev, ps[0:tn, :], bps, n)
                fw.dma("pool", P.vtok[t0:t0 + tn, hg * 512:(hg + 1) * 512], ev[0:tn, :], bev, P.b_vtok, "out")


def q_proj(fw, cs, P):
    nc = fw.nc
    V, A, G = nc.vector, nc.scalar, nc.gpsimd
    norm_fm(fw, cs, P.hT, P.b_hT, P.b_norm_g, P.b_in, P.xnT, P.b_xnT, T, D)
    with Scope(fw) as sc:
        ar = Arena(sc)
        routes = [(0, 1024, P.cqT, 0, 0, None, 1.0, F32, P.b_cqT)]
        gemm(fw, ar, P.xnT, P.b_xnT, D, P.w_dq, P.b_in, cb_range(0, 1024), store_epilogue(fw, sc, routes))
    norm_fm(fw, cs, P.cqT, P.b_cqT, P.q_lat_g, P.b_in, P.cqnT, P.b_cqnT, T, 1024)
    with Scope(fw) as sc:
        cosT, _ = sc.sb([32, TALL], F32, "qq_cos")
        sinT, _ = sc.sb([32, TALL], F32, "qq_sin")
        b_tab = sc.track(Buf("qq_tab", multi=True))
        gq, b_gq = sc.sb([128, 8], F32, "qq_g")
        fw.dma("sp", gq[:, :], P.kq_g, P.b_in, b_gq, "in")
        with Scope(fw) as s2:
            rope_tables(fw, P, cosT, sinT, b_tab, s2)
        with Scope(fw) as s4:
            ar = Arena(s4, Tn=T, kmax=8, nps=6)
            sqb = [s4.sb([128, 512], BF16, f"qe_sq{i}") for i in range(2)]
            sqa = [s4.sb([32, 1024], BF16, f"qe_sa{i}") for i in range(2)]
            rsb = [s4.sb([128, 512], F32, f"qe_rs{i}") for i in range(2)]
            qnb = [s4.sb([128, 512], BF16, f"qe_qn{i}") for i in range(3)]
            uab = [s4.sb([32, 1024], F32, f"qe_u{i}") for i in range(2)]
            m1b = [s4.sb([32, 512], F32, f"qe_m1{i}") for i in range(2)]
            m2b = [s4.sb([32, 512], F32, f"qe_m2{i}") for i in range(2)]
            o1b = [s4.sb([32, 1024], BF16, f"qe_o{i}", multi=True) for i in range(3)]
            pss = [s4.ps(name=f"qe_ps{i}") for i in range(2)]
            cnt = [0]

            def q_epi(sbi, t0, tn, blocks, kp, nkp):
                (cn, _, psn, bpn), (ca, _, psa, bpa), (cb_, _, psb, bpb) = blocks
                h = cn // 192
                i = cnt[0]
                cnt[0] += 1
                (sq, bsq), (sa, bsa), (rs, brs), (qn, bqn), (ua, bua) = sqb[i % 2], sqa[i % 2], rsb[i % 2], qnb[i % 3], uab[i % 2]
                (m1, bm1), (m2, bm2), (o1, bo1), (p2, bp2) = m1b[i % 2], m2b[i % 2], o1b[i % 3], pss[i % 2]
                fw.op("act", lambda: A.activation(sq[:, 0:tn], psn[:, 0:tn], AF.Square), reads=[bpn], writes=[bsq])
                fw.op("act", lambda: A.activation(sa[:, 0:tn], psa[0:32, 0:tn], AF.Square), reads=[bpa], writes=[bsa])
                fw.op("act", lambda: A.activation(sa[:, 512:512 + tn], psb[0:32, 0:tn], AF.Square), reads=[bpb], writes=[bsa])
                fw.op("pe", lambda: nc.tensor.matmul(p2[:, 0:tn], cs.onesb[:, :], sq[:, 0:tn], start=True, stop=False),
                      reads=[bsq, cs.b], writes=[bp2], signal=False)
                fw.op("pe", lambda: nc.tensor.matmul(p2[:, 0:tn], cs.onesb[0:32, :], sa[:, 0:tn], start=False, stop=False),
                      reads=[bsa, cs.b], writes=[bp2], signal=False)
                fw.op("pe", lambda: nc.tensor.matmul(p2[:, 0:tn], cs.onesb[0:32, :], sa[:, 512:512 + tn], start=False, stop=True),
                      reads=[bsa, cs.b], writes=[bp2])
                rstd_from_sumsq(fw, p2[:, 0:tn], bp2, rs[:, 0:tn], brs, 1.0 / 192.0)
                fw.op("dve", lambda: V.scalar_tensor_tensor(qn[:, 0:tn], psn[:, 0:tn], gq[:, 3:4], rs[:, 0:tn], ALU.mult, ALU.mult),
                      reads=[bpn, b_gq, brs], writes=[bqn])
                fw.op("dve", lambda: V.scalar_tensor_tensor(ua[:, 0:tn], psa[0:32, 0:tn], gq[0:32, 4:5], rs[0:32, 0:tn], ALU.mult, ALU.mult),
                      reads=[bpa, b_gq, brs], writes=[bua])
                fw.op("dve", lambda: V.scalar_tensor_tensor(ua[:, 512:512 + tn], psb[0:32, 0:tn], gq[0:32, 5:6], rs[0:32, 0:tn], ALU.mult, ALU.mult),
                      reads=[bpb, b_gq, brs], writes=[bua])
                tt0 = TPRE + t0
                rope_pair(fw, ua[:, 0:tn], bua, ua[:, 512:512 + tn], bua, cosT[:, tt0:tt0 + tn], sinT[:, tt0:tt0 + tn], b_tab,
                          m1[:, 0:tn], bm1, m2[:, 0:tn], bm2, o1[:, 0:tn], bo1, o1[:, 512:512 + tn], bo1)
                r0 = h * 192
                fw.dma("pool", P.qTh[r0:r0 + 128, t0:t0 + tn], qn[:, 0:tn], bqn, P.b_qTh, "out")
                fw.dma("pool", P.qTh[r0 + 128:r0 + 160, t0:t0 + tn], o1[:, 0:tn], bo1, P.b_qTh, "out")
                fw.dma("pool", P.qTh[r0 + 160:r0 + 192, t0:t0 + tn], o1[:, 512:512 + tn], bo1, P.b_qTh, "out")

            sbs = [[(h * 192, 128), (h * 192 + 128, 32), (h * 192 + 160, 32)] for h in range(NH)]
            gemm(fw, ar, P.cqnT, P.b_cqnT, 1024, P.w_uq, P.b_in, sbs, q_epi, kpass=8)


KB = [(128 * i, 128) for i in range(16)] + [(NEUT, 16)] + [(TPRE + 128 * i, 128) for i in range(16)]
ATT_SCALE = 192.0 ** -0.5


def attention(fw, cs, P):
    nc = fw.nc
    V, A, G = nc.vector, nc.scalar, nc.gpsimd
    with Scope(fw) as sc:
        tri, b_tri = sc.sb([128, 128], BF16, "at_tri")
        bias, b_bias = sc.sb([128, 2], F32, "at_bias")
        fw.dma("sp", tri[:, :], P.maskT, P.b_in, b_tri, "in")
        fw.dma("sp", bias[:, :], P.aflag, P.b_in, b_bias, "in")
        qnb = [sc.sb([128, T], BF16, f"at_qn{i}") for i in range(2)]
        qrb = [sc.sb([64, T], BF16, f"at_qr{i}") for i in range(2)]
        knb = [sc.sb([128, TALL], BF16, f"at_kn{i}") for i in range(2)]
        krb = [sc.sb([64, TALL], BF16, f"at_kr{i}") for i in range(2)]
        vgb = [sc.sb([128, 33, 512], BF16, f"at_vg{i}", multi=True) for i in range(2)]
        ptb = [sc.sb([128, 512], BF16, f"at_pt{i}") for i in range(4)]
        rcb = [sc.sb([128, 512], F32, f"at_rc{i}") for i in range(2)]
        accb = [sc.sb([128, 512], F32, f"at_acc{i}") for i in range(2)]
        onesf, b_onesf = sc.sb([128, 128], F32, "at_onesf")
        fw.op("pool", lambda: G.memset(onesf[:, :], 1.0), writes=[b_onesf])
        obb = [sc.sb([128, 512], BF16, f"at_ob{i}") for i in range(2)]
        Sb = [sc.ps(name=f"at_S{i}") for i in range(2)]
        OTb = [sc.ps(name=f"at_O{i}") for i in range(2)]
        SMb = [sc.ps(name=f"at_M{i}") for i in range(2)]

        def load_head(h):
            if h >= NH:
                return
            r0 = h * 192
            fw.dma("sp", qnb[h % 2][0][:, :], P.qTh[r0:r0 + 128, :], P.b_qTh, qnb[h % 2][1], "in")
            fw.dma("sp", qrb[h % 2][0][:, :], P.qTh[r0 + 128:r0 + 192, :], P.b_qTh, qrb[h % 2][1], "in")
            fw.dma("sp", knb[h % 2][0][:, :], P.kTh[r0:r0 + 128, :], P.b_kTh, knb[h % 2][1], "in")
            fw.dma("sp", krb[h % 2][0][:, :], P.kTh[r0 + 128:r0 + 192, :], P.b_kTh, krb[h % 2][1], "in")
            if h % 4 == 0:
                g = h // 4
                vt, bvt = vgb[g % 2]
                fw.dma("sp", vt[:, 0:16, :], P.vtok[0:NEUT, g * 512:(g + 1) * 512].rearrange("(b p) c -> p b c", p=128),
                       P.b_vtok, bvt, "in")
                fw.dma("sp", vt[0:16, 16, :], P.vtok[NEUT:TPRE, g * 512:(g + 1) * 512], P.b_vtok, bvt, "in")
                fw.dma("sp", vt[:, 17:33, :], P.vtok[TPRE:TALL, g * 512:(g + 1) * 512].rearrange("(b p) c -> p b c", p=128),
                       P.b_vtok, bvt, "in")

        units = []
        ti = 0
        for h in range(NH):
            for (t0, tn) in TT:
                us = []
                for kb, (k0, kn_) in enumerate(KB):
                    if k0 < TPRE:
                        us.append([h, ti, t0, tn, kb, k0, kn_, 0, False, k0 < NEUT])
                    else:
                        lk0 = k0 - TPRE
                        if lk0 > t0 + tn - 1:
                            continue
                        if lk0 + kn_ - 1 <= t0:
                            us.append([h, ti, t0, tn, kb, k0, kn_, 0, False, False])
                        else:
                            us.append([h, ti, t0, tn, kb, k0, kn_, lk0 - t0, True, False])
                for i, u in enumerate(us):
                    u.append(i == 0)
                    u.append(i == len(us) - 1)
                units += us
                ti += 1

        def emit_S(ui):
            h, ti, t0, tn, kb, k0, kn_, c0, diag, pre, first, last = units[ui]
            S, bS = Sb[ui % 2]
            pt, bpt = ptb[ui % 4]
            ncol = tn - c0
            (qn, bqn), (qr, bqr), (kn, bkn), (kr, bkr) = qnb[h % 2], qrb[h % 2], knb[h % 2], krb[h % 2]
            fw.op("pe", lambda: nc.tensor.matmul(S[0:kn_, 0:ncol], kn[:, k0:k0 + kn_], qn[:, t0 + c0:t0 + tn], start=True, stop=False),
                  reads=[bkn, bqn], writes=[bS], signal=False)
            fw.op("pe", lambda: nc.tensor.matmul(S[0:kn_, 0:ncol], kr[:, k0:k0 + kn_], qr[:, t0 + c0:t0 + tn], start=False, stop=True),
                  reads=[bkr, bqr], writes=[bS])
            bcol = bias[0:kn_, 1:2] if pre else bias[0:kn_, 0:1]
            fw.op("act", lambda: A.activation(pt[0:kn_, 0:ncol], S[0:kn_, 0:ncol], AF.Exp, bias=bcol, scale=ATT_SCALE),
                  reads=[bS, b_bias], writes=[bpt])
            if diag:
                fw.op("pool", lambda: G.tensor_tensor(pt[0:kn_, 0:kn_], pt[0:kn_, 0:kn_], tri[0:kn_, 0:kn_], ALU.mult),
                      reads=[bpt, b_tri], writes=[bpt])

        def emit_PV(ui):
            h, ti, t0, tn, kb, k0, kn_, c0, diag, pre, first, last = units[ui]
            pt, bpt = ptb[ui % 4]
            OT, bOT = OTb[ti % 2]
            SM, bSM = SMb[ti % 2]
            acc, bacc = accb[ti % 2]
            vt, bvt = vgb[(h // 4) % 2]
            hh = h % 4
            fw.op("pe", lambda: nc.tensor.matmul(OT[:, c0:tn], vt[0:kn_, kb, hh * 128:(hh + 1) * 128], pt[0:kn_, 0:tn - c0],
                                                 start=first, stop=last),
                  reads=[bvt, bpt], writes=[bOT], signal=False)
            fw.op("pe", lambda: nc.tensor.matmul(SM[:, c0:tn], cs.onesb[0:kn_, :], pt[0:kn_, 0:tn - c0], start=first, stop=last),
                  reads=[cs.b, bpt], writes=[bSM])
            if last:
                rc, brc = rcb[ti % 2]
                ob, bob = obb[ti % 2]
                fw.op("dve", lambda: V.reciprocal(rc[:, 0:tn], SM[:, 0:tn]), reads=[bSM], writes=[brc])
                fw.op("dve", lambda: V.tensor_tensor(ob[:, 0:tn], OT[:, 0:tn], rc[:, 0:tn], ALU.mult), reads=[bOT, brc], writes=[bob])
                fw.dma("pool", P.oT[h * 128:(h + 1) * 128, t0:t0 + tn], ob[:, 0:tn], bob, P.b_oT, "out")

        load_head(0)
        n = len(units)
        import os
        if os.environ.get("ATT_UNITS"):
            n = int(os.environ["ATT_UNITS"])
        for ui in range(n + 1):
            if ui < n:
                emit_S(ui)
            if ui >= 1:
                emit_PV(ui - 1)
            if ui < n and (ui == 0 or units[ui][0] != units[ui - 1][0]):
                load_head(units[ui][0] + 1)


def o_proj(fw, cs, P):
    with Scope(fw) as sc:
        ar = Arena(sc)
        gemm(fw, ar, P.oT, P.b_oT, NH * 128, P.w_o, P.b_in, cb_range(0, D), resid_epilogue(fw, sc, P.hT, P.b_hT))


def transpose_out(fw, cs, P, out, b_out):
    nc = fw.nc
    with Scope(fw) as sc:
        hin = [sc.sb([128, 32, 512], F32, f"to_h{i}") for i in range(2)]
        xo = [sc.sb([128, D], F32, f"to_x{i}") for i in range(2)]
        pss = [sc.ps(name=f"to_ps{i}") for i in range(4)]
        ips = 0
        nb = 0
        for gi in range(4):
            g0 = gi * 512
            h_t, h_b = hin[gi % 2]
            fw.dma("sp", h_t[:, :, :], P.hT[:, g0:g0 + 512].rearrange("(c p) t -> p c t", p=128), P.b_hT, h_b, "in")
            for bi in range(4):
                x_t, x_b = xo[nb % 2]
                nb += 1
                for c4 in range(0, 32, 4):
                    ps_t, ps_b = pss[ips % 4]
                    ips += 1
                    for j in range(4):
                        c = c4 + j
                        fw.op("pe", lambda: nc.tensor.transpose(ps_t[:, j * 128:(j + 1) * 128], h_t[:, c, bi * 128:(bi + 1) * 128],
                                                                cs.ident[:, :]),
                              reads=[h_b, cs.b], writes=[ps_b], signal=(j == 3))
                    if (c4 // 4) % 2 == 0:
                        fw.op("dve", lambda: nc.vector.tensor_copy(x_t[:, c4 * 128:(c4 + 4) * 128], ps_t[:, :]), reads=[ps_b], writes=[x_b])
                    else:
                        fw.op("act", lambda: nc.scalar.copy(x_t[:, c4 * 128:(c4 + 4) * 128], ps_t[:, :]), reads=[ps_b], writes=[x_b])
                r0 = gi * 512 + bi * 128
                fw.dma("pool", out[r0:r0 + 128, :], x_t[:, :], x_b, b_out, "out")


INPUTS.update({"hT_in": ([D, T], F32), "aT_in": ([576, T], F32), "apT_in": ([576, TPRE], F32)})
SCRATCH.update({"out": ([T, D], F32)})


def copy_dram(fw, dst, b_dst, src, b_src, rows, tmpname="cp"):
    b_tmp = Buf(tmpname)
    step = 512
    for r0 in range(0, rows, step):
        r1 = min(rows, r0 + step)
        fw.dma("sp", dst[r0:r1, :], src[r0:r1, :], b_src, b_dst, "in")


def build_layer1(fw, cs, P, out, b_out):
    kv_k(fw, cs, P)
    kv_v(fw, cs, P)
    q_proj(fw, cs, P)
    attention(fw, cs, P)
    o_proj(fw, cs, P)
    ffn(fw, cs, P, 1)
    transpose_out(fw, cs, P, out, b_out)


NEED_A = ["xloc", "xpre", "w_in", "w_out", "a_norm_g", "gout_l", "mparams", "sel8", "maskT", "ident_f", "ident_b", "ones_b",
          "w_gu0", "w_dn0", "ffn_g0", "kv_norm_g", "kv_w_down"]
NEED_B = ["hT_in", "aT_in", "apT_in", "kv_w_up", "kv_lat_g", "kq_g", "posf", "invf", "aflag", "maskT", "ident_f", "ident_b", "ones_b",
          "b_norm_g", "w_dq", "q_lat_g", "w_uq", "w_o", "ffn_g1", "w_gu1", "w_dn1"]


def build_A():
    nc = bass.Bass("TRN2", target_bir_lowering=False)
    P = make_P(nc, NEED_A, ["hT", "aT"])
    fw = FW(nc)
    cs = Consts(fw, P.ident_f, P.ident_b, P.ones_b)
    build_layer0(nc, P, fw, cs)
    kv_down(fw, cs, P)
    fw.drain([P.b_hT, P.b_aT])
    return nc


def build_B():
    nc = bass.Bass("TRN2", target_bir_lowering=False)
    P = make_P(nc, NEED_B, ["out"])
    fw = FW(nc)
    cs = Consts(fw, P.ident_f, P.ident_b, P.ones_b)
    copy_dram(fw, P.hT, P.b_hT, P.hT_in, P.b_in, D)
    P.aT, P.b_aT = P.aT_in, P.b_in
    P.apT, P.b_apT = P.apT_in, P.b_in
    build_layer1(fw, cs, P, P.out, P.b_out)
    fw.drain([P.b_out])
    return nc


def kernel_2launch(**inputs):
    inputs = {k: np.asarray(v) for k, v in inputs.items()}
    maps = [host_inputs(inputs, c) for c in range(NCORES)]
    ncA = build_A()
    resA = run_bass_kernel_spmd(ncA, [{k: m[k] for k in NEED_A} for m in maps], core_ids=list(range(NCORES)))
    hTs = [np.asarray(r["hT"]) for r in resA.results]
    aTs = [np.asarray(r["aT"]) for r in resA.results]
    del resA
    for c in range(NCORES):
        maps[c]["hT_in"] = hTs[c]
        maps[c]["aT_in"] = aTs[c]
        if c % 2 == 1:
            maps[c]["apT_in"] = np.ascontiguousarray(aTs[c - 1][:, 0:TPRE])
        else:
            maps[c]["apT_in"] = np.zeros((576, TPRE), np.float32)
    ncB = build_B()
    resB = run_bass_kernel_spmd(ncB, [{k: m[k] for k in NEED_B} for m in maps], core_ids=list(range(NCORES)))
    out = np.empty((4, 4096, D), np.float32)
    for c in range(NCORES):
        b, s = c // 2, c % 2
        out[b, s * 2048:(s + 1) * 2048] = np.asarray(resB.results[c]["out"])
    return out


SCRATCH_UNUSED = {"agT": ([NCORES * 576, T], F32)}
INPUTS.update({"selw": ([128, NCORES], F32)})


def exchange_latent(fw, cs, P):
    nc = fw.nc
    V = nc.vector
    sem = fw.get_sem()
    fw._deps("pool", [P.b_aT], [P.b_agT])
    inst = nc.gpsimd.collective_compute("AllGather", mybir.AluOpType.bypass,
                                        replica_groups=[list(range(NCORES))],
                                        ins=[P.aT[:, :]], outs=[P.agT[:, :]])
    sem.n += 16
    inst.then_inc(sem.h, 16)
    P.b_agT.w[sem] = sem.n
    P.b_aT.r[sem] = sem.n
    with Scope(fw) as sc:
        sw, b_sw = sc.sb([128, NCORES], F32, "ex_w")
        fw.dma("sp", sw[:, :], P.selw, P.b_in, b_sw, "in")
        ins_ = [sc.sb([128, 512], F32, f"ex_i{i}") for i in range(4)]
        accs = [sc.sb([128, 512], F32, f"ex_a{i}") for i in range(2)]
        n = 0
        k = 0
        for (r0, rn) in [(0, 128), (128, 128), (256, 128), (384, 128), (512, 64)]:
            for t0 in range(0, TPRE, 512):
                acc, b_acc = accs[n % 2]
                n += 1
                for r in range(NCORES):
                    it, b_it = ins_[k % 4]
                    k += 1
                    fw.dma("sp", it[0:rn, :], P.agT[r * 576 + r0:r * 576 + r0 + rn, t0:t0 + 512], P.b_agT, b_it, "in")
                    if r == 0:
                        fw.op("dve", lambda: V.tensor_scalar(acc[0:rn, :], it[0:rn, :], sw[0:rn, 0:1], None, ALU.mult),
                              reads=[b_it, b_sw], writes=[b_acc])
                    else:
                        fw.op("dve", lambda: V.scalar_tensor_tensor(acc[0:rn, :], it[0:rn, :], sw[0:rn, r:r + 1], acc[0:rn, :], ALU.mult, ALU.add),
                              reads=[b_it, b_sw, b_acc], writes=[b_acc])
                fw.dma("pool", P.apT[r0:r0 + rn, t0:t0 + 512], acc[0:rn, :], b_acc, P.b_apT, "out")
    fw.put_sems([])
    fw.sem_pool.append(sem)


NEED_F = sorted(set(NEED_A + [k for k in NEED_B if k not in ("hT_in", "aT_in", "apT_in")]))


def build_fused():
    nc = bass.Bass("TRN2", target_bir_lowering=False)
    P = make_P(nc, NEED_F, ["out"])
    fw = FW(nc)
    cs = Consts(fw, P.ident_f, P.ident_b, P.ones_b)
    build_layer0(nc, P, fw, cs)
    kv_down(fw, cs, P)
    build_layer1(fw, cs, P, P.out, P.b_out)
    fw.drain([P.b_out])
    return nc


def kernel(**inputs):
    inputs = {k: np.asarray(v) for k, v in inputs.items()}
    maps = [host_inputs(inputs, c) for c in range(NCORES)]
    nc = build_fused()
    res = run_bass_kernel_spmd(nc, [{k: m[k] for k in NEED_F} for m in maps], core_ids=list(range(NCORES)))
    out = np.empty((4, 4096, D), np.float32)
    for c in range(NCORES):
        b, s = c // 2, c % 2
        out[b, s * 2048:(s + 1) * 2048] = np.asarray(res.results[c]["out"])
    return out
```

```python
import numpy as np
import concourse.bass as bass
import concourse.mybir as mybir
from concourse.bass_utils import run_bass_kernel_spmd

F32 = mybir.dt.float32
BF16 = mybir.dt.bfloat16
I32 = mybir.dt.int32
AF = mybir.ActivationFunctionType
ALU = mybir.AluOpType
AX = mybir.AxisListType

NCORES = 8
D = 4096
T = 2048
TPRE = 2064
NEUT = 2048
TT = [(i * 512, 512) for i in range(4)]
TT_P = [(0, 512), (512, 512), (1024, 512), (1536, 512), (2048, 16)]


class Sem:
    _k = 0

    def __init__(self, nc, name):
        Sem._k += 1
        self.h = nc.alloc_semaphore(f"{name}_{Sem._k}")
        self.n = 0


class Buf:
    _id = 0

    def __init__(self, name=None, multi=False):
        Buf._id += 1
        self.name = name or f"b{Buf._id}"
        self.w = {}
        self.r = {}
        self.sem_in = None
        self.sem_out = None
        self.multi = multi


class FW:
    ENG = ("pe", "act", "dve", "pool", "sp")

    def __init__(self, nc):
        self.nc = nc
        self.eng = {"pe": nc.tensor, "act": nc.scalar, "dve": nc.vector,
                    "pool": nc.gpsimd, "sp": nc.sync}
        self.prog = {e: Sem(nc, f"prog_{e}") for e in self.ENG}
        self.waited = {e: {} for e in self.ENG}
        self.ninst = 0
        self.sem_pool = []

    def get_sem(self):
        if self.sem_pool:
            return self.sem_pool.pop()
        return Sem(self.nc, "dq")

    def put_sems(self, bufs):
        for b in bufs:
            for a in ("sem_in", "sem_out"):
                sm = getattr(b, a)
                if sm is not None:
                    self.sem_pool.append(sm)
                    setattr(b, a, None)

    def _wait(self, e, sem, cnt):
        if cnt <= 0:
            return
        w = self.waited[e]
        if w.get(sem, 0) >= cnt:
            return
        if sem is self.prog[e] and cnt > sem.n:
            return
        self.eng[e].wait_ge(sem.h, cnt)
        w[sem] = cnt

    def _deps(self, e, reads, writes, skip=None):
        pe = self.prog["pe"]
        for b in reads:
            for s, c in b.w.items():
                if (e == "pe" and s is pe) or s is skip:
                    continue
                self._wait(e, s, c)
        for b in writes:
            for s, c in b.r.items():
                if (e == "pe" and s is pe) or s is skip:
                    continue
                self._wait(e, s, c)
            if not b.multi:
                for s, c in b.w.items():
                    if (e == "pe" and s is pe) or s is skip:
                        continue
                    self._wait(e, s, c)

    def _mark(self, tok, reads, writes):
        s, c = tok
        for b in reads:
            if b.r.get(s, 0) < c:
                b.r[s] = c
        for b in writes:
            if b.multi:
                if b.w.get(s, 0) < c:
                    b.w[s] = c
            else:
                b.w = {s: c}
                b.r = {}

    def op(self, e, fn, reads=(), writes=(), signal=True):
        self._deps(e, reads, writes)
        inst = fn()
        self.ninst += 1
        p = self.prog[e]
        if signal:
            p.n += 1
            inst.then_inc(p.h, 1)
            tok = (p, p.n)
        else:
            tok = (p, p.n + 1)
        self._mark(tok, reads, writes)
        self.last = inst
        return tok

    def no_ldweights(self):
        old = self.last.ins
        new = mybir.InstMatmult(
            name=old.name, opcode=old.opcode, engine=old.engine, debug=old.debug, ins=old.ins, outs=old.outs,
            sync_info=old.sync_info, start_tensor_calc=old.start_tensor_calc, stop_tensor_calc=old.stop_tensor_calc,
            is_transpose=old.is_transpose, tile_size=old.tile_size, tile_position=old.tile_position,
            perf_mode=old.perf_mode, bass_skip_group_check=old.bass_skip_group_check, ldweights=False)
        self.nc.register_instruction(new, overwrite=True)

    def dma(self, q, out_ap, in_ap, src, dst, side, **kw):
        if side == "in":
            if dst.sem_in is None:
                dst.sem_in = self.get_sem()
            sem = dst.sem_in
        else:
            if src.sem_out is None:
                src.sem_out = self.get_sem()
            sem = src.sem_out
        self._deps(q, [src], [dst], skip=sem)
        inst = self.eng[q].dma_start(out=out_ap, in_=in_ap, **kw)
        sem.n += 16
        inst.then_inc(sem.h, 16)
        self.ninst += 1
        s, c = sem, sem.n
        if src.r.get(s, 0) < c:
            src.r[s] = c
        if dst.multi:
            dst.w[s] = c
        else:
            dst.w = {s: c}
            dst.r = {}
        return (s, c)

    def drain(self, bufs):
        for b in bufs:
            for s, c in list(b.w.items()) + list(b.r.items()):
                self._wait("sp", s, c)
        for e in self.ENG:
            if e != "sp":
                self._wait("sp", self.prog[e], self.prog[e].n)
        self.nc.all_engine_barrier()


import os as _os
GEMM_KOUTER = bool(int(_os.environ.get('GEMM_KOUTER', '0')))
GEMM_ROWTILE = bool(int(_os.environ.get('GEMM_ROWTILE', '0')))
GEMM_NOLDW = bool(int(_os.environ.get('GEMM_NOLDW', '0')))


class Arena:
    def __init__(self, sc, Tn=T, kmax=32, nps=8):
        self.act, _ = sc.sb([128, kmax, Tn], BF16, "g_act")
        self.b_act = [sc.track(Buf(f"act{k}")) for k in range(kmax)]
        self.wb = [sc.sb([128, kmax, 256], BF16, f"g_wb{i}", multi=True) for i in range(2)]
        self.ws = [sc.sb([128, 8, 256], F32, f"g_ws{i}", multi=True) for i in range(4)]
        self.ps = [sc.ps(name=f"g_ps{i}") for i in range(nps)]
        self.nps = nps
        self.ips = 0
        self.iws = 0
        self.iwb = 0


def gemm(fw, ar, actT, b_actT, K, W, b_W, superblocks, epilogue, tts=TT, kpass=32, Tn=T):
    nc = fw.nc
    nkc = K // 128
    npass = -(-nkc // kpass)
    base, rem = nkc // npass, nkc % npass
    passes = []
    k0 = 0
    for p in range(npass):
        n = base + (1 if p < rem else 0)
        passes.append((k0, n))
        k0 += n
    for kp, (kc0, kc) in enumerate(passes):
        for k in range(kc):
            r0 = (kc0 + k) * 128
            fw.dma("sp", ar.act[:, k, 0:Tn], actT[r0:r0 + 128, 0:Tn], b_actT, ar.b_act[k], "in")
        def sb_geom(sb):
            offs = []
            o = 0
            for (c0, n) in sb:
                offs.append(o)
                o += n
            assert o <= 256
            segs = []
            for (c0, n), oo in zip(sb, offs):
                if segs and segs[-1][0] + segs[-1][1] == c0 and segs[-1][2] + segs[-1][1] == oo:
                    segs[-1][1] += n
                else:
                    segs.append([c0, n, oo])
            return offs, o, segs

        def issue_dma(sbi):
            offs, o, segs = sb_geom(superblocks[sbi])
            for gi, g0 in enumerate(range(0, kc, 8)):
                gn = min(8, kc - g0)
                ws, b_ws = ar.ws[gi]
                r0 = (kc0 + g0) * 128
                for (c0, n, oo) in segs:
                    src = W[r0:r0 + gn * 128, c0:c0 + n].rearrange("(c p) j -> p c j", p=128)
                    fw.dma("sp", ws[:, 0:gn, oo:oo + n], src, b_W, b_ws, "in")

        def issue_cast(sbi):
            offs, o, segs = sb_geom(superblocks[sbi])
            wb, b_wb = ar.wb[sbi % 2]
            for gi, g0 in enumerate(range(0, kc, 8)):
                gn = min(8, kc - g0)
                ws, b_ws = ar.ws[gi]
                if gi % 2 == 0:
                    fw.op("dve", lambda: nc.vector.tensor_copy(wb[:, g0:g0 + gn, 0:o], ws[:, 0:gn, 0:o]),
                          reads=[b_ws], writes=[b_wb])
                else:
                    fw.op("act", lambda: nc.scalar.copy(wb[:, g0:g0 + gn, 0:o], ws[:, 0:gn, 0:o]),
                          reads=[b_ws], writes=[b_wb])

        nsb = len(superblocks)
        issue_dma(0)
        issue_cast(0)
        for sbi, sb in enumerate(superblocks):
            offs, o, segs = sb_geom(sb)
            wb, b_wb = ar.wb[sbi % 2]
            if sbi + 1 < nsb:
                issue_dma(sbi + 1)
            if GEMM_KOUTER and len(tts) <= ar.nps:
                for (c0, ncol), oo in zip(sb, offs):
                    pss = []
                    for _ in tts:
                        pss.append(ar.ps[ar.ips % ar.nps])
                        ar.ips += 1
                    for k in range(kc):
                        for ti_, ((t0, tn), (ps, b_ps)) in enumerate(zip(tts, pss)):
                            fw.op("pe", lambda: nc.tensor.matmul(ps[0:ncol, 0:tn], wb[:, k, oo:oo + ncol],
                                                                 ar.act[:, k, t0:t0 + tn],
                                                                 start=(k == 0), stop=(k == kc - 1)),
                                  reads=[b_wb, ar.b_act[k]], writes=[b_ps], signal=(k == kc - 1))
                            if ti_ > 0 and GEMM_NOLDW:
                                fw.no_ldweights()
                    for (t0, tn), (ps, b_ps) in zip(tts, pss):
                        epilogue(sbi, t0, tn, [(c0, ncol, ps, b_ps)], kp, len(passes))
                continue
            if GEMM_ROWTILE:
                for (t0, tn) in tts:
                    blocks = []
                    for (c0, ncol), oo in zip(sb, offs):
                        psA, b_psA = ar.ps[ar.ips % ar.nps]
                        psB, b_psB = ar.ps[(ar.ips + 1) % ar.nps]
                        ar.ips += 2
                        for k in range(kc):
                            fw.op("pe", lambda: nc.tensor.matmul(psA[0:ncol, 0:tn], wb[0:64, k, oo:oo + ncol],
                                                                 ar.act[0:64, k, t0:t0 + tn],
                                                                 start=(k == 0), stop=(k == kc - 1)),
                                  reads=[b_wb, ar.b_act[k]], writes=[b_psA], signal=(k == kc - 1))
                            fw.op("pe", lambda: nc.tensor.matmul(psB[0:ncol, 0:tn], wb[64:128, k, oo:oo + ncol],
                                                                 ar.act[64:128, k, t0:t0 + tn],
                                                                 start=(k == 0), stop=(k == kc - 1)),
                                  reads=[b_wb, ar.b_act[k]], writes=[b_psB], signal=(k == kc - 1))
                        blocks.append((c0, ncol, psA, b_psA, psB, b_psB))
                    epilogue(sbi, t0, tn, blocks, kp, len(passes))
                continue
            for ti_, (t0, tn) in enumerate(tts):
                if ti_ == len(tts) - 1 and sbi + 1 < nsb:
                    issue_cast(sbi + 1)
                blocks = []
                for (c0, ncol), oo in zip(sb, offs):
                    ps, b_ps = ar.ps[ar.ips % ar.nps]
                    ar.ips += 1
                    for k in range(kc):
                        fw.op("pe", lambda: nc.tensor.matmul(ps[0:ncol, 0:tn], wb[:, k, oo:oo + ncol],
                                                             ar.act[:, k, t0:t0 + tn],
                                                             start=(k == 0), stop=(k == kc - 1)),
                              reads=[b_wb, ar.b_act[k]], writes=[b_ps], signal=(k == kc - 1))
                    blocks.append((c0, ncol, ps, b_ps))
                epilogue(sbi, t0, tn, blocks, kp, len(passes))


def cb_range(c0, c1):
    out = []
    c = c0
    while c < c1:
        sb = []
        e = min(c + 256, c1)
        cc = c
        while cc < e:
            n = min(128, e - cc)
            sb.append((cc, n))
            cc += n
        out.append(sb)
        c = e
    return out


import contextlib


class Scope:
    def __init__(self, fw):
        self.fw = fw
        self.nc = fw.nc
        self.stack = contextlib.ExitStack()
        self.bufs = []

    def __enter__(self):
        self.stack.__enter__()
        return self

    def __exit__(self, *a):
        if a[0] is None:
            self.fw.drain(self.bufs)
            self.fw.put_sems(self.bufs)
        return self.stack.__exit__(*a)

    def sb(self, shape, dtype, name=None, multi=False):
        t = self.stack.enter_context(self.nc.sbuf_tensor(list(shape), dtype))
        b = Buf(name, multi=multi)
        self.bufs.append(b)
        return t, b

    def ps(self, shape=(128, 512), dtype=F32, name=None):
        t = self.stack.enter_context(self.nc.psum_tensor(list(shape), dtype))
        b = Buf(name)
        self.bufs.append(b)
        return t, b

    def track(self, b):
        self.bufs.append(b)
        return b


class Consts:
    def __init__(self, fw, ident_f32, ident_bf16, ones_bf16):
        nc = fw.nc
        self.b = Buf("consts", multi=True)
        self.ident = nc.alloc_sbuf_tensor("c_ident", [128, 128], F32)
        self.identb = nc.alloc_sbuf_tensor("c_identb", [128, 128], BF16)
        self.onesb = nc.alloc_sbuf_tensor("c_onesb", [128, 128], BF16)
        bd = Buf("cdram", multi=True)
        fw.dma("sp", self.ident[:, :], ident_f32, bd, self.b, "in")
        fw.dma("sp", self.identb[:, :], ident_bf16, bd, self.b, "in")
        fw.dma("sp", self.onesb[:, :], ones_bf16, bd, self.b, "in")


def tok_blocks(n, bs=128):
    out = []
    t = 0
    while t < n:
        out.append((t, min(bs, n - t)))
        t += bs
    return out


def transpose_in(fw, cs, x, b_x, hT, b_hT, Tn, Dn=D):
    nc = fw.nc
    nk = Dn // 128
    with Scope(fw) as sc:
        xin = [sc.sb([128, Dn], F32, f"ti_x{i}") for i in range(2)]
        ho = [sc.sb([128, nk, 512], F32, f"ti_h{i}") for i in range(2)]
        pss = [sc.ps(name=f"ti_ps{i}") for i in range(4)]
        ips = 0
        groups = tok_blocks(Tn, 512)
        for gi, (g0, gn) in enumerate(groups):
            h_t, h_b = ho[gi % 2]
            for bi, (t0, tn) in enumerate(tok_blocks(gn, 128)):
                x_t, x_b = xin[bi % 2]
                fw.dma("sp", x_t[0:tn, :], x[g0 + t0:g0 + t0 + tn, :], b_x, x_b, "in")
                for c4 in range(0, nk, 4):
                    ps_t, ps_b = pss[ips % 4]
                    ips += 1
                    for j in range(4):
                        c = c4 + j
                        fw.op("pe", lambda: nc.tensor.transpose(ps_t[:, j * 128:j * 128 + tn],
                                                                x_t[0:tn, c * 128:(c + 1) * 128],
                                                                cs.ident[0:tn, 0:tn]),
                              reads=[x_b, cs.b], writes=[ps_b], signal=(j == 3))
                    e = "dve" if (c4 // 4) % 2 == 0 else "act"
                    src = ps_t[:, :].rearrange("p (j t) -> p j t", j=4)[:, :, 0:tn]
                    dst = h_t[:, c4:c4 + 4, t0:t0 + tn]
                    if e == "dve":
                        fw.op("dve", lambda: nc.vector.tensor_copy(dst, src), reads=[ps_b], writes=[h_b])
                    else:
                        fw.op("act", lambda: nc.scalar.copy(dst, src), reads=[ps_b], writes=[h_b])
            fw.dma("pool", hT[:, g0:g0 + gn].rearrange("(c p) t -> p c t", p=128), h_t[:, :, 0:gn],
                   h_b, b_hT, "out")


def norm_fm(fw, cs, src, b_src, g_l, b_g, dst, b_dst, Tn, Kd, tw=256):
    nc = fw.nc
    nk = Kd // 128
    with Scope(fw) as sc:
        g_t, g_b = sc.sb([128, nk], F32, "nf_g")
        fw.dma("sp", g_t[:, :], g_l, b_g, g_b, "in")
        xs = [sc.sb([128, nk, tw], F32, f"nf_x{i}") for i in range(2)]
        sq = [sc.sb([128, nk, tw], BF16, f"nf_sq{i}") for i in range(2)]
        ys = [sc.sb([128, nk, tw], BF16, f"nf_y{i}") for i in range(2)]
        rs = [sc.sb([128, tw], F32, f"nf_r{i}") for i in range(2)]
        pss = [sc.ps(name=f"nf_ps{i}") for i in range(2)]
        for i, (t0, tn) in enumerate(tok_blocks(Tn, tw)):
            x_t, x_b = xs[i % 2]
            s_t, s_b = sq[i % 2]
            y_t, y_b = ys[i % 2]
            r_t, r_b = rs[i % 2]
            ps_t, ps_b = pss[i % 2]
            fw.dma("sp", x_t[:, :, 0:tn], src[:, t0:t0 + tn].rearrange("(c p) t -> p c t", p=128),
                   b_src, x_b, "in")
            fw.op("act", lambda: nc.scalar.activation(s_t[:, :, 0:tn], x_t[:, :, 0:tn], AF.Square),
                  reads=[x_b], writes=[s_b])
            for c in range(nk):
                fw.op("pe", lambda: nc.tensor.matmul(ps_t[:, 0:tn], cs.onesb[:, :], s_t[:, c, 0:tn],
                                                     start=(c == 0), stop=(c == nk - 1)),
                      reads=[s_b, cs.b], writes=[ps_b], signal=(c == nk - 1))
            rstd_from_sumsq(fw, ps_t[:, 0:tn], ps_b, r_t[:, 0:tn], r_b, 1.0 / Kd)
            for c in range(nk):
                fw.op("dve", lambda: nc.vector.scalar_tensor_tensor(
                    y_t[:, c, 0:tn], x_t[:, c, 0:tn], g_t[:, c:c + 1], r_t[:, 0:tn],
                    ALU.mult, ALU.mult), reads=[x_b, g_b, r_b], writes=[y_b], signal=(c == nk - 1))
            fw.dma("pool", dst[:, t0:t0 + tn].rearrange("(c p) t -> p c t", p=128), y_t[:, :, 0:tn],
                   y_b, b_dst, "out")


EPS = 1e-6


def rstd_from_sumsq(fw, ss_ap, ss_b, out_ap, out_b, inv_n):
    nc = fw.nc
    fw.op("act", lambda: nc.scalar.activation(out_ap, ss_ap, AF.Ln, bias=EPS, scale=inv_n),
          reads=[ss_b], writes=[out_b])
    fw.op("act", lambda: nc.scalar.activation(out_ap, out_ap, AF.Exp, scale=-0.5),
          reads=[out_b], writes=[out_b])


class Evac:
    def __init__(self, sc, n=4, dtype=BF16, w=512, name="ev"):
        self.bufs = [sc.sb([128, w], dtype, f"{name}{i}") for i in range(n)]
        self.i = 0

    def next(self):
        t = self.bufs[self.i % len(self.bufs)]
        self.i += 1
        return t


def evac_copy(fw, dst_ap, dst_b, src_ap, src_b, idx, func=None, scale=1.0):
    nc = fw.nc
    if func is not None or idx % 2 == 1:
        f = func if func is not None else AF.Copy
        fw.op("act", lambda: nc.scalar.activation(dst_ap, src_ap, f, scale=scale), reads=[src_b], writes=[dst_b])
    else:
        if scale == 1.0:
            fw.op("dve", lambda: nc.vector.tensor_copy(dst_ap, src_ap), reads=[src_b], writes=[dst_b])
        else:
            fw.op("dve", lambda: nc.vector.tensor_scalar(dst_ap, src_ap, scale, None, ALU.mult),
                  reads=[src_b], writes=[dst_b])


def store_epilogue(fw, sc, routes):
    evb = Evac(sc, 3, BF16, name="evb")
    evf = Evac(sc, 2, F32, name="evf")
    cnt = [0]

    def epi(sbi, t0, tn, blocks, kp, nkp):
        for (c0, ncol, ps, b_ps) in blocks:
            for (lo, hi, dst, roff, toff, func, scale, dt, b_dst) in routes:
                if lo <= c0 < hi:
                    break
            else:
                raise AssertionError(c0)
            ev, b_ev = (evb if dt == BF16 else evf).next()
            evac_copy(fw, ev[0:ncol, 0:tn], b_ev, ps[0:ncol, 0:tn], b_ps, cnt[0], func, scale)
            cnt[0] += 1
            r = roff + c0 - lo
            fw.dma("pool", dst[r:r + ncol, toff + t0:toff + t0 + tn], ev[0:ncol, 0:tn], b_ev, b_dst, "out")
    return epi


def resid_epilogue(fw, sc, hT, b_hT):
    nc = fw.nc
    hb = Evac(sc, 4, F32, name="rh")
    pre_w = dict(b_hT.w)
    regions = {}

    def region(c0, t0):
        key = (c0, t0)
        if key not in regions:
            rb = Buf(f"hreg_{c0}_{t0}")
            rb.w = dict(pre_w)
            regions[key] = rb
        return regions[key]

    def epi(sbi, t0, tn, blocks, kp, nkp):
        for (c0, ncol, ps, b_ps) in blocks:
            h, b_h = hb.next()
            rb = region(c0, t0)
            s_, c_ = fw.dma("sp", h[0:ncol, 0:tn], hT[c0:c0 + ncol, t0:t0 + tn], rb, b_h, "in")
            if b_hT.r.get(s_, 0) < c_:
                b_hT.r[s_] = c_
            fw.op("dve", lambda: nc.vector.tensor_tensor(h[0:ncol, 0:tn], ps[0:ncol, 0:tn], h[0:ncol, 0:tn], ALU.add),
                  reads=[b_ps, b_h], writes=[b_h])
            s_, c_ = fw.dma("pool", hT[c0:c0 + ncol, t0:t0 + tn], h[0:ncol, 0:tn], b_h, rb, "out")
            if b_hT.w.get(s_, 0) < c_:
                b_hT.w[s_] = c_
    return epi


def swiglu_epilogue(fw, sc, hidT, b_hid, dff):
    nc = fw.nc
    sg = Evac(sc, 2, F32, name="sg")
    hb = Evac(sc, 3, BF16, name="hb")

    def epi(sbi, t0, tn, blocks, kp, nkp):
        (cg, ncol, psg, b_psg), (cu, ncol2, psu, b_psu) = blocks
        assert cu == cg + dff and ncol == ncol2
        s, b_s = sg.next()
        h, b_h = hb.next()
        fw.op("act", lambda: nc.scalar.activation(s[0:ncol, 0:tn], psg[0:ncol, 0:tn], AF.Silu),
              reads=[b_psg], writes=[b_s])
        fw.op("dve", lambda: nc.vector.tensor_tensor(h[0:ncol, 0:tn], s[0:ncol, 0:tn], psu[0:ncol, 0:tn], ALU.mult),
              reads=[b_s, b_psu], writes=[b_h])
        fw.dma("pool", hidT[cg:cg + ncol, t0:t0 + tn], h[0:ncol, 0:tn], b_h, b_hid, "out")
    return epi


DFF = 11008


class Seg:
    def __init__(self, hT, b_hT, xnT, b_xnT, Tn, tts, off):
        self.hT, self.b_hT, self.xnT, self.b_xnT, self.Tn, self.tts, self.off = hT, b_hT, xnT, b_xnT, Tn, tts, off


def seg_L(P):
    return Seg(P.hT, P.b_hT, P.xnT, P.b_xnT, T, TT, TPRE)


def seg_P(P):
    return Seg(P.hpT, P.b_hpT, P.xnpT, P.b_xnpT, TPRE, TT_P, 0)


def ffn(fw, cs, P, layer, sg=None):
    sg = sg or seg_L(P)
    norm_fm(fw, cs, sg.hT, sg.b_hT, P.ffn_g[layer], P.b_in, sg.xnT, sg.b_xnT, sg.Tn, D)
    hid = P.hidT[:, 0:sg.Tn]
    with Scope(fw) as sc:
        ar = Arena(sc, Tn=sg.Tn)
        sbs = [[(j * 128, 128), (DFF + j * 128, 128)] for j in range(DFF // 128)]
        gemm(fw, ar, sg.xnT, sg.b_xnT, D, P.w_gu[layer], P.b_in, sbs,
             swiglu_epilogue(fw, sc, hid, P.b_hidT, DFF), tts=sg.tts, Tn=sg.Tn)
    with Scope(fw) as sc:
        ar = Arena(sc, Tn=sg.Tn, kmax=29)
        gemm(fw, ar, hid, P.b_hidT, DFF, P.w_dn[layer], P.b_in, cb_range(0, D),
             resid_epilogue(fw, sc, sg.hT, sg.b_hT), kpass=29, tts=sg.tts, Tn=sg.Tn)


TALL = TPRE + T
MH = 8
MDK = 256
MDV = 512
CHUNKS = [(128 * c, 128, True) for c in range(16)] + [(NEUT, 16, True)] + \
         [(NEUT + 16 + 128 * c, 128, True) for c in range(16)]
NCH = len(CHUNKS)


def l0_proj(fw, cs, P):
    for (xn, b_xn, Tn, tts, off) in ((P.xnT, P.b_xnT, T, TT, TPRE), (P.xnpT, P.b_xnpT, TPRE, TT_P, 0)):
        with Scope(fw) as sc:
            ar = Arena(sc, Tn=Tn)
            routes = [
                (0, 2048, P.qT, 0, off, None, 1.0, BF16, P.b_qT),
                (2048, 4096, P.kT, 0, off, None, 1.0 / 16.0, BF16, P.b_kT),
                (4096, 8192, P.vT, 0, off, None, 1.0, BF16, P.b_vT),
                (8192, 12288, P.ogT, 0, off, AF.Sigmoid, 1.0, BF16, P.b_ogT),
                (12288, 12296, P.graw, 0, off, None, 1.0, F32, P.b_graw),
                (12296, 12304, P.graw, 8, off, None, 1.0, F32, P.b_graw),
            ]
            gemm(fw, ar, xn, b_xn, D, P.w_in, P.b_in, cb_range(0, 12288) + [[(12288, 8), (12296, 8)]],
                 store_epilogue(fw, sc, routes), tts=tts, Tn=Tn)


def mlstm_gates(fw, cs, P, efc, b_efc, lam, b_lam):
    nc = fw.nc
    with Scope(fw) as sc:
        A1, b1 = sc.sb([8, TALL], F32, "mg1")
        A2, b2 = sc.sb([8, TALL], F32, "mg2")
        A3, b3 = sc.sb([8, TALL], F32, "mg3")
        A4, b4 = sc.sb([8, TALL + 1], F32, "mg4")
        sm, bsm = sc.sb([8, 8], F32, "mgs")
        sel, bsel = sc.sb([8, 8, 128], F32, "mgsel")
        zb, bzb = sc.sb([128, 8, 34], F32, "mgzb")
        ps1, bp1 = sc.ps(name="mgp1")
        ps2, bp2 = sc.ps(name="mgp2")
        fw.dma("sp", A1[:, :], P.graw[0:8, :], P.b_graw, b1, "in")
        fw.dma("sp", A2[:, :], P.graw[8:16, :], P.b_graw, b2, "in")
        fw.dma("sp", sm[:, 0:3], P.mparams, P.b_in, bsm, "in")
        fw.dma("sp", sel[:, :, :], P.sel8, P.b_in, bsel, "in")
        V = nc.vector
        fw.op("dve", lambda: V.tensor_scalar(sm[:, 3:5], sm[:, 0:2], 1.0 / 15.0, None, ALU.mult), reads=[bsm], writes=[bsm])
        fw.op("dve", lambda: V.tensor_scalar(sm[:, 5:6], sm[:, 2:3], -1.0, 30000.0, ALU.add, ALU.mult), reads=[bsm], writes=[bsm])
        fw.op("act", lambda: nc.scalar.activation(A1[:, :], A1[:, :], AF.Tanh, bias=sm[:, 3:4], scale=1.0 / 15.0),
              reads=[b1, bsm], writes=[b1])
        fw.op("dve", lambda: V.tensor_scalar(A1[:, :], A1[:, :], 15.0, None, ALU.mult), reads=[b1], writes=[b1])
        fw.op("act", lambda: nc.scalar.activation(A2[:, :], A2[:, :], AF.Tanh, bias=sm[:, 4:5], scale=1.0 / 15.0),
              reads=[b2, bsm], writes=[b2])
        fw.op("act", lambda: nc.scalar.activation(A2[:, :], A2[:, :], AF.Exp, scale=-15.0), reads=[b2], writes=[b2])
        fw.op("act", lambda: nc.scalar.activation(A2[:, :], A2[:, :], AF.Ln, bias=1.0, scale=1.0), reads=[b2], writes=[b2])
        fw.op("dve", lambda: V.tensor_scalar(A2[:, :], A2[:, :], -1.0, None, ALU.mult), reads=[b2], writes=[b2])
        fw.op("dve", lambda: V.tensor_scalar(A2[:, 0:NEUT], A2[:, 0:NEUT], sm[:, 2:3], None, ALU.mult),
              reads=[b2, bsm], writes=[b2])
        fw.op("dve", lambda: V.tensor_scalar(A1[:, 0:NEUT], A1[:, 0:NEUT], sm[:, 2:3], sm[:, 5:6], ALU.mult, ALU.add),
              reads=[b1, bsm], writes=[b1])
        fw.op("pool", lambda: nc.gpsimd.memset(A4[:, :], 0.0), writes=[b4])
        fw.op("dve", lambda: V.tensor_tensor_scan(A3[:, :], A2[:, :], A4[:, 0:TALL], 0.0, ALU.add, ALU.add),
              reads=[b2, b4], writes=[b3])
        fw.op("dve", lambda: V.tensor_tensor(A1[:, :], A1[:, :], A3[:, :], ALU.subtract), reads=[b1, b3], writes=[b1])
        fw.op("dve", lambda: V.tensor_tensor_scan(A4[:, 1:TALL + 1], A1[:, :], A1[:, :], 0.0, ALU.max, ALU.max),
              reads=[b1], writes=[b4])
        fw.op("dve", lambda: V.tensor_scalar(A4[:, 1:TALL + 1], A4[:, 1:TALL + 1], -1.0, None, ALU.mult),
              reads=[b4], writes=[b4])
        for c, (t0, L, _) in enumerate(CHUNKS):
            fw.op("act", lambda: nc.scalar.activation(A1[:, t0:t0 + L], A1[:, t0:t0 + L], AF.Exp,
                                                      bias=A4[:, t0:t0 + 1], scale=1.0),
                  reads=[b1, b4], writes=[b1], signal=False)
            fw.op("act", lambda: nc.scalar.activation(A3[:, t0:t0 + L], A3[:, t0:t0 + L], AF.Exp,
                                                      bias=A4[:, t0:t0 + 1], scale=-1.0),
                  reads=[b3, b4], writes=[b3], signal=(c == NCH - 1))
        for c, (t0, L, _) in enumerate(CHUNKS):
            pst, bpt = (ps1, bp1) if c % 2 == 0 else (ps2, bp2)
            fw.op("pe", lambda: nc.tensor.transpose(pst[0:L, 0:8], A1[:, t0:t0 + L], cs.ident[0:8, 0:8]),
                  reads=[b1, cs.b], writes=[bpt], signal=False)
            fw.op("pe", lambda: nc.tensor.transpose(pst[0:L, 8:16], A3[:, t0:t0 + L], cs.ident[0:8, 0:8]),
                  reads=[b3, cs.b], writes=[bpt])
            fw.op("dve", lambda: V.tensor_copy(efc[0:L, c, :], pst[0:L, 0:16]), reads=[bpt], writes=[b_efc])
        for h in range(8):
            pst, bpt = (ps1, bp1) if h % 2 == 0 else (ps2, bp2)
            fw.op("pe", lambda: nc.tensor.matmul(pst[:, 0:17], sel[:, h, :], A4[:, 0:NEUT + 1:128], start=True, stop=True),
                  reads=[bsel, b4], writes=[bpt], signal=False)
            fw.op("pe", lambda: nc.tensor.matmul(pst[:, 17:34], sel[:, h, :], A4[:, NEUT + 16:TALL + 1:128], start=True, stop=True),
                  reads=[bsel, b4], writes=[bpt])
            fw.op("dve", lambda: V.tensor_copy(zb[:, h, :], pst[:, 0:34]), reads=[bpt], writes=[bzb])
        fw.op("dve", lambda: V.tensor_tensor(lam[:, :, :], zb[:, :, 1:34], zb[:, :, 0:33], ALU.subtract),
              reads=[bzb], writes=[b_lam])
        fw.op("act", lambda: nc.scalar.activation(lam[:, :, :], lam[:, :, :], AF.Exp), reads=[b_lam], writes=[b_lam])


def mlstm(fw, cs, P):
    import os
    nc = fw.nc
    V = nc.vector
    A = nc.scalar
    G = nc.gpsimd
    efc = nc.alloc_sbuf_tensor("m_efc", [128, NCH, 16], F32)
    lam = nc.alloc_sbuf_tensor("m_lam", [128, 8, NCH], F32)
    b_efc, b_lam = Buf("efc"), Buf("lam")
    mlstm_gates(fw, cs, P, efc, b_efc, lam, b_lam)
    if getattr(P, "debug_gates", False):
        fw.dma("sp", P.dbg_efc, efc[:, :, :].rearrange("p a b -> p (a b)"), b_efc, P.b_dbg_efc, "out")
        fw.dma("sp", P.dbg_lam, lam[:, :, :].rearrange("p a b -> p (a b)"), b_lam, P.b_dbg_lam, "out")
        return
    groups = [(256 * g, 256, [2 * g, 2 * g + 1]) for g in range(8)] + [(NEUT, 16, [16])] + \
             [(NEUT + 16 + 256 * g, 256, [17 + 2 * g, 18 + 2 * g]) for g in range(8)]
    with Scope(fw) as sc:
        kTg = [sc.sb([128, 16, 256], BF16, f"m_k{i}") for i in range(2)]
        vTg = [sc.sb([128, 32, 256], BF16, f"m_v{i}") for i in range(2)]
        qTg = [sc.sb([128, 16, 256], BF16, f"m_q{i}") for i in range(2)]
        ogg = [sc.sb([128, 32, 256], BF16, f"m_o{i}", multi=True) for i in range(2)]
        Cst = [sc.sb([128, 2, 513], F32, f"m_C{h}") for h in range(8)]
        Cbf = [sc.sb([128, 2, 513], BF16, f"m_Cb{h}") for h in range(8)]
        Clt = [sc.sb([128, 513], F32, f"m_Cl{i}") for i in range(2)]
        NR = 4
        kE = [sc.sb([128, 256], BF16, f"m_kE{i}") for i in range(NR)]
        vx = [sc.sb([128, 513], BF16, f"m_vx{i}", multi=True) for i in range(NR)]
        Sp = [sc.sb([128, 128], BF16, f"m_Sp{i}") for i in range(NR)]
        junk = [sc.sb([128, 512], BF16, f"m_jk{i}") for i in range(2)]
        hn = [sc.sb([128, 512], BF16, f"m_hn{i}") for i in range(NR)]
        sml = [sc.sb([128, 8], F32, f"m_sm{i}") for i in range(NR)]
        recl = [sc.sb([128, 1], F32, f"m_rec{i}") for i in range(NR)]
        scal = [sc.sb([128, 1], F32, f"m_scl{i}") for i in range(NR)]
        mask, b_mask = sc.sb([128, 128], BF16, "m_mask")
        gout, b_gout = sc.sb([128, 32], F32, "m_gout")
        B0, b0 = sc.ps([128, 1024], BF16, "m_B0")
        B1, bb1 = sc.ps(name="m_B1")
        B2, bb2 = sc.ps(name="m_B2")
        B3, bb3 = sc.ps(name="m_B3")
        B4, bb4 = sc.ps([128, 1024], BF16, "m_B4")
        B5, bb5 = sc.ps(name="m_B5")
        B6, bb6 = sc.ps(name="m_B6")
        B7, bb7 = sc.ps(name="m_B7")
        fw.dma("sp", mask[:, :], P.maskT, P.b_in, b_mask, "in")
        fw.dma("sp", gout[:, :], P.gout_l, P.b_in, b_gout, "in")
        for h in range(8):
            fw.op("pool", lambda: G.memset(Cst[h][0][:, :, :], 0.0), writes=[Cst[h][1]])
            fw.op("pool", lambda: G.memset(Cbf[h][0][:, :, :], 0.0), writes=[Cbf[h][1]])
        for i in range(NR):
            fw.op("pool", lambda: G.memset(vx[i][0][:, 512:513], 1.0), writes=[vx[i][1]])

        items = []
        for gi, (g0, gn, chs) in enumerate(groups):
            for c in chs:
                for h in range(8):
                    items.append((gi, c, h))
        loaded = set()

        def load_group(gi):
            if gi in loaded or gi >= len(groups):
                return
            loaded.add(gi)
            g0, gn, chs = groups[gi]
            main = CHUNKS[chs[0]][2]
            kt, bk = kTg[gi % 2]
            vt, bv = vTg[gi % 2]
            fw.dma("sp", kt[:, :, 0:gn], P.kT[:, g0:g0 + gn].rearrange("(c p) t -> p c t", p=128), P.b_kT, bk, "in")
            fw.dma("sp", vt[:, :, 0:gn], P.vT[:, g0:g0 + gn].rearrange("(c p) t -> p c t", p=128), P.b_vT, bv, "in")
            if main:
                qt, bq = qTg[gi % 2]
                ot, bo = ogg[gi % 2]
                l0 = g0
                fw.dma("sp", qt[:, :, 0:gn], P.qT[:, l0:l0 + gn].rearrange("(c p) t -> p c t", p=128), P.b_qT, bq, "in")
                fw.dma("sp", ot[:, :, 0:gn], P.ogT[:, l0:l0 + gn].rearrange("(c p) t -> p c t", p=128), P.b_ogT, bo, "in")

        def ctx(idx):
            gi, c, h = items[idx]
            g0, gn, chs = groups[gi]
            t0, L, main = CHUNKS[c]
            return gi, c, h, t0 - g0, L, main

        def stage1(idx):
            gi, c, h, o, L, main = ctx(idx)
            r = idx % NR
            kt, bk = kTg[gi % 2]
            vt, bv = vTg[gi % 2]
            parts = os.environ.get("S1_PARTS", "12")
            if "1" in parts:
                for j in range(2):
                    fw.op("pe", lambda: nc.tensor.transpose(B0[0:L, j * 128:(j + 1) * 128], kt[:, 2 * h + j, o:o + L], cs.identb[:, :]),
                          reads=[bk, cs.b], writes=[b0], signal=(j == 1))
                fw.op("dve", lambda: V.tensor_scalar(kE[r][0][0:L, :], B0[0:L, 0:256], efc[0:L, c, h:h + 1], None, ALU.mult),
                      reads=[b0, b_efc], writes=[kE[r][1]])
            if "2" in parts:
                for j in range(4):
                    fw.op("pe", lambda: nc.tensor.transpose(B0[0:L, 256 + j * 128:256 + (j + 1) * 128], vt[:, 4 * h + j, o:o + L], cs.identb[:, :]),
                          reads=[bv, cs.b], writes=[b0], signal=(j == 3))
                fw.op("act", lambda: A.copy(vx[r][0][0:L, 0:512], B0[0:L, 256:768]), reads=[b0], writes=[vx[r][1]])
            if main:
                qt, bq = qTg[gi % 2]
                for j in range(2):
                    fw.op("pe", lambda: nc.tensor.matmul(B1[0:L, 0:L], kt[:, 2 * h + j, o:o + L], qt[:, 2 * h + j, o:o + L],
                                                         start=(j == 0), stop=(j == 1)),
                          reads=[bk, bq], writes=[bb1], signal=(j == 1))
                fw.op("dve", lambda: V.scalar_tensor_tensor(Sp[r][0][0:L, 0:L], B1[0:L, 0:L], efc[0:L, c, h:h + 1],
                                                            mask[0:L, 0:L], ALU.mult, ALU.mult),
                      reads=[bb1, b_efc, b_mask], writes=[Sp[r][1]])

        def stage2(idx):
            gi, c, h, o, L, main = ctx(idx)
            r = idx % NR
            Ct, bC = Cst[h]
            Cb, bCb = Cbf[h]
            if main:
                qt, bq = qTg[gi % 2]
                sm_t, b_sm = sml[r]
                for j in range(2):
                    fw.op("pe", lambda: nc.tensor.matmul(B2[0:L, 0:512], qt[:, 2 * h + j, o:o + L], Cb[:, j, 0:512],
                                                         start=(j == 0), stop=False),
                          reads=[bq, bCb], writes=[bb2], signal=False)
                fw.op("pe", lambda: nc.tensor.matmul(B2[0:L, 0:512], Sp[r][0][0:L, 0:L], vx[r][0][0:L, 0:512], start=False, stop=True),
                      reads=[Sp[r][1], vx[r][1]], writes=[bb2])
                for j in range(2):
                    fw.op("pe", lambda: nc.tensor.matmul(B3[0:L, 0:1], qt[:, 2 * h + j, o:o + L], Cb[:, j, 512:513],
                                                         start=(j == 0), stop=False),
                          reads=[bq, bCb], writes=[bb3], signal=False)
                fw.op("pe", lambda: nc.tensor.matmul(B3[0:L, 0:1], Sp[r][0][0:L, 0:L], vx[r][0][0:L, 512:513], start=False, stop=True),
                      reads=[Sp[r][1], vx[r][1]], writes=[bb3])
                fw.op("act", lambda: A.activation(sm_t[0:L, 0:1], B3[0:L, 0:1], AF.Abs), reads=[bb3], writes=[b_sm])
                fw.op("dve", lambda: V.tensor_tensor(sm_t[0:L, 0:1], sm_t[0:L, 0:1], efc[0:L, c, 8 + h:9 + h], ALU.max),
                      reads=[b_sm, b_efc], writes=[b_sm])
                rc_t, b_rc = recl[r]
                sl_t, b_sl = scal[r]
                fw.op("dve", lambda: V.reciprocal(rc_t[0:L, 0:1], sm_t[0:L, 0:1]), reads=[b_sm], writes=[b_rc])
                jk, b_jk = junk[idx % 2]
                fw.op("act", lambda: A.activation(jk[0:L, :], B2[0:L, 0:512], AF.Square, scale=rc_t[0:L, 0:1], accum_out=sm_t[0:L, 2:3]),
                      reads=[bb2, b_rc], writes=[b_jk, b_sm])
                fw.op("act", lambda: A.activation(sm_t[0:L, 3:4], sm_t[0:L, 2:3], AF.Ln, bias=EPS, scale=1.0 / MDV),
                      reads=[b_sm], writes=[b_sm])
                fw.op("act", lambda: A.activation(sm_t[0:L, 3:4], sm_t[0:L, 3:4], AF.Exp, scale=-0.5), reads=[b_sm], writes=[b_sm])
                fw.op("dve", lambda: V.tensor_tensor(sl_t[0:L, 0:1], sm_t[0:L, 3:4], rc_t[0:L, 0:1], ALU.mult), reads=[b_sm, b_rc], writes=[b_sl])
                fw.op("act", lambda: A.activation(hn[r][0][0:L, :], B2[0:L, 0:512], AF.Copy, scale=sl_t[0:L, 0:1]),
                      reads=[bb2, b_sl], writes=[hn[r][1]])
            lsc = lam[:, h, c:c + 1]
            for j, (Bj, bbj) in enumerate(((B5, bb5), (B6, bb6))):
                fw.op("pe", lambda: nc.tensor.matmul(Bj[:, 0:512], kE[r][0][0:L, j * 128:(j + 1) * 128], vx[r][0][0:L, 0:512], start=True, stop=True),
                      reads=[kE[r][1], vx[r][1]], writes=[bbj])
                fw.op("pe", lambda: nc.tensor.matmul(B7[:, j:j + 1], kE[r][0][0:L, j * 128:(j + 1) * 128], vx[r][0][0:L, 512:513], start=True, stop=True),
                      reads=[kE[r][1], vx[r][1]], writes=[bb7])
                cl, b_cl = Clt[j]
                fw.op("pool", lambda: G.tensor_scalar(cl[:, :], Ct[:, j, :], lsc, 0.0, ALU.mult, ALU.add), reads=[bC, b_lam], writes=[b_cl])
                fw.op("dve", lambda: V.scalar_tensor_tensor(Ct[:, j, 0:512], Bj[:, 0:512], lsc, cl[:, 0:512], ALU.mult, ALU.add),
                      reads=[bbj, b_lam, b_cl], writes=[bC])
                fw.op("dve", lambda: V.scalar_tensor_tensor(Ct[:, j, 512:513], B7[:, j:j + 1], lsc, cl[:, 512:513], ALU.mult, ALU.add),
                      reads=[bb7, b_lam, b_cl], writes=[bC])
                fw.op("act", lambda: A.copy(Cb[:, j, :], Ct[:, j, :]), reads=[bC], writes=[bCb])

        def stage3(idx):
            gi, c, h, o, L, main = ctx(idx)
            if not main:
                return
            r = idx % NR
            ot, bo = ogg[gi % 2]
            for j in range(4):
                fw.op("pe", lambda: nc.tensor.transpose(B4[:, j * 128:j * 128 + L], hn[r][0][0:L, j * 128:(j + 1) * 128], cs.identb[0:L, 0:L]),
                      reads=[hn[r][1], cs.b], writes=[bb4], signal=(j == 3))
            for j in range(4):
                fw.op("dve", lambda: V.scalar_tensor_tensor(ot[:, 4 * h + j, o:o + L], B4[:, j * 128:j * 128 + L], gout[:, 4 * h + j:4 * h + j + 1],
                                                            ot[:, 4 * h + j, o:o + L], ALU.mult, ALU.mult),
                      reads=[bb4, b_gout, bo], writes=[bo], signal=(j == 3))
            g0, gn, chs = groups[gi]
            if h == 7 and c == chs[-1]:
                l0 = g0
                fw.dma("pool", P.yT[:, l0:l0 + gn].rearrange("(c p) t -> p c t", p=128), ot[:, :, 0:gn], bo, P.b_yT, "out")

        load_group(0)
        load_group(1)
        import os
        n = len(items)
        if os.environ.get("MLSTM_ITEMS"):
            n = int(os.environ["MLSTM_ITEMS"])
        for it in range(n + 2):
            if it < n:
                stage1(it)
            if 0 <= it - 1 < n and not os.environ.get("MLSTM_SKIP2"):
                stage2(it - 1)
            if 0 <= it - 2 < n:
                stage3(it - 2)
                gi, c, h = items[it - 2]
                if h == 7 and c == groups[gi][2][-1]:
                    load_group(gi + 2)


def w_out_phase(fw, cs, P, sg=None):
    sg = sg or seg_L(P)
    with Scope(fw) as sc:
        ar = Arena(sc, Tn=sg.Tn)
        gemm(fw, ar, P.yT[:, sg.off:sg.off + sg.Tn], P.b_yT, D, P.w_out, P.b_in, cb_range(0, D),
             resid_epilogue(fw, sc, sg.hT, sg.b_hT), tts=sg.tts, Tn=sg.Tn)


class Params:
    pass


def declare(nc, name, shape, dtype, kind):
    return nc.dram_tensor(name, list(shape), dtype, kind=kind).ap()


def build_layer0(nc, P, fw, cs):
    transpose_in(fw, cs, P.xloc, P.b_in, P.hT, P.b_hT, T)
    transpose_in(fw, cs, P.xpre, P.b_in, P.hpT, P.b_hpT, TPRE)
    norm_fm(fw, cs, P.hT, P.b_hT, P.a_norm_g, P.b_in, P.xnT, P.b_xnT, T, D)
    norm_fm(fw, cs, P.hpT, P.b_hpT, P.a_norm_g, P.b_in, P.xnpT, P.b_xnpT, TPRE, D)
    l0_proj(fw, cs, P)
    mlstm(fw, cs, P)
    for sg in (seg_L(P), seg_P(P)):
        w_out_phase(fw, cs, P, sg)
        ffn(fw, cs, P, 0, sg)


INPUTS = {
    "xloc": ([T, D], F32), "xpre": ([TPRE, D], F32),
    "w_in": ([D, 12304], F32), "w_out": ([D, D], F32),
    "w_gu0": ([D, 2 * DFF], F32), "w_gu1": ([D, 2 * DFF], F32),
    "w_dn0": ([DFF, D], F32), "w_dn1": ([DFF, D], F32),
    "kv_w_down": ([D, 576], F32), "kv_w_up": ([512, 16384], F32),
    "w_dq": ([D, 1024], F32), "w_uq": ([1024, 12288], F32), "w_o": ([8192, D], F32),
    "a_norm_g": ([128, 32], F32), "ffn_g0": ([128, 32], F32), "ffn_g1": ([128, 32], F32),
    "kv_norm_g": ([128, 32], F32), "b_norm_g": ([128, 32], F32),
    "kv_lat_g": ([128, 4], F32), "q_lat_g": ([128, 8], F32), "gout_l": ([128, 32], F32),
    "kq_g": ([128, 8], F32),
    "mparams": ([8, 3], F32), "sel8": ([8, 8, 128], F32), "maskT": ([128, 128], BF16),
    "ident_f": ([128, 128], F32), "ident_b": ([128, 128], BF16), "ones_b": ([128, 128], BF16),
    "posf": ([1, TALL], I32), "invf": ([32, 1], F32), "aflag": ([128, 2], F32),
}
SCRATCH = {
    "hT": ([D, T], F32), "hpT": ([D, TPRE], F32), "xnT": ([D, T], BF16), "xnpT": ([D, TPRE], BF16),
    "qT": ([2048, TALL], BF16), "kT": ([2048, TALL], BF16), "vT": ([4096, TALL], BF16), "ogT": ([D, TALL], BF16),
    "graw": ([16, TALL], F32), "yT": ([D, TALL], BF16), "hidT": ([DFF, TPRE], BF16),
}


def make_P(nc, need, outputs=()):
    P = Params()
    P.b_in = Buf("inputs", multi=True)
    P.ffn_g, P.w_gu, P.w_dn = {}, {}, {}
    for name in need:
        shape, dt = INPUTS[name]
        ap = declare(nc, name, shape, dt, "ExternalInput")
        setattr(P, name, ap)
    for l in (0, 1):
        if f"ffn_g{l}" in need:
            P.ffn_g[l] = getattr(P, f"ffn_g{l}")
            P.w_gu[l] = getattr(P, f"w_gu{l}")
            P.w_dn[l] = getattr(P, f"w_dn{l}")
    for name, (shape, dt) in SCRATCH.items():
        kind = "ExternalOutput" if name in outputs else "Internal"
        setattr(P, name, declare(nc, name, shape, dt, kind))
        setattr(P, "b_" + name, Buf(name, multi=True))
    return P


def host_consts():
    import ml_dtypes
    bf = ml_dtypes.bfloat16
    sel8 = np.zeros((8, 8, 128), np.float32)
    for h in range(8):
        sel8[h, h, :] = 1.0
    s = np.arange(128)
    maskT = (s[:, None] <= s[None, :]).astype(np.float32).astype(bf)
    inv = (1.0 / (10000.0 ** (np.arange(0, 64, 2, dtype=np.float32) / 64.0))).astype(np.float32)
    return {
        "sel8": sel8, "maskT": maskT, "ident_f": np.eye(128, dtype=np.float32),
        "ident_b": np.eye(128, dtype=np.float32).astype(bf), "ones_b": np.ones((128, 128), np.float32).astype(bf),
        "invf": inv.reshape(32, 1),
    }


def lay(g, nk):
    return np.ascontiguousarray(np.asarray(g, np.float32).reshape(nk, 128).T)


def host_inputs(inputs, core):
    b, s = core // 2, core % 2
    x = inputs["x"][b]
    meta = inputs["meta_tokens"]
    if s == 0:
        xloc = x[0:2048]
        xpre = np.concatenate([np.zeros((NEUT, D), np.float32), meta], axis=0)
    else:
        xloc = x[2048:4096]
        xpre = np.concatenate([meta, x[0:2048]], axis=0)
    m = {
        "xloc": np.ascontiguousarray(xloc, dtype=np.float32), "xpre": np.ascontiguousarray(xpre, dtype=np.float32),
        "w_in": inputs["a_w_in"][0], "w_out": inputs["a_w_out"][0],
        "w_gu0": inputs["ffn_w_gate_up"][0], "w_gu1": inputs["ffn_w_gate_up"][1],
        "w_dn0": inputs["ffn_w_down"][0], "w_dn1": inputs["ffn_w_down"][1],
        "kv_w_down": inputs["kv_w_down"], "kv_w_up": inputs["kv_w_up"],
        "w_dq": inputs["b_w_dq"][0], "w_uq": inputs["b_w_uq"][0], "w_o": inputs["b_w_o"][0],
        "a_norm_g": lay(inputs["a_norm_g"][0], 32), "ffn_g0": lay(inputs["ffn_norm_g"][0], 32),
        "ffn_g1": lay(inputs["ffn_norm_g"][1], 32), "kv_norm_g": lay(inputs["kv_norm_g"], 32),
        "b_norm_g": lay(inputs["b_norm_g"][0], 32), "kv_lat_g": lay(inputs["kv_latent_norm_g"], 4),
        "q_lat_g": lay(inputs["b_q_latent_norm_g"][0], 8), "gout_l": lay(inputs["a_out_norm_g"][0], 32),
        "mparams": np.stack([np.asarray(inputs["a_b_i"][0], np.float32), np.asarray(inputs["a_b_f"][0], np.float32),
                             np.full(8, float(s), np.float32)], axis=1),
    }
    kq = np.zeros((128, 8), np.float32)
    gk = np.asarray(inputs["k_norm_g"], np.float32)
    gq = np.asarray(inputs["q_norm_g"][0], np.float32)
    kq[:, 0] = gk[0:128]
    kq[0:32, 1] = gk[128:160]
    kq[0:32, 2] = gk[160:192]
    kq[:, 3] = gq[0:128]
    kq[0:32, 4] = gq[128:160]
    kq[0:32, 5] = gq[160:192]
    m["kq_g"] = kq
    pos = np.asarray(inputs["positions"][b], np.int32)
    metap = np.arange(16, dtype=np.int32) - 16
    if s == 0:
        posf = np.concatenate([np.zeros(NEUT, np.int32), metap, pos[0:2048]])
    else:
        posf = np.concatenate([metap, pos[0:2032], pos[2032:4096]])
    m["posf"] = posf.reshape(1, TALL).astype(np.int32)
    af = np.full((128, 2), -10.0, np.float32)
    if s == 0:
        af[:, 1] = -30000.0
    m["aflag"] = af
    m.update(host_consts())
    return m


NH = 64
TT_ALL = TT_P + [(TPRE + i * 512, 512) for i in range(4)]
TWO_PI = 6.283185307179586
C1_2PI = 6.28125
C2_2PI = TWO_PI - C1_2PI

SCRATCH.update({
    "aT": ([576, T], F32), "apT": ([576, TPRE], F32), "ckvnT": ([512, TALL], BF16),
    "kTh": ([NH * 192, TALL], BF16), "vtok": ([TALL, NH * 128], BF16),
    "cqT": ([1024, T], F32), "cqnT": ([1024, T], BF16), "qTh": ([NH * 192, T], BF16),
    "oT": ([NH * 128, T], BF16),
})


def kv_down(fw, cs, P):
    for sg, dst, b_dst in ((seg_L(P), P.aT, P.b_aT), (seg_P(P), P.apT, P.b_apT)):
        norm_fm(fw, cs, sg.hT, sg.b_hT, P.kv_norm_g, P.b_in, sg.xnT, sg.b_xnT, sg.Tn, D)
        with Scope(fw) as sc:
            ar = Arena(sc, Tn=sg.Tn)
            routes = [(0, 576, dst, 0, 0, None, 1.0, F32, b_dst)]
            gemm(fw, ar, sg.xnT, sg.b_xnT, D, P.kv_w_down, P.b_in, cb_range(0, 512) + [[(512, 32), (544, 32)]],
                 store_epilogue(fw, sc, routes), tts=sg.tts, Tn=sg.Tn)


def rope_tables(fw, P, cosT, sinT, b_tab, sc):
    nc = fw.nc
    V = nc.vector
    A = nc.scalar
    pi_, b_pi = sc.sb([32, TALL], I32, "rp_i")
    ang, b_ang = sc.sb([32, TALL], F32, "rp_a")
    kk, b_kk = sc.sb([32, TALL], F32, "rp_k")
    inv, b_inv = sc.sb([32, 1], F32, "rp_inv")
    fw.dma("sp", pi_[:, :], P.posf.partition_broadcast(32), P.b_in, b_pi, "in")
    fw.dma("sp", inv[:, :], P.invf, P.b_in, b_inv, "in")
    fw.op("dve", lambda: V.tensor_copy(ang[:, :], pi_[:, :]), reads=[b_pi], writes=[b_ang])
    fw.op("dve", lambda: V.tensor_scalar(ang[:, :], ang[:, :], 16.0, inv[:, 0:1], ALU.add, ALU.mult),
          reads=[b_ang, b_inv], writes=[b_ang])
    fw.op("dve", lambda: V.tensor_scalar(kk[:, :], ang[:, :], 1.0 / TWO_PI, 12582912.0, ALU.mult, ALU.add),
          reads=[b_ang], writes=[b_kk])
    fw.op("dve", lambda: V.tensor_scalar(kk[:, :], kk[:, :], 12582912.0, None, ALU.subtract), reads=[b_kk], writes=[b_kk])
    fw.op("dve", lambda: V.scalar_tensor_tensor(ang[:, :], kk[:, :], -C1_2PI, ang[:, :], ALU.mult, ALU.add),
          reads=[b_kk, b_ang], writes=[b_ang])
    fw.op("dve", lambda: V.scalar_tensor_tensor(ang[:, :], kk[:, :], -C2_2PI, ang[:, :], ALU.mult, ALU.add),
          reads=[b_kk, b_ang], writes=[b_ang])
    fw.op("dve", lambda: V.tensor_scalar(ang[:, :], ang[:, :], 3.1415925, -3.1415925, ALU.min, ALU.max),
          reads=[b_ang], writes=[b_ang])
    fw.op("act", lambda: A.activation(sinT[:, :], ang[:, :], AF.Sin), reads=[b_ang], writes=[b_tab])
    fw.op("act", lambda: A.activation(kk[:, :], ang[:, :], AF.Abs), reads=[b_ang], writes=[b_kk])
    fw.op("act", lambda: A.activation(cosT[:, :], kk[:, :], AF.Sin, bias=1.5707963, scale=-1.0), reads=[b_kk], writes=[b_tab])


def a_src(P, t0):
    return (P.apT, P.b_apT, t0) if t0 < TPRE else (P.aT, P.b_aT, t0 - TPRE)


def kv_k(fw, cs, P):
    nc = fw.nc
    V, A, G = nc.vector, nc.scalar, nc.gpsimd
    norm_fm(fw, cs, P.apT[0:512, :], P.b_apT, P.kv_lat_g, P.b_in, P.ckvnT[:, 0:TPRE], P.b_ckvnT, TPRE, 512)
    norm_fm(fw, cs, P.aT[0:512, :], P.b_aT, P.kv_lat_g, P.b_in, P.ckvnT[:, TPRE:TALL], P.b_ckvnT, T, 512)
    with Scope(fw) as sc:
        cosT, _ = sc.sb([32, TALL], F32, "kk_cos")
        sinT, _ = sc.sb([32, TALL], F32, "kk_sin")
        b_tab = sc.track(Buf("kk_tab", multi=True))
        kr1, b_kr1 = sc.sb([32, TALL], F32, "kk_kr1", multi=True)
        kr2, b_kr2 = sc.sb([32, TALL], F32, "kk_kr2", multi=True)
        ssr, b_ssr = sc.sb([128, TALL], F32, "kk_ssr", multi=True)
        gk, b_gk = sc.sb([128, 8], F32, "kk_g")
        fw.dma("sp", gk[:, :], P.kq_g, P.b_in, b_gk, "in")
        with Scope(fw) as s2:
            rope_tables(fw, P, cosT, sinT, b_tab, s2)
        with Scope(fw) as s3:
            t1s = [s3.sb([32, 512], F32, f"kk_t1{i}") for i in range(2)]
            t2s = [s3.sb([32, 512], F32, f"kk_t2{i}") for i in range(2)]
            sqs = [s3.sb([32, 1024], BF16, f"kk_sq{i}") for i in range(2)]
            m1s = [s3.sb([32, 512], F32, f"kk_m1{i}") for i in range(2)]
            m2s = [s3.sb([32, 512], F32, f"kk_m2{i}") for i in range(2)]
            pss = [s3.ps(name=f"kk_ps{i}") for i in range(2)]
            for i, (t0, tn) in enumerate(TT_ALL):
                src, b_src, l0 = a_src(P, t0)
                (t1, b1), (t2, b2), (sq, bsq), (m1, bm1), (m2, bm2), (ps, bps) = \
                    t1s[i % 2], t2s[i % 2], sqs[i % 2], m1s[i % 2], m2s[i % 2], pss[i % 2]
                fw.dma("sp", t1[:, 0:tn], src[512:544, l0:l0 + tn], b_src, b1, "in")
                fw.dma("sp", t2[:, 0:tn], src[544:576, l0:l0 + tn], b_src, b2, "in")
                fw.op("act", lambda: A.activation(sq[:, 0:tn], t1[:, 0:tn], AF.Square), reads=[b1], writes=[bsq])
                fw.op("act", lambda: A.activation(sq[:, 512:512 + tn], t2[:, 0:tn], AF.Square), reads=[b2], writes=[bsq])
                fw.op("pe", lambda: nc.tensor.matmul(ps[:, 0:tn], cs.onesb[0:32, :], sq[:, 0:tn], start=True, stop=False),
                      reads=[bsq, cs.b], writes=[bps], signal=False)
                fw.op("pe", lambda: nc.tensor.matmul(ps[:, 0:tn], cs.onesb[0:32, :], sq[:, 512:512 + tn], start=False, stop=True),
                      reads=[bsq, cs.b], writes=[bps])
                fw.op("act", lambda: A.copy(ssr[:, t0:t0 + tn], ps[:, 0:tn]), reads=[bps], writes=[b_ssr])
                fw.op("dve", lambda: V.tensor_scalar(t1[:, 0:tn], t1[:, 0:tn], gk[0:32, 1:2], None, ALU.mult), reads=[b1, b_gk], writes=[b1])
                fw.op("dve", lambda: V.tensor_scalar(t2[:, 0:tn], t2[:, 0:tn], gk[0:32, 2:3], None, ALU.mult), reads=[b2, b_gk], writes=[b2])
                rope_pair(fw, t1[:, 0:tn], b1, t2[:, 0:tn], b2, cosT[:, t0:t0 + tn], sinT[:, t0:t0 + tn], b_tab,
                          m1[:, 0:tn], bm1, m2[:, 0:tn], bm2, kr1[:, t0:t0 + tn], b_kr1, kr2[:, t0:t0 + tn], b_kr2)
        with Scope(fw) as s4:
            ar = Arena(s4, Tn=TALL, kmax=4, nps=6)
            sqb = [s4.sb([128, 512], BF16, f"ke_sq{i}") for i in range(2)]
            rsb = [s4.sb([128, 512], F32, f"ke_rs{i}") for i in range(2)]
            knb = [s4.sb([128, 512], BF16, f"ke_kn{i}") for i in range(3)]
            k1b = [s4.sb([32, 1024], BF16, f"ke_k1{i}") for i in range(3)]
            pss = [s4.ps(name=f"ke_ps{i}") for i in range(2)]
            cnt = [0]

            def k_epi(sbi, t0, tn, blocks, kp, nkp):
                for (c0, ncol, ps, b_ps) in blocks:
                    h = c0 // 256
                    i = cnt[0]
                    cnt[0] += 1
                    (sq, bsq), (rs, brs), (kn, bkn), (k1, bk1), (p2, bp2) = sqb[i % 2], rsb[i % 2], knb[i % 3], k1b[i % 3], pss[i % 2]
                    fw.op("act", lambda: A.activation(sq[:, 0:tn], ps[:, 0:tn], AF.Square), reads=[b_ps], writes=[bsq])
                    fw.op("pe", lambda: nc.tensor.matmul(p2[:, 0:tn], cs.onesb[:, :], sq[:, 0:tn], start=True, stop=True),
                          reads=[bsq, cs.b], writes=[bp2])
                    fw.op("dve", lambda: V.tensor_tensor(rs[:, 0:tn], p2[:, 0:tn], ssr[:, t0:t0 + tn], ALU.add),
                          reads=[bp2, b_ssr], writes=[brs])
                    rstd_from_sumsq(fw, rs[:, 0:tn], brs, rs[:, 0:tn], brs, 1.0 / 192.0)
                    fw.op("dve", lambda: V.scalar_tensor_tensor(kn[:, 0:tn], ps[:, 0:tn], gk[:, 0:1], rs[:, 0:tn], ALU.mult, ALU.mult),
                          reads=[b_ps, b_gk, brs], writes=[bkn])
                    fw.op("pool", lambda: G.tensor_tensor(k1[:, 0:tn], kr1[:, t0:t0 + tn], rs[0:32, 0:tn], ALU.mult),
                          reads=[b_kr1, brs], writes=[bk1])
                    fw.op("pool", lambda: G.tensor_tensor(k1[:, 512:512 + tn], kr2[:, t0:t0 + tn], rs[0:32, 0:tn], ALU.mult),
                          reads=[b_kr2, brs], writes=[bk1])
                    r0 = h * 192
                    fw.dma("pool", P.kTh[r0:r0 + 128, t0:t0 + tn], kn[:, 0:tn], bkn, P.b_kTh, "out")
                    fw.dma("pool", P.kTh[r0 + 128:r0 + 160, t0:t0 + tn], k1[:, 0:tn], bk1, P.b_kTh, "out")
                    fw.dma("pool", P.kTh[r0 + 160:r0 + 192, t0:t0 + tn], k1[:, 512:512 + tn], bk1, P.b_kTh, "out")

            sbs = [[(h * 256, 128), ((h + 1) * 256, 128)] for h in range(0, NH, 2)]
            gemm(fw, ar, P.ckvnT, P.b_ckvnT, 512, P.kv_w_up, P.b_in, sbs, k_epi, tts=TT_ALL, Tn=TALL)


def rope_pair(fw, u1, b1, u2, b2, cos, sin, b_tab, m1, bm1, m2, bm2, o1, bo1, o2, bo2):
    nc = fw.nc
    V, G = nc.vector, nc.gpsimd
    fw.op("pool", lambda: G.tensor_tensor(m1, u1, cos, ALU.mult), reads=[b1, b_tab], writes=[bm1])
    fw.op("pool", lambda: G.tensor_tensor(m2, u2, sin, ALU.mult), reads=[b2, b_tab], writes=[bm2])
    fw.op("dve", lambda: V.tensor_tensor(o1, m1, m2, ALU.subtract), reads=[bm1, bm2], writes=[bo1])
    fw.op("pool", lambda: G.tensor_tensor(m1, u1, sin, ALU.mult), reads=[b1, b_tab], writes=[bm1])
    fw.op("pool", lambda: G.tensor_tensor(m2, u2, cos, ALU.mult), reads=[b2, b_tab], writes=[bm2])
    fw.op("dve", lambda: V.tensor_tensor(o2, m1, m2, ALU.add), reads=[bm1, bm2], writes=[bo2])


def kv_v(fw, cs, P):
    nc = fw.nc
    V, A, G = nc.vector, nc.scalar, nc.gpsimd
    wv = P.kv_w_up.rearrange("k (h two c) -> k h two c", two=2, c=128)
    with Scope(fw) as sc:
        act, b_act = sc.sb([128, 4, TALL], BF16, "vv_act")
        fw.dma("sp", act[:, :, :], P.ckvnT.rearrange("(c p) t -> p c t", p=128), P.b_ckvnT, b_act, "in")
        wss = [sc.sb([128, 4, 4, 128], F32, f"vv_ws{i}") for i in range(2)]
        wbs = [sc.sb([128, 4, 512], BF16, f"vv_wb{i}") for i in range(2)]
        evs = [sc.sb([128, 512], BF16, f"vv_ev{i}") for i in range(3)]
        pss = [sc.ps(name=f"vv_ps{i}") for i in range(4)]
        n = 0
        for hg in range(NH // 4):
            (ws, bws), (wb, bwb) = wss[hg % 2], wbs[hg % 2]
            for c in range(4):
                fw.dma("sp", ws[:, c, :, :], wv[c * 128:(c + 1) * 128, hg * 4:hg * 4 + 4, 1, :], P.b_in, bws, "in")
            fw.op("pool", lambda: G.tensor_copy(wb[:, :, :], ws[:, :, :, :].rearrange("p c h d -> p c (h d)")),
                  reads=[bws], writes=[bwb])
            for (t0, tn) in tok_blocks(TALL, 128):
                ps, bps = pss[n % 4]
                ev, bev = evs[n % 3]
                n += 1
                for c in range(4):
                    fw.op("pe", lambda: nc.tensor.matmul(ps[0:tn, :], act[:, c, t0:t0 + tn], wb[:, c, :], start=(c == 0), stop=(c == 3)),
                          reads=[b_act, bwb], writes=[bps], signal=(c == 3))
                evac_copy(fw, ev[0:tn, :], bev, ps[0:tn, :], bps, n)
                fw.dma("pool", P.vtok[t0:t0 + tn, hg * 512:(hg + 1) * 512], ev[0:tn, :], bev, P.b_vtok, "out")


def q_proj(fw, cs, P):
    nc = fw.nc
    V, A, G = nc.vector, nc.scalar, nc.gpsimd
    norm_fm(fw, cs, P.hT, P.b_hT, P.b_norm_g, P.b_in, P.xnT, P.b_xnT, T, D)
    with Scope(fw) as sc:
        ar = Arena(sc)
        routes = [(0, 1024, P.cqT, 0, 0, None, 1.0, F32, P.b_cqT)]
        gemm(fw, ar, P.xnT, P.b_xnT, D, P.w_dq, P.b_in, cb_range(0, 1024), store_epilogue(fw, sc, routes))
    norm_fm(fw, cs, P.cqT, P.b_cqT, P.q_lat_g, P.b_in, P.cqnT, P.b_cqnT, T, 1024)
    with Scope(fw) as sc:
        cosT, _ = sc.sb([32, TALL], F32, "qq_cos")
        sinT, _ = sc.sb([32, TALL], F32, "qq_sin")
        b_tab = sc.track(Buf("qq_tab", multi=True))
        gq, b_gq = sc.sb([128, 8], F32, "qq_g")
        fw.dma("sp", gq[:, :], P.kq_g, P.b_in, b_gq, "in")
        with Scope(fw) as s2:
            rope_tables(fw, P, cosT, sinT, b_tab, s2)
        with Scope(fw) as s4:
            ar = Arena(s4, Tn=T, kmax=8, nps=6)
            sqb = [s4.sb([128, 512], BF16, f"qe_sq{i}") for i in range(2)]
            sqa = [s4.sb([32, 1024], BF16, f"qe_sa{i}") for i in range(2)]
            rsb = [s4.sb([128, 512], F32, f"qe_rs{i}") for i in range(2)]
            qnb = [s4.sb([128, 512], BF16, f"qe_qn{i}") for i in range(3)]
            uab = [s4.sb([32, 1024], F32, f"qe_u{i}") for i in range(2)]
            m1b = [s4.sb([32, 512], F32, f"qe_m1{i}") for i in range(2)]
            m2b = [s4.sb([32, 512], F32, f"qe_m2{i}") for i in range(2)]
            o1b = [s4.sb([32, 1024], BF16, f"qe_o{i}", multi=True) for i in range(3)]
            pss = [s4.ps(name=f"qe_ps{i}") for i in range(2)]
            cnt = [0]

            def q_epi(sbi, t0, tn, blocks, kp, nkp):
                (cn, _, psn, bpn), (ca, _, psa, bpa), (cb_, _, psb, bpb) = blocks
                h = cn // 192
                i = cnt[0]
                cnt[0] += 1
                (sq, bsq), (sa, bsa), (rs, brs), (qn, bqn), (ua, bua) = sqb[i % 2], sqa[i % 2], rsb[i % 2], qnb[i % 3], uab[i % 2]
                (m1, bm1), (m2, bm2), (o1, bo1), (p2, bp2) = m1b[i % 2], m2b[i % 2], o1b[i % 3], pss[i % 2]
                fw.op("act", lambda: A.activation(sq[:, 0:tn], psn[:, 0:tn], AF.Square), reads=[bpn], writes=[bsq])
                fw.op("act", lambda: A.activation(sa[:, 0:tn], psa[0:32, 0:tn], AF.Square), reads=[bpa], writes=[bsa])
                fw.op("act", lambda: A.activation(sa[:, 512:512 + tn], psb[0:32, 0:tn], AF.Square), reads=[bpb], writes=[bsa])
                fw.op("pe", lambda: nc.tensor.matmul(p2[:, 0:tn], cs.onesb[:, :], sq[:, 0:tn], start=True, stop=False),
                      reads=[bsq, cs.b], writes=[bp2], signal=False)
                fw.op("pe", lambda: nc.tensor.matmul(p2[:, 0:tn], cs.onesb[0:32, :], sa[:, 0:tn], start=False, stop=False),
                      reads=[bsa, cs.b], writes=[bp2], signal=False)
                fw.op("pe", lambda: nc.tensor.matmul(p2[:, 0:tn], cs.onesb[0:32, :], sa[:, 512:512 + tn], start=False, stop=True),
                      reads=[bsa, cs.b], writes=[bp2])
                rstd_from_sumsq(fw, p2[:, 0:tn], bp2, rs[:, 0:tn], brs, 1.0 / 192.0)
                fw.op("dve", lambda: V.scalar_tensor_tensor(qn[:, 0:tn], psn[:, 0:tn], gq[:, 3:4], rs[:, 0:tn], ALU.mult, ALU.mult),
                      reads=[bpn, b_gq, brs], writes=[bqn])
                fw.op("dve", lambda: V.scalar_tensor_tensor(ua[:, 0:tn], psa[0:32, 0:tn], gq[0:32, 4:5], rs[0:32, 0:tn], ALU.mult, ALU.mult),
                      reads=[bpa, b_gq, brs], writes=[bua])
                fw.op("dve", lambda: V.scalar_tensor_tensor(ua[:, 512:512 + tn], psb[0:32, 0:tn], gq[0:32, 5:6], rs[0:32, 0:tn], ALU.mult, ALU.mult),
                      reads=[bpb, b_gq, brs], writes=[bua])
                tt0 = TPRE + t0
                rope_pair(fw, ua[:, 0:tn], bua, ua[:, 512:512 + tn], bua, cosT[:, tt0:tt0 + tn], sinT[:, tt0:tt0 + tn], b_tab,
                          m1[:, 0:tn], bm1, m2[:, 0:tn], bm2, o1[:, 0:tn], bo1, o1[:, 512:512 + tn], bo1)
                r0 = h * 192
                fw.dma("pool", P.qTh[r0:r0 + 128, t0:t0 + tn], qn[:, 0:tn], bqn, P.b_qTh, "out")
                fw.dma("pool", P.qTh[r0 + 128:r0 + 160, t0:t0 + tn], o1[:, 0:tn], bo1, P.b_qTh, "out")
                fw.dma("pool", P.qTh[r0 + 160:r0 + 192, t0:t0 + tn], o1[:, 512:512 + tn], bo1, P.b_qTh, "out")

            sbs = [[(h * 192, 128), (h * 192 + 128, 32), (h * 192 + 160, 32)] for h in range(NH)]
            gemm(fw, ar, P.cqnT, P.b_cqnT, 1024, P.w_uq, P.b_in, sbs, q_epi, kpass=8)


KB = [(128 * i, 128) for i in range(16)] + [(NEUT, 16)] + [(TPRE + 128 * i, 128) for i in range(16)]
ATT_SCALE = 192.0 ** -0.5


def attention(fw, cs, P):
    nc = fw.nc
    V, A, G = nc.vector, nc.scalar, nc.gpsimd
    with Scope(fw) as sc:
        tri, b_tri = sc.sb([128, 128], BF16, "at_tri")
        bias, b_bias = sc.sb([128, 2], F32, "at_bias")
        fw.dma("sp", tri[:, :], P.maskT, P.b_in, b_tri, "in")
        fw.dma("sp", bias[:, :], P.aflag, P.b_in, b_bias, "in")
        qnb = [sc.sb([128, T], BF16, f"at_qn{i}") for i in range(2)]
        qrb = [sc.sb([64, T], BF16, f"at_qr{i}") for i in range(2)]
        knb = [sc.sb([128, TALL], BF16, f"at_kn{i}") for i in range(2)]
        krb = [sc.sb([64, TALL], BF16, f"at_kr{i}") for i in range(2)]
        vgb = [sc.sb([128, 33, 512], BF16, f"at_vg{i}", multi=True) for i in range(2)]
        ptb = [sc.sb([128, 512], BF16, f"at_pt{i}") for i in range(4)]
        rcb = [sc.sb([128, 512], F32, f"at_rc{i}") for i in range(2)]
        accb = [sc.sb([128, 512], F32, f"at_acc{i}") for i in range(2)]
        onesf, b_onesf = sc.sb([128, 128], F32, "at_onesf")
        fw.op("pool", lambda: G.memset(onesf[:, :], 1.0), writes=[b_onesf])
        obb = [sc.sb([128, 512], BF16, f"at_ob{i}") for i in range(2)]
        Sb = [sc.ps(name=f"at_S{i}") for i in range(2)]
        OTb = [sc.ps(name=f"at_O{i}") for i in range(2)]
        SMb = [sc.ps(name=f"at_M{i}") for i in range(2)]

        def load_head(h):
            if h >= NH:
                return
            r0 = h * 192
            fw.dma("sp", qnb[h % 2][0][:, :], P.qTh[r0:r0 + 128, :], P.b_qTh, qnb[h % 2][1], "in")
            fw.dma("sp", qrb[h % 2][0][:, :], P.qTh[r0 + 128:r0 + 192, :], P.b_qTh, qrb[h % 2][1], "in")
            fw.dma("sp", knb[h % 2][0][:, :], P.kTh[r0:r0 + 128, :], P.b_kTh, knb[h % 2][1], "in")
            fw.dma("sp", krb[h % 2][0][:, :], P.kTh[r0 + 128:r0 + 192, :], P.b_kTh, krb[h % 2][1], "in")
            if h % 4 == 0:
                g = h // 4
                vt, bvt = vgb[g % 2]
                fw.dma("sp", vt[:, 0:16, :], P.vtok[0:NEUT, g * 512:(g + 1) * 512].rearrange("(b p) c -> p b c", p=128),
                       P.b_vtok, bvt, "in")
                fw.dma("sp", vt[0:16, 16, :], P.vtok[NEUT:TPRE, g * 512:(g + 1) * 512], P.b_vtok, bvt, "in")
                fw.dma("sp", vt[:, 17:33, :], P.vtok[TPRE:TALL, g * 512:(g + 1) * 512].rearrange("(b p) c -> p b c", p=128),
                       P.b_vtok, bvt, "in")

        units = []
        ti = 0
        for h in range(NH):
            for (t0, tn) in TT:
                us = []
                for kb, (k0, kn_) in enumerate(KB):
                    if k0 < TPRE:
                        us.append([h, ti, t0, tn, kb, k0, kn_, 0, False, k0 < NEUT])
                    else:
                        lk0 = k0 - TPRE
                        if lk0 > t0 + tn - 1:
                            continue
                        if lk0 + kn_ - 1 <= t0:
                            us.append([h, ti, t0, tn, kb, k0, kn_, 0, False, False])
                        else:
                            us.append([h, ti, t0, tn, kb, k0, kn_, lk0 - t0, True, False])
                for i, u in enumerate(us):
                    u.append(i == 0)
                    u.append(i == len(us) - 1)
                units += us
                ti += 1

        def emit_S(ui):
            h, ti, t0, tn, kb, k0, kn_, c0, diag, pre, first, last = units[ui]
            S, bS = Sb[ui % 2]
            pt, bpt = ptb[ui % 4]
            ncol = tn - c0
            (qn, bqn), (qr, bqr), (kn, bkn), (kr, bkr) = qnb[h % 2], qrb[h % 2], knb[h % 2], krb[h % 2]
            fw.op("pe", lambda: nc.tensor.matmul(S[0:kn_, 0:ncol], kn[:, k0:k0 + kn_], qn[:, t0 + c0:t0 + tn], start=True, stop=False),
                  reads=[bkn, bqn], writes=[bS], signal=False)
            fw.op("pe", lambda: nc.tensor.matmul(S[0:kn_, 0:ncol], kr[:, k0:k0 + kn_], qr[:, t0 + c0:t0 + tn], start=False, stop=True),
                  reads=[bkr, bqr], writes=[bS])
            bcol = bias[0:kn_, 1:2] if pre else bias[0:kn_, 0:1]
            fw.op("act", lambda: A.activation(pt[0:kn_, 0:ncol], S[0:kn_, 0:ncol], AF.Exp, bias=bcol, scale=ATT_SCALE),
                  reads=[bS, b_bias], writes=[bpt])
            if diag:
                fw.op("pool", lambda: G.tensor_tensor(pt[0:kn_, 0:kn_], pt[0:kn_, 0:kn_], tri[0:kn_, 0:kn_], ALU.mult),
                      reads=[bpt, b_tri], writes=[bpt])

        def emit_PV(ui):
            h, ti, t0, tn, kb, k0, kn_, c0, diag, pre, first, last = units[ui]
            pt, bpt = ptb[ui % 4]
            OT, bOT = OTb[ti % 2]
            SM, bSM = SMb[ti % 2]
            acc, bacc = accb[ti % 2]
            vt, bvt = vgb[(h // 4) % 2]
            hh = h % 4
            fw.op("pe", lambda: nc.tensor.matmul(OT[:, c0:tn], vt[0:kn_, kb, hh * 128:(hh + 1) * 128], pt[0:kn_, 0:tn - c0],
                                                 start=first, stop=last),
                  reads=[bvt, bpt], writes=[bOT], signal=False)
            fw.op("pe", lambda: nc.tensor.matmul(SM[:, c0:tn], cs.onesb[0:kn_, :], pt[0:kn_, 0:tn - c0], start=first, stop=last),
                  reads=[cs.b, bpt], writes=[bSM])
            if last:
                rc, brc = rcb[ti % 2]
                ob, bob = obb[ti % 2]
                fw.op("dve", lambda: V.reciprocal(rc[:, 0:tn], SM[:, 0:tn]), reads=[bSM], writes=[brc])
                fw.op("dve", lambda: V.tensor_tensor(ob[:, 0:tn], OT[:, 0:tn], rc[:, 0:tn], ALU.mult), reads=[bOT, brc], writes=[bob])
                fw.dma("pool", P.oT[h * 128:(h + 1) * 128, t0:t0 + tn], ob[:, 0:tn], bob, P.b_oT, "out")

        load_head(0)
        n = len(units)
        import os
        if os.environ.get("ATT_UNITS"):
            n = int(os.environ["ATT_UNITS"])
        for ui in range(n + 1):
            if ui < n:
                emit_S(ui)
            if ui >= 1:
                emit_PV(ui - 1)
            if ui < n and (ui == 0 or units[ui][0] != units[ui - 1][0]):
                load_head(units[ui][0] + 1)


def o_proj(fw, cs, P):
    with Scope(fw) as sc:
        ar = Arena(sc)
        gemm(fw, ar, P.oT, P.b_oT, NH * 128, P.w_o, P.b_in, cb_range(0, D), resid_epilogue(fw, sc, P.hT, P.b_hT))


def transpose_out(fw, cs, P, out, b_out):
    nc = fw.nc
    with Scope(fw) as sc:
        hin = [sc.sb([128, 32, 512], F32, f"to_h{i}") for i in range(2)]
        xo = [sc.sb([128, D], F32, f"to_x{i}") for i in range(2)]
        pss = [sc.ps(name=f"to_ps{i}") for i in range(4)]
        ips = 0
        nb = 0
        for gi in range(4):
            g0 = gi * 512
            h_t, h_b = hin[gi % 2]
            fw.dma("sp", h_t[:, :, :], P.hT[:, g0:g0 + 512].rearrange("(c p) t -> p c t", p=128), P.b_hT, h_b, "in")
            for bi in range(4):
                x_t, x_b = xo[nb % 2]
                nb += 1
                for c4 in range(0, 32, 4):
                    ps_t, ps_b = pss[ips % 4]
                    ips += 1
                    for j in range(4):
                        c = c4 + j
                        fw.op("pe", lambda: nc.tensor.transpose(ps_t[:, j * 128:(j + 1) * 128], h_t[:, c, bi * 128:(bi + 1) * 128],
                                                                cs.ident[:, :]),
                              reads=[h_b, cs.b], writes=[ps_b], signal=(j == 3))
                    if (c4 // 4) % 2 == 0:
                        fw.op("dve", lambda: nc.vector.tensor_copy(x_t[:, c4 * 128:(c4 + 4) * 128], ps_t[:, :]), reads=[ps_b], writes=[x_b])
                    else:
                        fw.op("act", lambda: nc.scalar.copy(x_t[:, c4 * 128:(c4 + 4) * 128], ps_t[:, :]), reads=[ps_b], writes=[x_b])
                r0 = gi * 512 + bi * 128
                fw.dma("pool", out[r0:r0 + 128, :], x_t[:, :], x_b, b_out, "out")


INPUTS.update({"hT_in": ([D, T], F32), "aT_in": ([576, T], F32), "apT_in": ([576, TPRE], F32)})
SCRATCH.update({"out": ([T, D], F32)})


def copy_dram(fw, dst, b_dst, src, b_src, rows, tmpname="cp"):
    b_tmp = Buf(tmpname)
    step = 512
    for r0 in range(0, rows, step):
        r1 = min(rows, r0 + step)
        fw.dma("sp", dst[r0:r1, :], src[r0:r1, :], b_src, b_dst, "in")


def build_layer1(fw, cs, P, out, b_out):
    kv_k(fw, cs, P)
    kv_v(fw, cs, P)
    q_proj(fw, cs, P)
    attention(fw, cs, P)
    o_proj(fw, cs, P)
    ffn(fw, cs, P, 1)
    transpose_out(fw, cs, P, out, b_out)


NEED_A = ["xloc", "xpre", "w_in", "w_out", "a_norm_g", "gout_l", "mparams", "sel8", "maskT", "ident_f", "ident_b", "ones_b",
          "w_gu0", "w_dn0", "ffn_g0", "kv_norm_g", "kv_w_down"]
NEED_B = ["hT_in", "aT_in", "apT_in", "kv_w_up", "kv_lat_g", "kq_g", "posf", "invf", "aflag", "maskT", "ident_f", "ident_b", "ones_b",
          "b_norm_g", "w_dq", "q_lat_g", "w_uq", "w_o", "ffn_g1", "w_gu1", "w_dn1"]


def build_A():
    nc = bass.Bass("TRN2", target_bir_lowering=False)
    P = make_P(nc, NEED_A, ["hT", "aT"])
    fw = FW(nc)
    cs = Consts(fw, P.ident_f, P.ident_b, P.ones_b)
    build_layer0(nc, P, fw, cs)
    kv_down(fw, cs, P)
    fw.drain([P.b_hT, P.b_aT])
    return nc


def build_B():
    nc = bass.Bass("TRN2", target_bir_lowering=False)
    P = make_P(nc, NEED_B, ["out"])
    fw = FW(nc)
    cs = Consts(fw, P.ident_f, P.ident_b, P.ones_b)
    copy_dram(fw, P.hT, P.b_hT, P.hT_in, P.b_in, D)
    P.aT, P.b_aT = P.aT_in, P.b_in
    P.apT, P.b_apT = P.apT_in, P.b_in
    build_layer1(fw, cs, P, P.out, P.b_out)
    fw.drain([P.b_out])
    return nc


def kernel_2launch(**inputs):
    inputs = {k: np.asarray(v) for k, v in inputs.items()}
    maps = [host_inputs(inputs, c) for c in range(NCORES)]
    ncA = build_A()
    resA = run_bass_kernel_spmd(ncA, [{k: m[k] for k in NEED_A} for m in maps], core_ids=list(range(NCORES)))
    hTs = [np.asarray(r["hT"]) for r in resA.results]
    aTs = [np.asarray(r["aT"]) for r in resA.results]
    del resA
    for c in range(NCORES):
        maps[c]["hT_in"] = hTs[c]
        maps[c]["aT_in"] = aTs[c]
        if c % 2 == 1:
            maps[c]["apT_in"] = np.ascontiguousarray(aTs[c - 1][:, 0:TPRE])
        else:
            maps[c]["apT_in"] = np.zeros((576, TPRE), np.float32)
    ncB = build_B()
    resB = run_bass_kernel_spmd(ncB, [{k: m[k] for k in NEED_B} for m in maps], core_ids=list(range(NCORES)))
    out = np.empty((4, 4096, D), np.float32)
    for c in range(NCORES):
        b, s = c // 2, c % 2
        out[b, s * 2048:(s + 1) * 2048] = np.asarray(resB.results[c]["out"])
    return out


SCRATCH_UNUSED = {"agT": ([NCORES * 576, T], F32)}
INPUTS.update({"selw": ([128, NCORES], F32)})


def exchange_latent(fw, cs, P):
    nc = fw.nc
    V = nc.vector
    sem = fw.get_sem()
    fw._deps("pool", [P.b_aT], [P.b_agT])
    inst = nc.gpsimd.collective_compute("AllGather", mybir.AluOpType.bypass,
                                        replica_groups=[list(range(NCORES))],
                                        ins=[P.aT[:, :]], outs=[P.agT[:, :]])
    sem.n += 16
    inst.then_inc(sem.h, 16)
    P.b_agT.w[sem] = sem.n
    P.b_aT.r[sem] = sem.n
    with Scope(fw) as sc:
        sw, b_sw = sc.sb([128, NCORES], F32, "ex_w")
        fw.dma("sp", sw[:, :], P.selw, P.b_in, b_sw, "in")
        ins_ = [sc.sb([128, 512], F32, f"ex_i{i}") for i in range(4)]
        accs = [sc.sb([128, 512], F32, f"ex_a{i}") for i in range(2)]
        n = 0
        k = 0
        for (r0, rn) in [(0, 128), (128, 128), (256, 128), (384, 128), (512, 64)]:
            for t0 in range(0, TPRE, 512):
                acc, b_acc = accs[n % 2]
                n += 1
                for r in range(NCORES):
                    it, b_it = ins_[k % 4]
                    k += 1
                    fw.dma("sp", it[0:rn, :], P.agT[r * 576 + r0:r * 576 + r0 + rn, t0:t0 + 512], P.b_agT, b_it, "in")
                    if r == 0:
                        fw.op("dve", lambda: V.tensor_scalar(acc[0:rn, :], it[0:rn, :], sw[0:rn, 0:1], None, ALU.mult),
                              reads=[b_it, b_sw], writes=[b_acc])
                    else:
                        fw.op("dve", lambda: V.scalar_tensor_tensor(acc[0:rn, :], it[0:rn, :], sw[0:rn, r:r + 1], acc[0:rn, :], ALU.mult, ALU.add),
                              reads=[b_it, b_sw, b_acc], writes=[b_acc])
                fw.dma("pool", P.apT[r0:r0 + rn, t0:t0 + 512], acc[0:rn, :], b_acc, P.b_apT, "out")
    fw.put_sems([])
    fw.sem_pool.append(sem)


NEED_F = sorted(set(NEED_A + [k for k in NEED_B if k not in ("hT_in", "aT_in", "apT_in")]))


def build_fused():
    nc = bass.Bass("TRN2", target_bir_lowering=False)
    P = make_P(nc, NEED_F, ["out"])
    fw = FW(nc)
    cs = Consts(fw, P.ident_f, P.ident_b, P.ones_b)
    build_layer0(nc, P, fw, cs)
    kv_down(fw, cs, P)
    build_layer1(fw, cs, P, P.out, P.b_out)
    fw.drain([P.b_out])
    return nc


def kernel(**inputs):
    inputs = {k: np.asarray(v) for k, v in inputs.items()}
    maps = [host_inputs(inputs, c) for c in range(NCORES)]
    nc = build_fused()
    res = run_bass_kernel_spmd(nc, [{k: m[k] for k in NEED_F} for m in maps], core_ids=list(range(NCORES)))
    out = np.empty((4, 4096, D), np.float32)
    for c in range(NCORES):
        b, s = c // 2, c % 2
        out[b, s * 2048:(s + 1) * 2048] = np.asarray(res.results[c]["out"])
    return out
```
